# Optimizing a Trainium2 kernel written in Bass

```python
import math
import jax, jax.numpy as jnp
from jax import lax
import numpy as np

D_MODEL = 1024
BATCH = 8
SEQ = 4096
DEPTH = 1

PLE_DIM = 256
D_FF = 2816
FFN_RES_WEIGHT = 0.5
NORM_EPS = 1e-6
Q_BLOCK = 128

MLA_HEADS = 8
MLA_NOPE = 64
MLA_ROPE = 32
MLA_QK = MLA_NOPE + MLA_ROPE
MLA_V = 64
Q_LORA = 384
KV_LORA = 256
ROPE_BASE = 10000.0

SB_HEADS = 8
SB_HEAD_DIM = 64
SB_WIDTH = SB_HEADS * SB_HEAD_DIM
MLA_WIDTH = MLA_HEADS * MLA_V

COL_CQ = Q_LORA
COL_CKV = KV_LORA
COL_KROPE = MLA_ROPE
COL_SB = 3 * SB_WIDTH
COL_GATES = 2 * D_MODEL
IN_COLS = COL_CQ + COL_CKV + COL_KROPE + COL_SB + COL_GATES
SPLITS = list(np.cumsum([COL_CQ, COL_CKV, COL_KROPE, COL_SB])[:])

kernel_name = "hybrid_mla_stickbreaking_macaron_ple"


def rms_norm(x, g):
    xf = x.astype(jnp.float32)
    r = lax.rsqrt(jnp.mean(xf * xf, axis=-1, keepdims=True) + NORM_EPS)
    return (xf * r).astype(x.dtype) * g


def apply_rope(x, positions):
    r = x.shape[-1]
    inv_freq = ROPE_BASE ** (-jnp.arange(0, r, 2, dtype=jnp.float32) / r)
    ang = positions.astype(jnp.float32)[..., None] * inv_freq
    cos = jnp.cos(ang)[:, :, None, :].astype(x.dtype)
    sin = jnp.sin(ang)[:, :, None, :].astype(x.dtype)
    x1, x2 = x[..., : r // 2], x[..., r // 2:]
    return jnp.concatenate([x1 * cos - x2 * sin, x2 * cos + x1 * sin], axis=-1)


def swiglu(u, w_in, w_out):
    a, b = jnp.split(u @ w_in, 2, axis=-1)
    return (jax.nn.silu(a) * b) @ w_out


def causal_softmax_attention(q, k, v):
    s_len = q.shape[2]
    scale = 1.0 / math.sqrt(q.shape[-1])
    outs = []
    for i in range(s_len // Q_BLOCK):
        k_len = (i + 1) * Q_BLOCK
        qb = q[:, :, i * Q_BLOCK:(i + 1) * Q_BLOCK]
        kb, vb = k[:, :, :k_len], v[:, :, :k_len]
        sc = jnp.einsum('bhqd,bhkd->bhqk', qb, kb).astype(jnp.float32) * scale
        q_pos = i * Q_BLOCK + jnp.arange(Q_BLOCK)
        mask = jnp.arange(k_len)[None, :] <= q_pos[:, None]
        w = jax.nn.softmax(jnp.where(mask, sc, -jnp.inf), axis=-1)
        outs.append(jnp.einsum('bhqk,bhkd->bhqd', w.astype(vb.dtype), vb))
    return jnp.concatenate(outs, axis=2)


def stick_breaking_attention(q, k, v):
    s_len = q.shape[2]
    scale = 1.0 / math.sqrt(q.shape[-1])
    outs = []
    for i in range(s_len // Q_BLOCK):
        k_len = (i + 1) * Q_BLOCK
        qb = q[:, :, i * Q_BLOCK:(i + 1) * Q_BLOCK]
        kb, vb = k[:, :, :k_len], v[:, :, :k_len]
        z = jnp.einsum('bhqd,bhkd->bhqk', qb, kb).astype(jnp.float32) * scale
        q_pos = i * Q_BLOCK + jnp.arange(Q_BLOCK)
        mask = jnp.arange(k_len)[None, :] < q_pos[:, None]
        log_1m = jnp.where(mask, jax.nn.log_sigmoid(-z), 0.0)
        suffix = lax.cumsum(log_1m, axis=3, reverse=True) - log_1m
        a = jnp.where(mask, jnp.exp(jax.nn.log_sigmoid(z) + suffix), 0.0)
        outs.append(jnp.einsum('bhqk,bhkd->bhqd', a.astype(vb.dtype), vb))
    return jnp.concatenate(outs, axis=2)


def setup_inputs(seed: int = 0) -> dict:
    key = jax.random.key(seed)
    ks = iter(jax.random.split(key, 32))
    f32 = jnp.float32

    def w(shape, fan_in):
        return jax.random.normal(next(ks), (DEPTH,) + shape, f32) * fan_in ** -0.5

    def gain(n):
        return 1.0 + 0.02 * jax.random.normal(next(ks), (DEPTH, n), f32)

    x = jax.random.normal(next(ks), (BATCH, SEQ, D_MODEL), f32)
    p = jax.random.normal(next(ks), (DEPTH, BATCH, SEQ, PLE_DIM), f32)
    offset = jax.random.randint(next(ks), (BATCH, 1), 0, 1024, dtype=jnp.int32)
    positions = offset + jnp.arange(SEQ, dtype=jnp.int32)[None, :]
    return {
        "x": x,
        "p": p,
        "positions": positions,
        "ffn1_norm": gain(D_MODEL),
        "ffn1_w_in": w((D_MODEL, 2 * D_FF), D_MODEL),
        "ffn1_w_out": w((D_FF, D_MODEL), D_FF),
        "mix_norm": gain(D_MODEL),
        "w_in": w((D_MODEL, IN_COLS), D_MODEL),
        "q_latent_norm": gain(Q_LORA),
        "w_q_up": w((Q_LORA, MLA_HEADS * MLA_QK), Q_LORA),
        "kv_latent_norm": gain(KV_LORA),
        "w_kv_up": w((KV_LORA, MLA_HEADS * (MLA_NOPE + MLA_V)), KV_LORA),
        "q_head_norm": gain(MLA_QK),
        "k_head_norm": gain(MLA_QK),
        "w_branch_mla": w((MLA_WIDTH, D_MODEL), MLA_WIDTH),
        "w_branch_sb": w((SB_WIDTH, D_MODEL), SB_WIDTH),
        "w_out": w((D_MODEL, D_MODEL), D_MODEL),
        "ffn2_norm": gain(D_MODEL),
        "ffn2_w_in": w((D_MODEL, 2 * D_FF), D_MODEL),
        "ffn2_w_out": w((D_FF, D_MODEL), D_FF),
        "ple_norm": gain(D_MODEL),
        "w_ple_gate": w((D_MODEL, D_MODEL), D_MODEL),
        "w_ple_proj": w((PLE_DIM, D_MODEL), PLE_DIM),
    }


def reference(x, p, positions, ffn1_norm, ffn1_w_in, ffn1_w_out, mix_norm, w_in,
              q_latent_norm, w_q_up, kv_latent_norm, w_kv_up, q_head_norm, k_head_norm,
              w_branch_mla, w_branch_sb, w_out, ffn2_norm, ffn2_w_in, ffn2_w_out,
              ple_norm, w_ple_gate, w_ple_proj):
    b, s, _ = x.shape
    h = x
    for i in range(DEPTH):
        h = h + FFN_RES_WEIGHT * swiglu(rms_norm(h, ffn1_norm[i]), ffn1_w_in[i], ffn1_w_out[i])

        u = rms_norm(h, mix_norm[i])
        proj = u @ w_in[i]
        c_q, c_kv, k_rope, sb_qkv, gates = jnp.split(proj, SPLITS, axis=-1)

        q = (rms_norm(c_q, q_latent_norm[i]) @ w_q_up[i]).reshape(b, s, MLA_HEADS, MLA_QK)
        kv = (rms_norm(c_kv, kv_latent_norm[i]) @ w_kv_up[i]).reshape(b, s, MLA_HEADS, MLA_NOPE + MLA_V)
        k_nope, v_mla = kv[..., :MLA_NOPE], kv[..., MLA_NOPE:]
        k_r = jnp.broadcast_to(k_rope[:, :, None, :], (b, s, MLA_HEADS, MLA_ROPE))
        k = jnp.concatenate([k_nope, k_r], axis=-1)
        q = rms_norm(q, q_head_norm[i])
        k = rms_norm(k, k_head_norm[i])
        q = jnp.concatenate([q[..., :MLA_NOPE], apply_rope(q[..., MLA_NOPE:], positions)], axis=-1)
        k = jnp.concatenate([k[..., :MLA_NOPE], apply_rope(k[..., MLA_NOPE:], positions)], axis=-1)
        o_mla = causal_softmax_attention(q.transpose(0, 2, 1, 3), k.transpose(0, 2, 1, 3),
                                         v_mla.transpose(0, 2, 1, 3))
        o_mla = o_mla.transpose(0, 2, 1, 3).reshape(b, s, MLA_WIDTH)

        sq, sk, sv = [t.reshape(b, s, SB_HEADS, SB_HEAD_DIM).transpose(0, 2, 1, 3)
                      for t in jnp.split(sb_qkv, 3, axis=-1)]
        o_sb = stick_breaking_attention(sq, sk, sv)
        o_sb = o_sb.transpose(0, 2, 1, 3).reshape(b, s, SB_WIDTH)

        g_mla, g_sb = jnp.split(jax.nn.sigmoid(gates), 2, axis=-1)
        merged = g_mla * (o_mla @ w_branch_mla[i]) + g_sb * (o_sb @ w_branch_sb[i])
        h = h + merged @ w_out[i]

        h = h + FFN_RES_WEIGHT * swiglu(rms_norm(h, ffn2_norm[i]), ffn2_w_in[i], ffn2_w_out[i])

        ple_gate = jax.nn.sigmoid(rms_norm(h, ple_norm[i]) @ w_ple_gate[i])
        h = h + ple_gate * (p[i] @ w_ple_proj[i])
    return h
```

```python
import math
from contextlib import ExitStack

import numpy as np
import concourse.bass as bass
import concourse.mybir as mybir
from concourse.bass_utils import run_bass_kernel_spmd

F32 = mybir.dt.float32
BF16 = mybir.dt.bfloat16
I32 = mybir.dt.int32
AF = mybir.ActivationFunctionType
ALU = mybir.AluOpType
AX = mybir.AxisListType

T = 4096
D = 1024
DFF = 2816
G = 512
NG = T // G
NTG = G // 128
NJ = DFF // 128
EPS = 1e-6
DEBUG = False


class Res:
    __slots__ = ("name", "w", "r", "excl")

    def __init__(self, name="", excl=False):
        self.name = name
        self.w = None
        self.r = []
        self.excl = excl


class Prog:
    ENG = ["pe", "act", "dve", "pool", "sp"]
    EPOCH = 4096
    NDS = 16

    def __init__(self):
        self.ins = {e: [] for e in self.ENG}
        self.ndma = {e: 0 for e in self.ENG}
        self.dmas = {e: [] for e in self.ENG}

    def op(self, eng, fn, reads=(), writes=(), dma=False, extra=()):
        idx = len(self.ins[eng])
        me = (eng, idx)
        deps = set(extra)
        if any(r.excl for r in reads):
            writes = list(writes) + [r for r in reads if r.excl and r not in writes]
            reads = [r for r in reads if not r.excl]
        for r in reads:
            if r.w is not None:
                deps.add(r.w)
        for w in writes:
            if w.w is not None:
                deps.add(w.w)
            for rd in w.r:
                if rd[0] == eng and not dma and not self.ins[eng][rd[1]]["dma"]:
                    continue
                deps.add(rd)
        fd = set()
        for d in deps:
            if d == me:
                continue
            if d[0] == eng and eng == "pe":
                continue
            fd.add(d)
        rec = dict(fn=fn, deps=fd, dma=dma, signal=False, dj=None)
        if dma:
            rec["dj"] = self.ndma[eng]
            self.ndma[eng] += 1
            self.dmas[eng].append(me)
        self.ins[eng].append(rec)
        for d in fd:
            self.ins[d[0]][d[1]]["signal"] = True
        for r in reads:
            r.r.append(me)
        for w in writes:
            w.w = me
            w.r = []
        return me

    def barrier(self):
        tails = []
        for e in self.ENG:
            last = None
            for i in range(len(self.ins[e]) - 1, -1, -1):
                if not self.ins[e][i]["dma"]:
                    last = (e, i)
                    break
            if last is not None:
                tails.append(last)
            tails.extend(self.dmas[e][-self.NDS:])
        for e in self.ENG:
            self.op(e, lambda en: en.nop(), extra=[t for t in tails])

    def emit(self, nc, final_waits=()):
        with ExitStack() as es:
            esem = {}
            for e in self.ENG:
                k = 0
                for r in self.ins[e]:
                    if r["signal"] and not r["dma"]:
                        r["k"] = k
                        k += 1
                nep = (k + self.EPOCH - 1) // self.EPOCH
                esem[e] = [es.enter_context(nc.semaphore(f"s_{e}_{i}")) for i in range(nep)]
            dsem = {}
            for e in self.ENG:
                if self.ndma[e]:
                    dsem[e] = [es.enter_context(nc.semaphore(f"d_{e}_{i}")) for i in range(self.NDS)]
            block = es.enter_context(nc.Block())
            prog = self

            def run(e, engobj):
                seen = {}

                def wait_compute(e2, k):
                    if seen.get(e2, -1) >= k:
                        return
                    seen[e2] = k
                    engobj.wait_ge(esem[e2][k // prog.EPOCH], k % prog.EPOCH + 1)

                def wait_dma(q, j):
                    key = ("d", q, j % prog.NDS)
                    if seen.get(key, -1) >= j:
                        return
                    seen[key] = j
                    engobj.wait_ge(dsem[q][j % prog.NDS], 16 * (j // prog.NDS + 1))

                for r in prog.ins[e]:
                    for d in sorted(r["deps"]):
                        p = prog.ins[d[0]][d[1]]
                        if p["dma"]:
                            wait_dma(d[0], p["dj"])
                        else:
                            wait_compute(d[0], p["k"])
                    if r["dma"]:
                        j = r["dj"]
                        if j >= prog.NDS:
                            wait_dma(e, j - prog.NDS)
                        bi = r["fn"](engobj)
                        bi.then_inc(dsem[e][j % prog.NDS], 16)
                    else:
                        bi = r["fn"](engobj)
                        if r["signal"]:
                            k = r["k"]
                            bi.then_inc(esem[e][k // prog.EPOCH], 1)
                if e == "sp":
                    for d in final_waits:
                        p = prog.ins[d[0]][d[1]]
                        wait_dma(d[0], p["dj"])

            @block.tensor
            def _(eng):
                run("pe", eng)

            @block.scalar
            def _(eng):
                run("act", eng)

            @block.vector
            def _(eng):
                run("dve", eng)

            @block.gpsimd
            def _(eng):
                run("pool", eng)

            @block.sync
            def _(eng):
                run("sp", eng)


WEIGHTS = [
    ("ffn1_w_in", 1024, 5632), ("ffn1_w_out", 2816, 1024), ("w_in", 1024, 4256),
    ("w_q_up", 384, 768), ("w_kv_up", 256, 1024), ("w_branch_mla", 512, 1024),
    ("w_branch_sb", 512, 1024), ("w_out", 1024, 1024), ("ffn2_w_in", 1024, 5632),
    ("ffn2_w_out", 2816, 1024), ("w_ple_gate", 1024, 1024), ("w_ple_proj", 256, 1024),
]
GAINS = [("ffn1_norm", 1024), ("mix_norm", 1024), ("q_latent_norm", 384), ("kv_latent_norm", 256),
         ("q_head_norm", 96), ("k_head_norm", 96), ("ffn2_norm", 1024), ("ple_norm", 1024)]

C_SBQ = 672
C_SBK = C_SBQ + 512
C_SBV = C_SBK + 512
C_GATE = C_SBV + 512


def build_nc():
    nc = bass.Bass("TRN2", target_bir_lowering=False)
    P = Prog()
    ACTQ = "pool"
    WQ = "sp"

    def dram(name, shape, dt, kind):
        return nc.dram_tensor(name, shape, dt, kind=kind).ap()

    x_d = dram("x", [T, D], F32, "ExternalInput")
    p_d = dram("p", [T, 256], F32, "ExternalInput")
    pos_d = dram("posT", [128, 32], I32, "ExternalInput")
    w_d = {n: dram(n, [r, c], F32, "ExternalInput") for n, r, c in WEIGHTS}
    g_d = {n: dram(n, [1, c], F32, "ExternalInput") for n, c in GAINS}
    out_d = dram("out", [T, D], F32, "ExternalOutput")
    sk = "ExternalOutput" if DEBUG else "Internal"
    ws = {n: dram("s_" + n, [r, c], BF16, "Internal") for n, r, c in WEIGHTS}
    h1_s = dram("s_h1", [T, D], F32, sk)
    qt_s = dram("s_qt", [8, 96, T], BF16, sk)
    kt_s = dram("s_kt", [8, 96, T], BF16, sk)
    vm_s = dram("s_vm", [8, 128, 32, 64], BF16, sk)
    sqt_s = dram("s_sqt", [4, 128, T], BF16, sk)
    skt_s = dram("s_skt", [4, 128, T], BF16, sk)
    vs_s = dram("s_vs", [8, 128, 32, 64], BF16, sk)
    om_s = dram("s_om", [4, 128, T], BF16, sk)
    os_s = dram("s_os", [4, 128, T], BF16, sk)
    R_ws = {n: Res(n) for n, _, _ in WEIGHTS}
    R_h1, R_qt, R_kt, R_vm, R_sqt, R_skt, R_vs, R_om, R_os, R_out = [Res() for _ in range(10)]

    es = ExitStack()
    ARENA_N = 103 * 1024
    arena = es.enter_context(nc.sbuf_tensor("arena", [128, ARENA_N], BF16))
    ps = es.enter_context(nc.psum_tensor("ps", [128, 4096], F32))
    PB = [Res(f"bank{b}", excl=True) for b in range(8)]

    def bank(b, lo=0, hi=512, p0=0, p1=128):
        return ps[p0:p1, b * 512 + lo:b * 512 + hi]

    def bankbf(b):
        return ps[:, b * 512:(b + 1) * 512].bitcast(BF16)

    st = {"off": 0}

    def A(shape, dt):
        n = 1
        for s_ in shape[1:]:
            n *= s_
        nb = n * (2 if dt == BF16 else 4)
        ne = ((nb + 31) // 32) * 16
        off = st["off"]
        assert off + ne <= ARENA_N, ("arena overflow", off, ne)
        st["off"] = off + ne
        sl = arena[:, off:off + nb // 2]
        if dt != BF16:
            sl = sl.bitcast(dt)
        if len(shape) > 2:
            names = " ".join(f"d{i}" for i in range(len(shape) - 1))
            kw = {f"d{i}": shape[i + 1] for i in range(len(shape) - 1)}
            sl = sl.rearrange(f"p ({names}) -> p {names}", **kw)
        if shape[0] < 128:
            sl = sl[0:shape[0]]
        return sl

    def MM(out, lhsT, rhs, start, stop, r, w):
        P.op("pe", lambda e: e.matmul(out, lhsT=lhsT, rhs=rhs, start=start, stop=stop), r, w)

    def TR(out, in_, ident, r, w):
        P.op("pe", lambda e: e.transpose(out=out, in_=in_, identity=ident), r, w)

    def ACTF(out, in_, func, r, w, bias=None, scale=None, accum=None):
        kw = {}
        if bias is not None:
            kw["bias"] = bias
        if scale is not None:
            kw["scale"] = scale
        if accum is not None:
            kw["accum_out"] = accum
        P.op("act", lambda e: e.activation(out=out, in_=in_, func=func, **kw), r, w)

    def TT(eng, out, in0, in1, op, r, w):
        P.op(eng, lambda e: e.tensor_tensor(out=out, in0=in0, in1=in1, op=op), r, w)

    def TS(eng, out, in0, s1, op0, r, w, s2=None, op1=None):
        if op1 is None:
            P.op(eng, lambda e: e.tensor_scalar(out=out, in0=in0, scalar1=s1, scalar2=None, op0=op0), r, w)
        else:
            P.op(eng, lambda e: e.tensor_scalar(out=out, in0=in0, scalar1=s1, scalar2=s2, op0=op0, op1=op1), r, w)

    def STT(out, in0, scalar, in1, op0, op1, r, w):
        P.op("dve", lambda e: e.scalar_tensor_tensor(out=out, in0=in0, scalar=scalar, in1=in1, op0=op0, op1=op1), r, w)

    def CP(eng, out, in_, r, w, scale=None):
        if eng == "act":
            if scale is None:
                P.op("act", lambda e: e.activation(out=out, in_=in_, func=AF.Copy), r, w)
            else:
                P.op("act", lambda e: e.activation(out=out, in_=in_, func=AF.Copy, scale=scale), r, w)
        else:
            if scale is None:
                P.op(eng, lambda e: e.tensor_copy(out=out, in_=in_), r, w)
            else:
                P.op(eng, lambda e: e.tensor_scalar(out=out, in0=in_, scalar1=scale, scalar2=None, op0=ALU.mult), r, w)

    def DMA(q, out, in_, r, w):
        return P.op(q, lambda e: e.dma_start(out=out, in_=in_), r, w, dma=True)

    def MEMSET(eng, ap, val, w):
        P.op(eng, lambda e: e.memset(ap, val), (), w)

    def ASEL(out, in_, pattern, cmp, base, cm, r, w):
        P.op("pool", lambda e: e.affine_select(out=out, in_=in_, pattern=pattern, compare_op=cmp, fill=0.0,
                                                base=base, channel_multiplier=cm), r, w)

    ident = A([128, 128], BF16)
    onesb = A([128, 128], BF16)
    negones = A([128, 128], BF16)
    trineg = A([128, 128], BF16)
    epst = A([128, 1], F32)
    cosT = A([128, 32, 16], F32)
    sinT = A([128, 32, 16], F32)
    R_const = Res("const")
    R_cs = Res("cossin")
    MEMSET("pool", onesb, 1.0, [R_const])
    MEMSET("pool", negones, -1.0, [R_const])
    MEMSET("pool", epst, EPS, [R_const])
    ASEL(ident, onesb, [[-1, 128]], ALU.is_equal, 0, 1, [R_const], [R_const])
    ASEL(trineg, negones, [[-1, 128]], ALU.is_ge, 0, 1, [R_const], [R_const])
    gbc = {n: A([128, c], F32) for n, c in GAINS if c == 1024}
    R_g = Res("gains")
    for n in gbc:
        DMA(ACTQ, gbc[n], g_d[n].broadcast_to([128, 1024]), [], [R_g])
    PERSIST = st["off"]

    posi = A([128, 32], I32)
    posf = A([128, 32], F32)
    invf = A([128, 16], F32)
    ang = A([128, 32, 16], F32)
    kq = A([128, 32, 16], F32)
    ki = A([128, 32, 16], I32)
    r1 = A([128, 32, 16], F32)
    r2 = A([128, 32, 16], F32)
    Rt = Res("ropetmp")
    DMA(ACTQ, posi, pos_d, [], [Rt])
    CP("dve", posf, posi, [Rt], [Rt])
    inv_np = (np.float32(10000.0) ** (-np.arange(0, 32, 2, dtype=np.float32) / np.float32(32))).astype(np.float32)
    for j in range(16):
        MEMSET("dve", invf[:, j:j + 1], float(inv_np[j]), [Rt])
    TT("dve", ang, posf.unsqueeze(2).broadcast_to([128, 32, 16]), invf.unsqueeze(1).broadcast_to([128, 32, 16]),
       ALU.mult, [Rt], [Rt])
    TWO_PI = 2.0 * math.pi
    C1 = 6.28125
    C2 = TWO_PI - C1
    for (dst, shift) in ((sinT, 0.0), (cosT, math.pi / 2)):
        TS("dve", r1, ang, shift, ALU.add, [Rt], [Rt])
        TS("dve", kq, r1, 1.0 / TWO_PI, ALU.mult, [Rt], [Rt])
        CP("dve", ki, kq, [Rt], [Rt])
        CP("dve", kq, ki, [Rt], [Rt])
        STT(r1, kq, -C1, r1, ALU.mult, ALU.add, [Rt], [Rt])
        STT(r1, kq, -C2, r1, ALU.mult, ALU.add, [Rt], [Rt])
        TS("dve", r2, r1, math.pi, ALU.is_gt, [Rt], [Rt], s2=-TWO_PI, op1=ALU.mult)
        TT("dve", r1, r1, r2, ALU.add, [Rt], [Rt])
        TS("dve", r2, r1, -math.pi, ALU.is_lt, [Rt], [Rt], s2=TWO_PI, op1=ALU.mult)
        TT("dve", r1, r1, r2, ALU.add, [Rt], [Rt])
        TS("dve", r1, r1, math.pi, ALU.min, [Rt], [Rt], s2=-math.pi, op1=ALU.max)
        ACTF(dst, r1, AF.Sin, [Rt], [R_cs])

    stg_f = [A([128, 5632], F32) for _ in range(2)]
    stg_b = [A([128, 5632], BF16) for _ in range(2)]
    R_sf = [Res(), Res()]
    R_sb = [Res(), Res()]
    ui = 0
    for n, r, c in WEIGHTS:
        nr = r // 128
        per = max(1, 5632 // c)
        c0 = 0
        while c0 < nr:
            k = min(per, nr - c0)
            b = ui % 2
            src = w_d[n][c0 * 128:(c0 + k) * 128, :].rearrange("(i p) n -> p i n", p=128)
            dst = ws[n][c0 * 128:(c0 + k) * 128, :].rearrange("(i p) n -> p i n", p=128)
            sf = stg_f[b][:, 0:k * c].rearrange("p (i n) -> p i n", i=k)
            sb_ = stg_b[b][:, 0:k * c].rearrange("p (i n) -> p i n", i=k)
            DMA(WQ, sf, src, [], [R_sf[b]])
            CP("dve" if ui % 2 == 0 else "act", sb_, sf, [R_sf[b]], [R_sb[b]])
            DMA(ACTQ, dst, sb_, [R_sb[b]], [R_ws[n]])
            ui += 1
            c0 += k
    P.barrier()
    st["off"] = PERSIST

    RING_N = 4
    ring = [A([128, 4096], BF16) for _ in range(RING_N)]
    R_ring = [Res(f"ring{i}") for i in range(RING_N)]
    rst = {"i": 0}

    def ring_load(src_ap, shape, Rsrc):
        i = rst["i"] % RING_N
        rst["i"] += 1
        n = 1
        for s_ in shape[1:]:
            n *= s_
        dst = ring[i][:, 0:n]
        if len(shape) == 3:
            dst = dst.rearrange("p (a b) -> p a b", a=shape[1])
        DMA(WQ, dst, src_ap, [Rsrc], [R_ring[i]])
        return dst, R_ring[i]

    hg = A([128, NTG, 1024], F32)
    R_hg = Res("hg")
    _uT0 = A([128, 8, G], BF16)
    uT = [_uT0, _uT0]
    _RuT = Res("uT")
    R_uT = [_RuT, _RuT]
    u_tm = [A([128, 1024], BF16) for _ in range(2)]
    R_utm = [Res(), Res()]
    junk = A([128, 1024], BF16)
    R_junk = Res()
    gT = A([128, NJ, G], BF16)
    R_gT = Res("gT")
    sa = [A([128, G], BF16) for _ in range(2)]
    R_sa = [Res(), Res()]
    NSTAT = 12
    stat = A([128, NSTAT * 3 * 8], F32)
    sst = {"i": 0}

    def stat3():
        i = sst["i"] % NSTAT
        sst["i"] += 1
        base = i * 24
        return [stat[:, base + 8 * k: base + 8 * k + 8] for k in range(3)], Res()

    def rstd_from_ssq(ssq, lnv, rstd, n, ncol, Rs):
        ACTF(lnv[:, 0:ncol], ssq[:, 0:ncol], AF.Ln, [Rs, R_const], [Rs], bias=epst[:, 0:1], scale=1.0 / n)
        ACTF(rstd[:, 0:ncol], lnv[:, 0:ncol], AF.Exp, [Rs], [Rs], scale=-0.5)

    def norm_to_uT(gain, ui_):
        (ssq, lnv, rstd), Rs = stat3()
        for t in range(NTG):
            ACTF(junk, hg[:, t, :], AF.Square, [R_hg], [R_junk, Rs], accum=ssq[:, t:t + 1])
        rstd_from_ssq(ssq, lnv, rstd, 1024.0, NTG, Rs)
        for t in range(NTG):
            b = t % 2
            STT(u_tm[b], hg[:, t, :], rstd[:, t:t + 1], gain, ALU.mult, ALU.mult, [R_hg, Rs, R_g], [R_utm[b]])
            for c in range(8):
                TR(bankbf(b)[:, c * 128:(c + 1) * 128], u_tm[b][:, c * 128:(c + 1) * 128], ident,
                   [R_utm[b], R_const], [PB[b]])
            CP("act" if t % 2 == 0 else "dve", uT[ui_][:, :, t * 128:(t + 1) * 128],
               bankbf(b).rearrange("p (c k) -> p c k", c=8), [PB[b]], [R_uT[ui_]])

    JB = [(j0, min(j0 + 4, NJ)) for j0 in range(0, NJ, 4)]

    def ffn(w1n, w2n, ui_):
        W1 = ws[w1n].rearrange("(c p) n -> p c n", p=128)
        W2 = ws[w2n].rearrange("(j p) n -> p j n", p=128)
        for (j0, j1) in JB:
            nj = j1 - j0
            sA, RA = ring_load(W1[:, :, j0 * 128:j1 * 128], [128, 8, nj * 128], R_ws[w1n])
            sB, RB = ring_load(W1[:, :, DFF + j0 * 128:DFF + j1 * 128], [128, 8, nj * 128], R_ws[w1n])
            for jj in range(nj):
                j = j0 + jj
                ab = 2 * (j % 2)
                bb = ab + 1
                for c in range(8):
                    MM(bank(ab), sA[:, c, jj * 128:(jj + 1) * 128], uT[ui_][:, c, :], c == 0, c == 7,
                       [RA, R_uT[ui_]], [PB[ab]])
                for c in range(8):
                    MM(bank(bb), sB[:, c, jj * 128:(jj + 1) * 128], uT[ui_][:, c, :], c == 0, c == 7,
                       [RB, R_uT[ui_]], [PB[bb]])
                ACTF(sa[j % 2], bank(ab), AF.Silu, [PB[ab]], [R_sa[j % 2]])
                TT("dve", gT[:, j, :], bank(bb), sa[j % 2], ALU.mult, [PB[bb], R_sa[j % 2]], [R_gT])
        for tp in range(2):
            for (j0, j1) in JB:
                nj = j1 - j0
                sW, RW = ring_load(W2[:, j0:j1, :], [128, nj, 1024], R_ws[w2n])
                for jj in range(nj):
                    j = j0 + jj
                    for t in (2 * tp, 2 * tp + 1):
                        for hf in range(2):
                            bk = 4 + (t % 2) * 2 + hf
                            MM(bank(bk), gT[:, j, t * 128:(t + 1) * 128], sW[:, jj, hf * 512:(hf + 1) * 512],
                               j == 0, j == NJ - 1, [RW, R_gT], [PB[bk]])
            for t in (2 * tp, 2 * tp + 1):
                for hf in range(2):
                    bk = 4 + (t % 2) * 2 + hf
                    dst = hg[:, t, hf * 512:(hf + 1) * 512]
                    STT(dst, bank(bk), 0.5, dst, ALU.mult, ALU.add, [PB[bk], R_hg], [R_hg])

    P1 = st["off"]
    wtok = A([128, 8, 672], BF16)
    wv = A([128, 8, 512], BF16)
    wq = A([128, 3, 768], BF16)
    wkv = A([128, 2, 1024], BF16)
    R_w1 = Res("p1w")
    Win_v = ws["w_in"].rearrange("(c p) n -> p c n", p=128)
    DMA(WQ, wtok, Win_v[:, :, 0:672], [R_ws["w_in"]], [R_w1])
    DMA(WQ, wv, Win_v[:, :, C_SBV:C_SBV + 512], [R_ws["w_in"]], [R_w1])
    DMA(WQ, wq, ws["w_q_up"].rearrange("(c p) n -> p c n", p=128), [R_ws["w_q_up"]], [R_w1])
    DMA(WQ, wkv, ws["w_kv_up"].rearrange("(c p) n -> p c n", p=128), [R_ws["w_kv_up"]], [R_w1])
    gql = A([128, 384], F32)
    gkvl = A([128, 256], F32)
    gqh = A([128, 8, 96], F32)
    gkh = A([128, 8, 96], F32)
    DMA(ACTQ, gql, g_d["q_latent_norm"].broadcast_to([128, 384]), [], [R_g])
    DMA(ACTQ, gkvl, g_d["kv_latent_norm"].broadcast_to([128, 256]), [], [R_g])
    DMA(ACTQ, gqh, bass.AP(g_d["q_head_norm"].tensor, 0, [[0, 128], [0, 8], [1, 96]]), [], [R_g])
    DMA(ACTQ, gkh, bass.AP(g_d["k_head_norm"].tensor, 0, [[0, 128], [0, 8], [1, 96]]), [], [R_g])
    TS("dve", gqh, gqh, 1.0 / math.sqrt(96.0), ALU.mult, [R_g], [R_g])
    cql = A([128, 672], F32)
    R_cql = Res()
    lat_bf = A([128, 640], BF16)
    R_lat = Res()
    latT = A([128, 5, 128], BF16)
    R_latT = Res()
    qf = A([128, 8, 96], F32)
    R_qf = Res()
    kf = A([128, 8, 96], F32)
    R_kf = Res()
    sq = A([128, 8, 96], F32)
    R_sq = Res()
    rt = [A([128, 8, 16], F32) for _ in range(4)]
    R_rt = Res()
    qb = A([128, 8, 96], BF16)
    R_qb = Res()
    qTg = A([96, 8, G], BF16)
    R_qTg = Res()
    kTg = A([96, 8, G], BF16)
    R_kTg = Res()
    vmg = A([128, NTG, 8, 64], BF16)
    R_vmg = Res()
    vsg = A([128, NTG, 8, 64], BF16)
    R_vsg = Res()
    sqTg = A([128, 4, G], BF16)
    R_sqTg = Res()
    skTg = A([128, 4, G], BF16)
    R_skTg = Res()

    def head_norm_rope(src, Rsrc, gains, tile_idx, dstT, RdstT, tcol):
        (ssq, lnv, rstd), Rs = stat3()
        TT("dve", sq, src, src, ALU.mult, [Rsrc], [R_sq])
        P.op("dve", lambda e: e.tensor_reduce(out=ssq[:, 0:8], in_=sq, axis=AX.X, op=ALU.add), [R_sq], [Rs])
        rstd_from_ssq(ssq, lnv, rstd, 96.0, 8, Rs)
        TT("dve", src, src, gains, ALU.mult, [Rsrc, R_g], [Rsrc])
        TT("dve", src, src, rstd[:, 0:8].unsqueeze(2).broadcast_to([128, 8, 96]), ALU.mult, [Rsrc, Rs], [Rsrc])
        cs = cosT[:, tile_idx, :].unsqueeze(1).broadcast_to([128, 8, 16])
        sn = sinT[:, tile_idx, :].unsqueeze(1).broadcast_to([128, 8, 16])
        x1 = src[:, :, 64:80]
        x2 = src[:, :, 80:96]
        TT("dve", rt[0], x1, cs, ALU.mult, [Rsrc, R_cs], [R_rt])
        TT("dve", rt[1], x2, sn, ALU.mult, [Rsrc, R_cs], [R_rt])
        TT("dve", rt[2], x2, cs, ALU.mult, [Rsrc, R_cs], [R_rt])
        TT("dve", rt[3], x1, sn, ALU.mult, [Rsrc, R_cs], [R_rt])
        CP("dve", qb[:, :, 0:64], src[:, :, 0:64], [Rsrc], [R_qb])
        TT("dve", qb[:, :, 64:80], rt[0], rt[1], ALU.subtract, [R_rt], [R_qb])
        TT("dve", qb[:, :, 80:96], rt[2], rt[3], ALU.add, [R_rt], [R_qb])
        bq = bankbf(2)[0:96].rearrange("p (h k) -> p h k", h=8)
        for h in range(8):
            TR(bq[:, h, :], qb[:, h, :], ident, [R_qb, R_const], [PB[2]])
        CP("act", dstT[:, :, tcol * 128:(tcol + 1) * 128], bq, [PB[2]], [RdstT])

    for g in range(NG):
        rows = slice(g * G, (g + 1) * G)
        DMA(ACTQ, hg, x_d[rows, :].rearrange("(t p) d -> p t d", p=128), [], [R_hg])
        norm_to_uT(gbc["ffn1_norm"], 0)
        ffn("ffn1_w_in", "ffn1_w_out", 0)
        DMA(ACTQ, h1_s[rows, :].rearrange("(t p) d -> p t d", p=128), hg, [R_hg], [R_h1])
        norm_to_uT(gbc["mix_norm"], 1)
        U = uT[1]
        RU = R_uT[1]
        for (c0, dstT, RdT, scl) in ((C_SBQ, sqTg, R_sqTg, 0.125), (C_SBK, skTg, R_skTg, None)):
            sW, RW = ring_load(Win_v[:, :, c0:c0 + 512], [128, 8, 512], R_ws["w_in"])
            for m in range(4):
                bk = m % 2
                for c in range(8):
                    MM(bank(bk), sW[:, c, m * 128:(m + 1) * 128], U[:, c, :], c == 0, c == 7, [RW, RU], [PB[bk]])
                CP("act" if m % 2 == 0 else "dve", dstT[:, m, :], bank(bk), [PB[bk]], [RdT], scale=scl)
        DMA(ACTQ, sqt_s[:, :, rows].rearrange("m p t -> p m t"), sqTg, [R_sqTg], [R_sqt])
        DMA(ACTQ, skt_s[:, :, rows].rearrange("m p t -> p m t"), skTg, [R_skTg], [R_skt])
        for t in range(NTG):
            tl = g * NTG + t
            tc_ = slice(t * 128, (t + 1) * 128)
            for c in range(8):
                MM(bank(3), U[:, c, tc_], wv[:, c, :], c == 0, c == 7, [RU, R_w1], [PB[3]])
            CP("act", vsg[:, t].rearrange("p h d -> p (h d)"), bank(3), [PB[3]], [R_vsg])
            for c in range(8):
                MM(bank(4), U[:, c, tc_], wtok[:, c, 0:512], c == 0, c == 7, [RU, R_w1], [PB[4]])
            for c in range(8):
                MM(bank(5, 0, 160), U[:, c, tc_], wtok[:, c, 512:672], c == 0, c == 7, [RU, R_w1], [PB[5]])
            CP("dve", cql[:, 0:512], bank(4), [PB[4]], [R_cql])
            CP("act", cql[:, 512:672], bank(5, 0, 160), [PB[5]], [R_cql])
            (ssq, lnv, rstd), Rs = stat3()
            ACTF(junk[:, 0:384], cql[:, 0:384], AF.Square, [R_cql], [R_junk, Rs], accum=ssq[:, 0:1])
            ACTF(junk[:, 0:256], cql[:, 384:640], AF.Square, [R_cql], [R_junk, Rs], accum=ssq[:, 1:2])
            ACTF(lnv[:, 0:1], ssq[:, 0:1], AF.Ln, [Rs, R_const], [Rs], bias=epst[:, 0:1], scale=1.0 / 384)
            ACTF(lnv[:, 1:2], ssq[:, 1:2], AF.Ln, [Rs, R_const], [Rs], bias=epst[:, 0:1], scale=1.0 / 256)
            ACTF(rstd[:, 0:2], lnv[:, 0:2], AF.Exp, [Rs], [Rs], scale=-0.5)
            STT(lat_bf[:, 0:384], cql[:, 0:384], rstd[:, 0:1], gql, ALU.mult, ALU.mult, [R_cql, Rs, R_g], [R_lat])
            STT(lat_bf[:, 384:640], cql[:, 384:640], rstd[:, 1:2], gkvl, ALU.mult, ALU.mult, [R_cql, Rs, R_g], [R_lat])
            bl = bankbf(3)[:, 0:640].rearrange("p (c k) -> p c k", c=5)
            for c in range(5):
                TR(bl[:, c, :], lat_bf[:, c * 128:(c + 1) * 128], ident, [R_lat, R_const], [PB[3]])
            CP("dve", latT, bl, [PB[3]], [R_latT])
            for c in range(3):
                MM(bank(4), latT[:, c, :], wq[:, c, 0:512], c == 0, c == 2, [R_latT, R_w1], [PB[4]])
            for c in range(3):
                MM(bank(5, 0, 256), latT[:, c, :], wq[:, c, 512:768], c == 0, c == 2, [R_latT, R_w1], [PB[5]])
            qflat = qf.rearrange("p h d -> p (h d)")
            CP("act", qflat[:, 0:512], bank(4), [PB[4]], [R_qf])
            CP("dve", qflat[:, 512:768], bank(5, 0, 256), [PB[5]], [R_qf])
            for hf in range(2):
                for c in range(2):
                    MM(bank(6 + hf), latT[:, 3 + c, :], wkv[:, c, hf * 512:(hf + 1) * 512], c == 0, c == 1,
                       [R_latT, R_w1], [PB[6 + hf]])
            for hf in range(2):
                bv = bank(6 + hf).rearrange("p (h d) -> p h d", h=4)
                CP("act" if hf == 0 else "dve", kf[:, hf * 4:(hf + 1) * 4, 0:64], bv[:, :, 0:64], [PB[6 + hf]], [R_kf])
                CP("dve" if hf == 0 else "act", vmg[:, t, hf * 4:(hf + 1) * 4, :], bv[:, :, 64:128], [PB[6 + hf]], [R_vmg])
            CP("dve", kf[:, :, 64:96], cql[:, 640:672].unsqueeze(1).broadcast_to([128, 8, 32]), [R_cql], [R_kf])
            head_norm_rope(qf, R_qf, gqh, tl, qTg, R_qTg, t)
            head_norm_rope(kf, R_kf, gkh, tl, kTg, R_kTg, t)
        DMA(ACTQ, qt_s[:, :, rows].rearrange("h d t -> d h t"), qTg, [R_qTg], [R_qt])
        DMA(ACTQ, kt_s[:, :, rows].rearrange("h d t -> d h t"), kTg, [R_kTg], [R_kt])
        for h in range(8):
            DMA(ACTQ, vm_s[h, :, g * NTG:(g + 1) * NTG, :], vmg[:, :, h, :], [R_vmg], [R_vm])
            DMA(ACTQ, vs_s[h, :, g * NTG:(g + 1) * NTG, :], vsg[:, :, h, :], [R_vsg], [R_vs])
    P.barrier()

    st["off"] = PERSIST
    ones64 = onesb[:, 0:64]
    qT2 = [A([128, T], BF16) for _ in range(2)]
    kT2 = [A([128, T], BF16) for _ in range(2)]
    v2 = [A([128, 2, 32, 64], BF16) for _ in range(2)]
    R_q2 = [Res(), Res()]
    R_k2 = [Res(), Res()]
    R_v2 = [Res(), Res()]
    NPT = 4
    pT = [A([128, 512], BF16) for _ in range(NPT)]
    R_pT = [Res() for _ in range(NPT)]
    spT = [A([128, 512], BF16) for _ in range(NPT)]
    R_spT = [Res() for _ in range(NPT)]
    eT = [A([128, 512], F32) for _ in range(2)]
    R_eT = [Res(), Res()]
    spsum = [A([128, 512], BF16) for _ in range(2)]
    R_sps = [Res(), Res()]
    rden = [A([128, 512], F32) for _ in range(2)]
    R_rden = [Res(), Res()]
    ost = [A([128, 512], BF16) for _ in range(3)]
    R_ost = [Res() for _ in range(3)]
    cnt = {"pt": 0, "s": 0, "o": 0, "ost": 0, "e": 0}

    for h in range(8):
        b2 = h % 2
        p0 = (h % 2) * 64
        DMA(ACTQ, qT2[b2][0:96, :], qt_s[h], [R_qt], [R_q2[b2]])
        DMA(ACTQ, kT2[b2][0:96, :], kt_s[h], [R_kt], [R_k2[b2]])
        DMA(ACTQ, v2[b2][:, 0], vm_s[h], [R_vm], [R_v2[b2]])
        Q = qT2[b2]
        K = kT2[b2]
        V = v2[b2]
        for qg in range(NG):
            nkb = 4 * (qg + 1)
            ob = 4 + (cnt["o"] % 2) * 2
            db = ob + 1
            cnt["o"] += 1
            steps = list(range(nkb))

            def issue_s(kb):
                r = kb - 4 * qg
                lo = max(r, 0) * 128
                sb_ = cnt["s"] % 3
                cnt["s"] += 1
                MM(bank(sb_, lo, 512), K[0:96, kb * 128:(kb + 1) * 128], Q[0:96, qg * 512 + lo:(qg + 1) * 512],
                   True, True, [R_k2[b2], R_q2[b2]], [PB[sb_]])
                return sb_, lo, r

            pend = issue_s(0)
            for kb in steps:
                sb_, lo, r = pend
                if kb + 1 < nkb:
                    pend = issue_s(kb + 1)
                pi = cnt["pt"] % NPT
                cnt["pt"] += 1
                ACTF(pT[pi][:, lo:512], bank(sb_, lo, 512), AF.Exp, [PB[sb_]], [R_pT[pi]])
                if r >= 0:
                    ASEL(pT[pi][:, lo:lo + 128], pT[pi][:, lo:lo + 128], [[1, 128]], ALU.is_ge, 0, -1,
                         [R_pT[pi]], [R_pT[pi]])
                MM(bank(ob, lo, 512, p0, p0 + 64), V[:, 0, kb, :], pT[pi][:, lo:512], kb == 0, kb == nkb - 1,
                   [R_v2[b2], R_pT[pi]], [PB[ob]])
                MM(bank(db, lo, 512, p0, p0 + 64), ones64, pT[pi][:, lo:512], kb == 0, kb == nkb - 1,
                   [R_const, R_pT[pi]], [PB[db]])
            ri = qg % 2
            P.op("dve", lambda e, o_=rden[ri][p0:p0 + 64, :], i_=bank(db, 0, 512, p0, p0 + 64): e.reciprocal(out=o_, in_=i_),
                 [PB[db]], [R_rden[ri]])
            oi = cnt["ost"] % 3
            cnt["ost"] += 1
            TT("dve", ost[oi][p0:p0 + 64, :], bank(ob, 0, 512, p0, p0 + 64), rden[ri][p0:p0 + 64, :], ALU.mult,
               [PB[ob], R_rden[ri]], [R_ost[oi]])
            DMA(ACTQ, om_s[h // 2, p0:p0 + 64, qg * 512:(qg + 1) * 512], ost[oi][p0:p0 + 64, :], [R_ost[oi]], [R_om])

    for m in range(4):
        b2 = m % 2
        DMA(ACTQ, qT2[b2], sqt_s[m], [R_sqt], [R_q2[b2]])
        DMA(ACTQ, kT2[b2], skt_s[m], [R_skt], [R_k2[b2]])
        DMA(ACTQ, v2[b2][:, 0], vs_s[2 * m], [R_vs], [R_v2[b2]])
        DMA(ACTQ, v2[b2][:, 1], vs_s[2 * m + 1], [R_vs], [R_v2[b2]])
        for hh in range(2):
            p0 = hh * 64
            Q = qT2[b2][p0:p0 + 64, :]
            K = kT2[b2][p0:p0 + 64, :]
            for qg in range(NG):
                nkb = 4 * (qg + 1)
                ob = 6 + (cnt["o"] % 2)
                cnt["o"] += 1
                si = qg % 2
                MEMSET("pool", spsum[si], 0.0, [R_sps[si]])
                order = list(range(nkb - 1, -1, -1))

                def issue_z(kb):
                    r = kb - 4 * qg
                    lo = max(r, 0) * 128
                    sb_ = cnt["s"] % 3
                    cnt["s"] += 1
                    MM(bank(sb_, lo, 512), K[:, kb * 128:(kb + 1) * 128], Q[:, qg * 512 + lo:(qg + 1) * 512],
                       True, True, [R_k2[b2], R_q2[b2]], [PB[sb_]])
                    return sb_, lo, r

                pend = issue_z(order[0])
                for ii, kb in enumerate(order):
                    sb_, lo, r = pend
                    ei = cnt["e"] % 2
                    cnt["e"] += 1
                    pi = cnt["pt"] % NPT
                    cnt["pt"] += 1
                    ACTF(eT[ei][:, lo:512], bank(sb_, lo, 512), AF.Exp, [PB[sb_]], [R_eT[ei]])
                    ACTF(spT[pi][:, lo:512], eT[ei][:, lo:512], AF.Ln, [R_eT[ei]], [R_spT[pi]], bias=1.0)
                    if r >= 0:
                        ASEL(spT[pi][:, lo:lo + 128], spT[pi][:, lo:lo + 128], [[1, 128]], ALU.is_gt, 0, -1,
                             [R_spT[pi]], [R_spT[pi]])
                    if ii + 1 < len(order):
                        pend = issue_z(order[ii + 1])
                    last_acc = (ii == 0)
                    MM(bank(sb_, lo, 512), trineg, spT[pi][:, lo:512], False, last_acc,
                       [R_const, R_spT[pi]], [PB[sb_]])
                    if ii > 0:
                        MM(bank(sb_, lo, 512), negones, spsum[si][:, lo:512], False, True,
                           [R_const, R_sps[si]], [PB[sb_]])
                    ACTF(pT[pi][:, lo:512], bank(sb_, lo, 512), AF.Exp, [PB[sb_]], [R_pT[pi]])
                    if r >= 0:
                        ASEL(pT[pi][:, lo:lo + 128], pT[pi][:, lo:lo + 128], [[1, 128]], ALU.is_gt, 0, -1,
                             [R_pT[pi]], [R_pT[pi]])
                    if ii + 1 < len(order):
                        TT("dve", spsum[si][:, lo:512], spsum[si][:, lo:512], spT[pi][:, lo:512], ALU.add,
                           [R_sps[si], R_spT[pi]], [R_sps[si]])
                    MM(bank(ob, lo, 512, p0, p0 + 64), v2[b2][:, hh, kb, :], pT[pi][:, lo:512], ii == 0,
                       ii == len(order) - 1, [R_v2[b2], R_pT[pi]], [PB[ob]])
                oi = cnt["ost"] % 3
                cnt["ost"] += 1
                CP("dve", ost[oi][p0:p0 + 64, :], bank(ob, 0, 512, p0, p0 + 64), [PB[ob]], [R_ost[oi]])
                DMA(ACTQ, os_s[m, p0:p0 + 64, qg * 512:(qg + 1) * 512], ost[oi][p0:p0 + 64, :], [R_ost[oi]], [R_os])
    P.barrier()

    st["off"] = P1
    wbm = A([128, 4, 1024], BF16)
    wbs = A([128, 4, 1024], BF16)
    wo = A([128, 8, 1024], BF16)
    wpg = A([128, 8, 1024], BF16)
    wpp = A([128, 2, 1024], BF16)
    R_w3 = Res("p3w")
    for dst, n in ((wbm, "w_branch_mla"), (wbs, "w_branch_sb"), (wo, "w_out"), (wpg, "w_ple_gate"), (wpp, "w_ple_proj")):
        DMA(WQ, dst, ws[n].rearrange("(c p) n -> p c n", p=128), [R_ws[n]], [R_w3])
    omg = A([128, 4, G], BF16)
    osg = A([128, 4, G], BF16)
    R_omg = Res()
    R_osg = Res()
    sgm = [A([128, G], F32)] * 2
    sgs = [A([128, G], F32)] * 2
    R_sgm = [Res()] * 2
    R_sgs = [Res()] * 2
    tm1 = [A([128, G], F32)] * 2
    tm2 = [A([128, G], F32)] * 2
    R_tm1 = [Res()] * 2
    R_tm2 = [Res()] * 2
    mT = A([128, 8, G], BF16)
    R_mT = Res()
    pg = A([128, NTG, 256], F32)
    R_pg = Res()
    pgb = A([128, 256], BF16)
    R_pgb = Res()
    ppT = A([128, 2, 128], BF16)
    R_ppT = Res()
    sgp = A([128, 1024], F32)
    R_sgp = Res()
    final = []
    for g in range(NG):
        rows = slice(g * G, (g + 1) * G)
        DMA(ACTQ, hg, h1_s[rows, :].rearrange("(t p) d -> p t d", p=128), [R_h1], [R_hg])
        DMA(ACTQ, omg, om_s[:, :, rows].rearrange("m p t -> p m t"), [R_om], [R_omg])
        DMA(ACTQ, osg, os_s[:, :, rows].rearrange("m p t -> p m t"), [R_os], [R_osg])
        DMA(ACTQ, pg, p_d[rows, :].rearrange("(t p) d -> p t d", p=128), [], [R_pg])
        norm_to_uT(gbc["mix_norm"], 0)
        U = uT[0]
        RU = R_uT[0]
        for mb in range(2):
            sGm, RGm = ring_load(Win_v[:, :, C_GATE + mb * 512:C_GATE + (mb + 1) * 512], [128, 8, 512], R_ws["w_in"])
            sGs, RGs = ring_load(Win_v[:, :, C_GATE + 1024 + mb * 512:C_GATE + 1024 + (mb + 1) * 512], [128, 8, 512],
                                 R_ws["w_in"])
            for mm_ in range(4):
                m = mb * 4 + mm_
                i2 = m % 2
                for c in range(8):
                    MM(bank(0), sGm[:, c, mm_ * 128:(mm_ + 1) * 128], U[:, c, :], c == 0, c == 7, [RGm, RU], [PB[0]])
                for c in range(8):
                    MM(bank(1), sGs[:, c, mm_ * 128:(mm_ + 1) * 128], U[:, c, :], c == 0, c == 7, [RGs, RU], [PB[1]])
                for k in range(4):
                    MM(bank(2), wbm[:, k, m * 128:(m + 1) * 128], omg[:, k, :], k == 0, k == 3, [R_w3, R_omg], [PB[2]])
                for k in range(4):
                    MM(bank(3), wbs[:, k, m * 128:(m + 1) * 128], osg[:, k, :], k == 0, k == 3, [R_w3, R_osg], [PB[3]])
                ACTF(sgm[i2], bank(0), AF.Sigmoid, [PB[0]], [R_sgm[i2]])
                ACTF(sgs[i2], bank(1), AF.Sigmoid, [PB[1]], [R_sgs[i2]])
                TT("dve", tm1[i2], bank(2), sgm[i2], ALU.mult, [PB[2], R_sgm[i2]], [R_tm1[i2]])
                TT("dve", tm2[i2], bank(3), sgs[i2], ALU.mult, [PB[3], R_sgs[i2]], [R_tm2[i2]])
                TT("dve", mT[:, m, :], tm1[i2], tm2[i2], ALU.add, [R_tm1[i2], R_tm2[i2]], [R_mT])
        for t in range(NTG):
            tc_ = slice(t * 128, (t + 1) * 128)
            for hf in range(2):
                bk = 4 + (t % 2) * 2 + hf
                for c in range(8):
                    MM(bank(bk), mT[:, c, tc_], wo[:, c, hf * 512:(hf + 1) * 512], c == 0, c == 7, [R_mT, R_w3], [PB[bk]])
                dst = hg[:, t, hf * 512:(hf + 1) * 512]
                TT("dve", dst, bank(bk), dst, ALU.add, [PB[bk], R_hg], [R_hg])
        norm_to_uT(gbc["ffn2_norm"], 1)
        ffn("ffn2_w_in", "ffn2_w_out", 1)
        norm_to_uT(gbc["ple_norm"], 0)
        for t in range(NTG):
            tc_ = slice(t * 128, (t + 1) * 128)
            CP("dve", pgb, pg[:, t, :], [R_pg], [R_pgb])
            bp = bankbf(2)[:, 0:256].rearrange("p (c k) -> p c k", c=2)
            for c in range(2):
                TR(bp[:, c, :], pgb[:, c * 128:(c + 1) * 128], ident, [R_pgb, R_const], [PB[2]])
            CP("act", ppT, bp, [PB[2]], [R_ppT])
            for hf in range(2):
                bk = 4 + hf
                for c in range(8):
                    MM(bank(bk), U[:, c, tc_], wpg[:, c, hf * 512:(hf + 1) * 512], c == 0, c == 7, [RU, R_w3], [PB[bk]])
                ACTF(sgp[:, hf * 512:(hf + 1) * 512], bank(bk), AF.Sigmoid, [PB[bk]], [R_sgp])
                bk2 = 6 + hf
                for c in range(2):
                    MM(bank(bk2), ppT[:, c, :], wpp[:, c, hf * 512:(hf + 1) * 512], c == 0, c == 1, [R_ppT, R_w3], [PB[bk2]])
                TT("dve", sgp[:, hf * 512:(hf + 1) * 512], bank(bk2), sgp[:, hf * 512:(hf + 1) * 512], ALU.mult,
                   [PB[bk2], R_sgp], [R_sgp])
                dst = hg[:, t, hf * 512:(hf + 1) * 512]
                TT("dve", dst, dst, sgp[:, hf * 512:(hf + 1) * 512], ALU.add, [R_sgp, R_hg], [R_hg])
        final.append(DMA(ACTQ, out_d[rows, :].rearrange("(t p) d -> p t d", p=128), hg, [R_hg], [R_out]))

    print("arena marks", PERSIST, P1, st["off"], {e: len(P.ins[e]) for e in P.ENG})
    P.emit(nc, final_waits=final)
    es.close()
    return nc


_NC = None


def kernel(**inputs):
    global _NC
    if _NC is None:
        _NC = build_nc()
    nc = _NC
    x = np.asarray(inputs["x"], dtype=np.float32)
    p = np.asarray(inputs["p"], dtype=np.float32)
    pos = np.asarray(inputs["positions"], dtype=np.int32)
    common = {}
    for n, r, c in WEIGHTS:
        common[n] = np.ascontiguousarray(np.asarray(inputs[n], dtype=np.float32).reshape(r, c))
    for n, c in GAINS:
        common[n] = np.ascontiguousarray(np.asarray(inputs[n], dtype=np.float32).reshape(1, c))
    in_maps = []
    for b in range(8):
        m = dict(common)
        m["x"] = np.ascontiguousarray(x[b])
        m["p"] = np.ascontiguousarray(p[0, b])
        m["posT"] = np.ascontiguousarray(pos[b].reshape(32, 128).T)
        in_maps.append(m)
    res = run_bass_kernel_spmd(nc, in_maps, core_ids=list(range(8)))
    out = np.stack([np.asarray(r["out"], dtype=np.float32) for r in res.results], axis=0)
    return out
```

```python
import math
from contextlib import ExitStack

import numpy as np
import concourse.bass as bass
import concourse.mybir as mybir
from concourse.bass_utils import run_bass_kernel_spmd

F32 = mybir.dt.float32
BF16 = mybir.dt.bfloat16
I32 = mybir.dt.int32
AF = mybir.ActivationFunctionType
ALU = mybir.AluOpType
AX = mybir.AxisListType

T = 4096
D = 1024
DFF = 2816
G = 512
NG = T // G
NTG = G // 128
NJ = DFF // 128
EPS = 1e-6
DEBUG = False


class Res:
    __slots__ = ("name", "w", "r", "excl")

    def __init__(self, name="", excl=False):
        self.name = name
        self.w = None
        self.r = []
        self.excl = excl


class Prog:
    ENG = ["pe", "act", "dve", "pool", "sp"]
    EPOCH = 4096
    NDS = 16

    def __init__(self):
        self.ins = {e: [] for e in self.ENG}
        self.ndma = {e: 0 for e in self.ENG}
        self.dmas = {e: [] for e in self.ENG}

    def op(self, eng, fn, reads=(), writes=(), dma=False, extra=()):
        idx = len(self.ins[eng])
        me = (eng, idx)
        deps = set(extra)
        if any(r.excl for r in reads):
            writes = list(writes) + [r for r in reads if r.excl and r not in writes]
            reads = [r for r in reads if not r.excl]
        for r in reads:
            if r.w is not None:
                deps.add(r.w)
        for w in writes:
            if w.w is not None:
                deps.add(w.w)
            for rd in w.r:
                if rd[0] == eng and not dma and not self.ins[eng][rd[1]]["dma"]:
                    continue
                deps.add(rd)
        fd = set()
        for d in deps:
            if d == me:
                continue
            if d[0] == eng and eng == "pe":
                continue
            fd.add(d)
        rec = dict(fn=fn, deps=fd, dma=dma, signal=False, dj=None)
        if dma:
            rec["dj"] = self.ndma[eng]
            self.ndma[eng] += 1
            self.dmas[eng].append(me)
        self.ins[eng].append(rec)
        for d in fd:
            self.ins[d[0]][d[1]]["signal"] = True
        for r in reads:
            r.r.append(me)
        for w in writes:
            w.w = me
            w.r = []
        return me

    def barrier(self):
        tails = []
        for e in self.ENG:
            last = None
            for i in range(len(self.ins[e]) - 1, -1, -1):
                if not self.ins[e][i]["dma"]:
                    last = (e, i)
                    break
            if last is not None:
                tails.append(last)
            tails.extend(self.dmas[e][-self.NDS:])
        for e in self.ENG:
            self.op(e, lambda en: en.nop(), extra=[t for t in tails])

    def emit(self, nc, final_waits=()):
        with ExitStack() as es:
            esem = {}
            for e in self.ENG:
                k = 0
                for r in self.ins[e]:
                    if r["signal"] and not r["dma"]:
                        r["k"] = k
                        k += 1
                nep = (k + self.EPOCH - 1) // self.EPOCH
                esem[e] = [es.enter_context(nc.semaphore(f"s_{e}_{i}")) for i in range(nep)]
            dsem = {}
            for e in self.ENG:
                if self.ndma[e]:
                    dsem[e] = [es.enter_context(nc.semaphore(f"d_{e}_{i}")) for i in range(self.NDS)]
            block = es.enter_context(nc.Block())
            prog = self

            def run(e, engobj):
                seen = {}

                def wait_compute(e2, k):
                    if seen.get(e2, -1) >= k:
                        return
                    seen[e2] = k
                    engobj.wait_ge(esem[e2][k // prog.EPOCH], k % prog.EPOCH + 1)

                def wait_dma(q, j):
                    key = ("d", q, j % prog.NDS)
                    if seen.get(key, -1) >= j:
                        return
                    seen[key] = j
                    engobj.wait_ge(dsem[q][j % prog.NDS], 16 * (j // prog.NDS + 1))

                for r in prog.ins[e]:
                    for d in sorted(r["deps"]):
                        p = prog.ins[d[0]][d[1]]
                        if p["dma"]:
                            wait_dma(d[0], p["dj"])
                        else:
                            wait_compute(d[0], p["k"])
                    if r["dma"]:
                        j = r["dj"]
                        if j >= prog.NDS:
                            wait_dma(e, j - prog.NDS)
                        bi = r["fn"](engobj)
                        bi.then_inc(dsem[e][j % prog.NDS], 16)
                    else:
                        bi = r["fn"](engobj)
                        if r["signal"]:
                            k = r["k"]
                            bi.then_inc(esem[e][k // prog.EPOCH], 1)
                if e == "sp":
                    for d in final_waits:
                        p = prog.ins[d[0]][d[1]]
                        wait_dma(d[0], p["dj"])

            @block.tensor
            def _(eng):
                run("pe", eng)

            @block.scalar
            def _(eng):
                run("act", eng)

            @block.vector
            def _(eng):
                run("dve", eng)

            @block.gpsimd
            def _(eng):
                run("pool", eng)

            @block.sync
            def _(eng):
                run("sp", eng)


WEIGHTS = [
    ("ffn1_w_in", 1024, 5632), ("ffn1_w_out", 2816, 1024), ("w_in", 1024, 4256),
    ("w_q_up", 384, 768), ("w_kv_up", 256, 1024), ("w_branch_mla", 512, 1024),
    ("w_branch_sb", 512, 1024), ("w_out", 1024, 1024), ("ffn2_w_in", 1024, 5632),
    ("ffn2_w_out", 2816, 1024), ("w_ple_gate", 1024, 1024), ("w_ple_proj", 256, 1024),
]
GAINS = [("ffn1_norm", 1024), ("mix_norm", 1024), ("q_latent_norm", 384), ("kv_latent_norm", 256),
         ("q_head_norm", 96), ("k_head_norm", 96), ("ffn2_norm", 1024), ("ple_norm", 1024)]

C_SBQ = 672
C_SBK = C_SBQ + 512
C_SBV = C_SBK + 512
C_GATE = C_SBV + 512


def build_nc():
    nc = bass.Bass("TRN2", target_bir_lowering=False)
    P = Prog()
    ACTQ = "pool"
    WQ = "sp"

    def dram(name, shape, dt, kind):
        return nc.dram_tensor(name, shape, dt, kind=kind).ap()

    x_d = dram("x", [T, D], F32, "ExternalInput")
    p_d = dram("p", [T, 256], F32, "ExternalInput")
    pos_d = dram("posT", [128, 32], I32, "ExternalInput")
    w_d = {n: dram(n, [r, c], F32, "ExternalInput") for n, r, c in WEIGHTS}
    g_d = {n: dram(n, [1, c], F32, "ExternalInput") for n, c in GAINS}
    out_d = dram("out", [T, D], F32, "ExternalOutput")
    sk = "ExternalOutput" if DEBUG else "Internal"
    ws = {n: dram("s_" + n, [r, c], BF16, "Internal") for n, r, c in WEIGHTS}
    h1_s = dram("s_h1", [T, D], F32, sk)
    qt_s = dram("s_qt", [8, 96, T], BF16, sk)
    kt_s = dram("s_kt", [8, 96, T], BF16, sk)
    vm_s = dram("s_vm", [8, 128, 32, 64], BF16, sk)
    sqt_s = dram("s_sqt", [4, 128, T], BF16, sk)
    skt_s = dram("s_skt", [4, 128, T], BF16, sk)
    vs_s = dram("s_vs", [8, 128, 32, 64], BF16, sk)
    om_s = dram("s_om", [4, 128, T], BF16, sk)
    os_s = dram("s_os", [4, 128, T], BF16, sk)
    R_ws = {n: Res(n) for n, _, _ in WEIGHTS}
    R_h1, R_qt, R_kt, R_vm, R_sqt, R_skt, R_vs, R_om, R_os, R_out = [Res() for _ in range(10)]

    es = ExitStack()
    ARENA_N = 103 * 1024
    arena = es.enter_context(nc.sbuf_tensor("arena", [128, ARENA_N], BF16))
    ps = es.enter_context(nc.psum_tensor("ps", [128, 4096], F32))
    PB = [Res(f"bank{b}", excl=True) for b in range(8)]

    def bank(b, lo=0, hi=512, p0=0, p1=128):
        return ps[p0:p1, b * 512 + lo:b * 512 + hi]

    def bankbf(b):
        return ps[:, b * 512:(b + 1) * 512].bitcast(BF16)

    st = {"off": 0}

    def A(shape, dt):
        n = 1
        for s_ in shape[1:]:
            n *= s_
        nb = n * (2 if dt == BF16 else 4)
        ne = ((nb + 31) // 32) * 16
        off = st["off"]
        assert off + ne <= ARENA_N, ("arena overflow", off, ne)
        st["off"] = off + ne
        sl = arena[:, off:off + nb // 2]
        if dt != BF16:
            sl = sl.bitcast(dt)
        if len(shape) > 2:
            names = " ".join(f"d{i}" for i in range(len(shape) - 1))
            kw = {f"d{i}": shape[i + 1] for i in range(len(shape) - 1)}
            sl = sl.rearrange(f"p ({names}) -> p {names}", **kw)
        if shape[0] < 128:
            sl = sl[0:shape[0]]
        return sl

    def MM(out, lhsT, rhs, start, stop, r, w):
        P.op("pe", lambda e: e.matmul(out, lhsT=lhsT, rhs=rhs, start=start, stop=stop), r, w)

    def TR(out, in_, ident, r, w):
        P.op("pe", lambda e: e.transpose(out=out, in_=in_, identity=ident), r, w)

    def ACTF(out, in_, func, r, w, bias=None, scale=None, accum=None):
        kw = {}
        if bias is not None:
            kw["bias"] = bias
        if scale is not None:
            kw["scale"] = scale
        if accum is not None:
            kw["accum_out"] = accum
        P.op("act", lambda e: e.activation(out=out, in_=in_, func=func, **kw), r, w)

    def TT(eng, out, in0, in1, op, r, w):
        P.op(eng, lambda e: e.tensor_tensor(out=out, in0=in0, in1=in1, op=op), r, w)

    def TS(eng, out, in0, s1, op0, r, w, s2=None, op1=None):
        if op1 is None:
            P.op(eng, lambda e: e.tensor_scalar(out=out, in0=in0, scalar1=s1, scalar2=None, op0=op0), r, w)
        else:
            P.op(eng, lambda e: e.tensor_scalar(out=out, in0=in0, scalar1=s1, scalar2=s2, op0=op0, op1=op1), r, w)

    def STT(out, in0, scalar, in1, op0, op1, r, w):
        P.op("dve", lambda e: e.scalar_tensor_tensor(out=out, in0=in0, scalar=scalar, in1=in1, op0=op0, op1=op1), r, w)

    def CP(eng, out, in_, r, w, scale=None):
        if eng == "act":
            if scale is None:
                P.op("act", lambda e: e.activation(out=out, in_=in_, func=AF.Copy), r, w)
            else:
                P.op("act", lambda e: e.activation(out=out, in_=in_, func=AF.Copy, scale=scale), r, w)
        else:
            if scale is None:
                P.op(eng, lambda e: e.tensor_copy(out=out, in_=in_), r, w)
            else:
                P.op(eng, lambda e: e.tensor_scalar(out=out, in0=in_, scalar1=scale, scalar2=None, op0=ALU.mult), r, w)

    def DMA(q, out, in_, r, w):
        return P.op(q, lambda e: e.dma_start(out=out, in_=in_), r, w, dma=True)

    def MEMSET(eng, ap, val, w):
        P.op(eng, lambda e: e.memset(ap, val), (), w)

    def ASEL(out, in_, pattern, cmp, base, cm, r, w):
        P.op("pool", lambda e: e.affine_select(out=out, in_=in_, pattern=pattern, compare_op=cmp, fill=0.0,
                                                base=base, channel_multiplier=cm), r, w)

    ident = A([128, 128], BF16)
    onesb = A([128, 128], BF16)
    negones = A([128, 128], BF16)
    trineg = A([128, 128], BF16)
    epst = A([128, 1], F32)
    cosT = A([128, 32, 16], F32)
    sinT = A([128, 32, 16], F32)
    R_const = Res("const")
    R_cs = Res("cossin")
    MEMSET("pool", onesb, 1.0, [R_const])
    MEMSET("pool", negones, -1.0, [R_const])
    MEMSET("pool", epst, EPS, [R_const])
    ASEL(ident, onesb, [[-1, 128]], ALU.is_equal, 0, 1, [R_const], [R_const])
    ASEL(trineg, negones, [[-1, 128]], ALU.is_ge, 0, 1, [R_const], [R_const])
    gbc = {n: A([128, c], F32) for n, c in GAINS if c == 1024}
    R_g = Res("gains")
    for n in gbc:
        DMA(ACTQ, gbc[n], g_d[n].broadcast_to([128, 1024]), [], [R_g])
    PERSIST = st["off"]

    posi = A([128, 32], I32)
    posf = A([128, 32], F32)
    invf = A([128, 16], F32)
    ang = A([128, 32, 16], F32)
    kq = A([128, 32, 16], F32)
    ki = A([128, 32, 16], I32)
    r1 = A([128, 32, 16], F32)
    r2 = A([128, 32, 16], F32)
    Rt = Res("ropetmp")
    DMA(ACTQ, posi, pos_d, [], [Rt])
    CP("dve", posf, posi, [Rt], [Rt])
    inv_np = (np.float32(10000.0) ** (-np.arange(0, 32, 2, dtype=np.float32) / np.float32(32))).astype(np.float32)
    for j in range(16):
        MEMSET("dve", invf[:, j:j + 1], float(inv_np[j]), [Rt])
    TT("dve", ang, posf.unsqueeze(2).broadcast_to([128, 32, 16]), invf.unsqueeze(1).broadcast_to([128, 32, 16]),
       ALU.mult, [Rt], [Rt])
    TWO_PI = 2.0 * math.pi
    C1 = 6.28125
    C2 = TWO_PI - C1
    for (dst, shift) in ((sinT, 0.0), (cosT, math.pi / 2)):
        TS("dve", r1, ang, shift, ALU.add, [Rt], [Rt])
        TS("dve", kq, r1, 1.0 / TWO_PI, ALU.mult, [Rt], [Rt])
        CP("dve", ki, kq, [Rt], [Rt])
        CP("dve", kq, ki, [Rt], [Rt])
        STT(r1, kq, -C1, r1, ALU.mult, ALU.add, [Rt], [Rt])
        STT(r1, kq, -C2, r1, ALU.mult, ALU.add, [Rt], [Rt])
        TS("dve", r2, r1, math.pi, ALU.is_gt, [Rt], [Rt], s2=-TWO_PI, op1=ALU.mult)
        TT("dve", r1, r1, r2, ALU.add, [Rt], [Rt])
        TS("dve", r2, r1, -math.pi, ALU.is_lt, [Rt], [Rt], s2=TWO_PI, op1=ALU.mult)
        TT("dve", r1, r1, r2, ALU.add, [Rt], [Rt])
        TS("dve", r1, r1, math.pi, ALU.min, [Rt], [Rt], s2=-math.pi, op1=ALU.max)
        ACTF(dst, r1, AF.Sin, [Rt], [R_cs])

    stg_f = [A([128, 5632], F32) for _ in range(2)]
    stg_b = [A([128, 5632], BF16) for _ in range(2)]
    R_sf = [Res(), Res()]
    R_sb = [Res(), Res()]
    ui = 0
    for n, r, c in WEIGHTS:
        nr = r // 128
        per = max(1, 5632 // c)
        c0 = 0
        while c0 < nr:
            k = min(per, nr - c0)
            b = ui % 2
            src = w_d[n][c0 * 128:(c0 + k) * 128, :].rearrange("(i p) n -> p i n", p=128)
            dst = ws[n][c0 * 128:(c0 + k) * 128, :].rearrange("(i p) n -> p i n", p=128)
            sf = stg_f[b][:, 0:k * c].rearrange("p (i n) -> p i n", i=k)
            sb_ = stg_b[b][:, 0:k * c].rearrange("p (i n) -> p i n", i=k)
            DMA(WQ, sf, src, [], [R_sf[b]])
            CP("dve" if ui % 2 == 0 else "act", sb_, sf, [R_sf[b]], [R_sb[b]])
            DMA(ACTQ, dst, sb_, [R_sb[b]], [R_ws[n]])
            ui += 1
            c0 += k
    P.barrier()
    st["off"] = PERSIST

    RING_N = 4
    ring = [A([128, 4096], BF16) for _ in range(RING_N)]
    R_ring = [Res(f"ring{i}") for i in range(RING_N)]
    rst = {"i": 0}

    def ring_load(src_ap, shape, Rsrc):
        i = rst["i"] % RING_N
        rst["i"] += 1
        n = 1
        for s_ in shape[1:]:
            n *= s_
        dst = ring[i][:, 0:n]
        if len(shape) == 3:
            dst = dst.rearrange("p (a b) -> p a b", a=shape[1])
        DMA(WQ, dst, src_ap, [Rsrc], [R_ring[i]])
        return dst, R_ring[i]

    hg = A([128, NTG, 1024], F32)
    R_hg = Res("hg")
    _uT0 = A([128, 8, G], BF16)
    uT = [_uT0, _uT0]
    _RuT = Res("uT")
    R_uT = [_RuT, _RuT]
    u_tm = [A([128, 1024], BF16) for _ in range(2)]
    R_utm = [Res(), Res()]
    junk = A([128, 1024], BF16)
    R_junk = Res()
    gT = A([128, NJ, G], BF16)
    R_gT = Res("gT")
    sa = [A([128, G], BF16) for _ in range(2)]
    R_sa = [Res(), Res()]
    NSTAT = 12
    stat = A([128, NSTAT * 3 * 8], F32)
    sst = {"i": 0}

    def stat3():
        i = sst["i"] % NSTAT
        sst["i"] += 1
        base = i * 24
        return [stat[:, base + 8 * k: base + 8 * k + 8] for k in range(3)], Res()

    def rstd_from_ssq(ssq, lnv, rstd, n, ncol, Rs):
        ACTF(lnv[:, 0:ncol], ssq[:, 0:ncol], AF.Ln, [Rs, R_const], [Rs], bias=epst[:, 0:1], scale=1.0 / n)
        ACTF(rstd[:, 0:ncol], lnv[:, 0:ncol], AF.Exp, [Rs], [Rs], scale=-0.5)

    def norm_to_uT(gain, ui_):
        (ssq, lnv, rstd), Rs = stat3()
        for t in range(NTG):
            ACTF(junk, hg[:, t, :], AF.Square, [R_hg], [R_junk, Rs], accum=ssq[:, t:t + 1])
        rstd_from_ssq(ssq, lnv, rstd, 1024.0, NTG, Rs)
        for t in range(NTG):
            b = t % 2
            STT(u_tm[b], hg[:, t, :], rstd[:, t:t + 1], gain, ALU.mult, ALU.mult, [R_hg, Rs, R_g], [R_utm[b]])
            for c in range(8):
                TR(bankbf(b)[:, c * 128:(c + 1) * 128], u_tm[b][:, c * 128:(c + 1) * 128], ident,
                   [R_utm[b], R_const], [PB[b]])
            CP("act" if t % 2 == 0 else "dve", uT[ui_][:, :, t * 128:(t + 1) * 128],
               bankbf(b).rearrange("p (c k) -> p c k", c=8), [PB[b]], [R_uT[ui_]])

    JB = [(j0, min(j0 + 4, NJ)) for j0 in range(0, NJ, 4)]

    def ffn(w1n, w2n, ui_):
        W1 = ws[w1n].rearrange("(c p) n -> p c n", p=128)
        W2 = ws[w2n].rearrange("(j p) n -> p j n", p=128)
        for (j0, j1) in JB:
            nj = j1 - j0
            sA, RA = ring_load(W1[:, :, j0 * 128:j1 * 128], [128, 8, nj * 128], R_ws[w1n])
            sB, RB = ring_load(W1[:, :, DFF + j0 * 128:DFF + j1 * 128], [128, 8, nj * 128], R_ws[w1n])
            for jj in range(nj):
                j = j0 + jj
                ab = 2 * (j % 2)
                bb = ab + 1
                for c in range(8):
                    MM(bank(ab), sA[:, c, jj * 128:(jj + 1) * 128], uT[ui_][:, c, :], c == 0, c == 7,
                       [RA, R_uT[ui_]], [PB[ab]])
                for c in range(8):
                    MM(bank(bb), sB[:, c, jj * 128:(jj + 1) * 128], uT[ui_][:, c, :], c == 0, c == 7,
                       [RB, R_uT[ui_]], [PB[bb]])
                ACTF(sa[j % 2], bank(ab), AF.Silu, [PB[ab]], [R_sa[j % 2]])
                TT("dve", gT[:, j, :], bank(bb), sa[j % 2], ALU.mult, [PB[bb], R_sa[j % 2]], [R_gT])
        for tp in range(2):
            for (j0, j1) in JB:
                nj = j1 - j0
                sW, RW = ring_load(W2[:, j0:j1, :], [128, nj, 1024], R_ws[w2n])
                for jj in range(nj):
                    j = j0 + jj
                    for t in (2 * tp, 2 * tp + 1):
                        for hf in range(2):
                            bk = 4 + (t % 2) * 2 + hf
                            MM(bank(bk), gT[:, j, t * 128:(t + 1) * 128], sW[:, jj, hf * 512:(hf + 1) * 512],
                               j == 0, j == NJ - 1, [RW, R_gT], [PB[bk]])
            for t in (2 * tp, 2 * tp + 1):
                for hf in range(2):
                    bk = 4 + (t % 2) * 2 + hf
                    dst = hg[:, t, hf * 512:(hf + 1) * 512]
                    STT(dst, bank(bk), 0.5, dst, ALU.mult, ALU.add, [PB[bk], R_hg], [R_hg])

    P1 = st["off"]
    wtok = A([128, 8, 672], BF16)
    wv = A([128, 8, 512], BF16)
    wq = A([128, 3, 768], BF16)
    wkv = A([128, 2, 1024], BF16)
    R_w1 = Res("p1w")
    Win_v = ws["w_in"].rearrange("(c p) n -> p c n", p=128)
    DMA(WQ, wtok, Win_v[:, :, 0:672], [R_ws["w_in"]], [R_w1])
    DMA(WQ, wv, Win_v[:, :, C_SBV:C_SBV + 512], [R_ws["w_in"]], [R_w1])
    DMA(WQ, wq, ws["w_q_up"].rearrange("(c p) n -> p c n", p=128), [R_ws["w_q_up"]], [R_w1])
    DMA(WQ, wkv, ws["w_kv_up"].rearrange("(c p) n -> p c n", p=128), [R_ws["w_kv_up"]], [R_w1])
    gql = A([128, 384], F32)
    gkvl = A([128, 256], F32)
    gqh = A([128, 8, 96], F32)
    gkh = A([128, 8, 96], F32)
    DMA(ACTQ, gql, g_d["q_latent_norm"].broadcast_to([128, 384]), [], [R_g])
    DMA(ACTQ, gkvl, g_d["kv_latent_norm"].broadcast_to([128, 256]), [], [R_g])
    DMA(ACTQ, gqh, bass.AP(g_d["q_head_norm"].tensor, 0, [[0, 128], [0, 8], [1, 96]]), [], [R_g])
    DMA(ACTQ, gkh, bass.AP(g_d["k_head_norm"].tensor, 0, [[0, 128], [0, 8], [1, 96]]), [], [R_g])
    TS("dve", gqh, gqh, 1.0 / math.sqrt(96.0), ALU.mult, [R_g], [R_g])
    cql = A([128, 672], F32)
    R_cql = Res()
    lat_bf = A([128, 640], BF16)
    R_lat = Res()
    latT = A([128, 5, 128], BF16)
    R_latT = Res()
    qf = A([128, 8, 96], F32)
    R_qf = Res()
    kf = A([128, 8, 96], F32)
    R_kf = Res()
    sq = A([128, 8, 96], F32)
    R_sq = Res()
    rt = [A([128, 8, 16], F32) for _ in range(4)]
    R_rt = Res()
    qb = A([128, 8, 96], BF16)
    R_qb = Res()
    qTg = A([96, 8, G], BF16)
    R_qTg = Res()
    kTg = A([96, 8, G], BF16)
    R_kTg = Res()
    vmg = A([128, NTG, 8, 64], BF16)
    R_vmg = Res()
    vsg = A([128, NTG, 8, 64], BF16)
    R_vsg = Res()
    sqTg = A([128, 4, G], BF16)
    R_sqTg = Res()
    skTg = A([128, 4, G], BF16)
    R_skTg = Res()

    def head_norm_rope(src, Rsrc, gains, tile_idx, dstT, RdstT, tcol):
        (ssq, lnv, rstd), Rs = stat3()
        TT("dve", sq, src, src, ALU.mult, [Rsrc], [R_sq])
        P.op("dve", lambda e: e.tensor_reduce(out=ssq[:, 0:8], in_=sq, axis=AX.X, op=ALU.add), [R_sq], [Rs])
        rstd_from_ssq(ssq, lnv, rstd, 96.0, 8, Rs)
        TT("dve", src, src, gains, ALU.mult, [Rsrc, R_g], [Rsrc])
        TT("dve", src, src, rstd[:, 0:8].unsqueeze(2).broadcast_to([128, 8, 96]), ALU.mult, [Rsrc, Rs], [Rsrc])
        cs = cosT[:, tile_idx, :].unsqueeze(1).broadcast_to([128, 8, 16])
        sn = sinT[:, tile_idx, :].unsqueeze(1).broadcast_to([128, 8, 16])
        x1 = src[:, :, 64:80]
        x2 = src[:, :, 80:96]
        TT("dve", rt[0], x1, cs, ALU.mult, [Rsrc, R_cs], [R_rt])
        TT("dve", rt[1], x2, sn, ALU.mult, [Rsrc, R_cs], [R_rt])
        TT("dve", rt[2], x2, cs, ALU.mult, [Rsrc, R_cs], [R_rt])
        TT("dve", rt[3], x1, sn, ALU.mult, [Rsrc, R_cs], [R_rt])
        CP("dve", qb[:, :, 0:64], src[:, :, 0:64], [Rsrc], [R_qb])
        TT("dve", qb[:, :, 64:80], rt[0], rt[1], ALU.subtract, [R_rt], [R_qb])
        TT("dve", qb[:, :, 80:96], rt[2], rt[3], ALU.add, [R_rt], [R_qb])
        bq = bankbf(2)[0:96].rearrange("p (h k) -> p h k", h=8)
        for h in range(8):
            TR(bq[:, h, :], qb[:, h, :], ident, [R_qb, R_const], [PB[2]])
        CP("act", dstT[:, :, tcol * 128:(tcol + 1) * 128], bq, [PB[2]], [RdstT])

    for g in range(NG):
        rows = slice(g * G, (g + 1) * G)
        DMA(ACTQ, hg, x_d[rows, :].rearrange("(t p) d -> p t d", p=128), [], [R_hg])
        norm_to_uT(gbc["ffn1_norm"], 0)
        ffn("ffn1_w_in", "ffn1_w_out", 0)
        DMA(ACTQ, h1_s[rows, :].rearrange("(t p) d -> p t d", p=128), hg, [R_hg], [R_h1])
        norm_to_uT(gbc["mix_norm"], 1)
        U = uT[1]
        RU = R_uT[1]
        for (c0, dstT, RdT, scl) in ((C_SBQ, sqTg, R_sqTg, 0.125), (C_SBK, skTg, R_skTg, None)):
            sW, RW = ring_load(Win_v[:, :, c0:c0 + 512], [128, 8, 512], R_ws["w_in"])
            for m in range(4):
                bk = m % 2
                for c in range(8):
                    MM(bank(bk), sW[:, c, m * 128:(m + 1) * 128], U[:, c, :], c == 0, c == 7, [RW, RU], [PB[bk]])
                CP("act" if m % 2 == 0 else "dve", dstT[:, m, :], bank(bk), [PB[bk]], [RdT], scale=scl)
        DMA(ACTQ, sqt_s[:, :, rows].rearrange("m p t -> p m t"), sqTg, [R_sqTg], [R_sqt])
        DMA(ACTQ, skt_s[:, :, rows].rearrange("m p t -> p m t"), skTg, [R_skTg], [R_skt])
        for t in range(NTG):
            tl = g * NTG + t
            tc_ = slice(t * 128, (t + 1) * 128)
            for c in range(8):
                MM(bank(3), U[:, c, tc_], wv[:, c, :], c == 0, c == 7, [RU, R_w1], [PB[3]])
            CP("act", vsg[:, t].rearrange("p h d -> p (h d)"), bank(3), [PB[3]], [R_vsg])
            for c in range(8):
                MM(bank(4), U[:, c, tc_], wtok[:, c, 0:512], c == 0, c == 7, [RU, R_w1], [PB[4]])
            for c in range(8):
                MM(bank(5, 0, 160), U[:, c, tc_], wtok[:, c, 512:672], c == 0, c == 7, [RU, R_w1], [PB[5]])
            CP("dve", cql[:, 0:512], bank(4), [PB[4]], [R_cql])
            CP("act", cql[:, 512:672], bank(5, 0, 160), [PB[5]], [R_cql])
            (ssq, lnv, rstd), Rs = stat3()
            ACTF(junk[:, 0:384], cql[:, 0:384], AF.Square, [R_cql], [R_junk, Rs], accum=ssq[:, 0:1])
            ACTF(junk[:, 0:256], cql[:, 384:640], AF.Square, [R_cql], [R_junk, Rs], accum=ssq[:, 1:2])
            ACTF(lnv[:, 0:1], ssq[:, 0:1], AF.Ln, [Rs, R_const], [Rs], bias=epst[:, 0:1], scale=1.0 / 384)
            ACTF(lnv[:, 1:2], ssq[:, 1:2], AF.Ln, [Rs, R_const], [Rs], bias=epst[:, 0:1], scale=1.0 / 256)
            ACTF(rstd[:, 0:2], lnv[:, 0:2], AF.Exp, [Rs], [Rs], scale=-0.5)
            STT(lat_bf[:, 0:384], cql[:, 0:384], rstd[:, 0:1], gql, ALU.mult, ALU.mult, [R_cql, Rs, R_g], [R_lat])
            STT(lat_bf[:, 384:640], cql[:, 384:640], rstd[:, 1:2], gkvl, ALU.mult, ALU.mult, [R_cql, Rs, R_g], [R_lat])
            bl = bankbf(3)[:, 0:640].rearrange("p (c k) -> p c k", c=5)
            for c in range(5):
                TR(bl[:, c, :], lat_bf[:, c * 128:(c + 1) * 128], ident, [R_lat, R_const], [PB[3]])
            CP("dve", latT, bl, [PB[3]], [R_latT])
            for c in range(3):
                MM(bank(4), latT[:, c, :], wq[:, c, 0:512], c == 0, c == 2, [R_latT, R_w1], [PB[4]])
            for c in range(3):
                MM(bank(5, 0, 256), latT[:, c, :], wq[:, c, 512:768], c == 0, c == 2, [R_latT, R_w1], [PB[5]])
            qflat = qf.rearrange("p h d -> p (h d)")
            CP("act", qflat[:, 0:512], bank(4), [PB[4]], [R_qf])
            CP("dve", qflat[:, 512:768], bank(5, 0, 256), [PB[5]], [R_qf])
            for hf in range(2):
                for c in range(2):
                    MM(bank(6 + hf), latT[:, 3 + c, :], wkv[:, c, hf * 512:(hf + 1) * 512], c == 0, c == 1,
                       [R_latT, R_w1], [PB[6 + hf]])
            for hf in range(2):
                bv = bank(6 + hf).rearrange("p (h d) -> p h d", h=4)
                CP("act" if hf == 0 else "dve", kf[:, hf * 4:(hf + 1) * 4, 0:64], bv[:, :, 0:64], [PB[6 + hf]], [R_kf])
                CP("dve" if hf == 0 else "act", vmg[:, t, hf * 4:(hf + 1) * 4, :], bv[:, :, 64:128], [PB[6 + hf]], [R_vmg])
            CP("dve", kf[:, :, 64:96], cql[:, 640:672].unsqueeze(1).broadcast_to([128, 8, 32]), [R_cql], [R_kf])
            head_norm_rope(qf, R_qf, gqh, tl, qTg, R_qTg, t)
            head_norm_rope(kf, R_kf, gkh, tl, kTg, R_kTg, t)
        DMA(ACTQ, qt_s[:, :, rows].rearrange("h d t -> d h t"), qTg, [R_qTg], [R_qt])
        DMA(ACTQ, kt_s[:, :, rows].rearrange("h d t -> d h t"), kTg, [R_kTg], [R_kt])
        for h in range(8):
            DMA(ACTQ, vm_s[h, :, g * NTG:(g + 1) * NTG, :], vmg[:, :, h, :], [R_vmg], [R_vm])
            DMA(ACTQ, vs_s[h, :, g * NTG:(g + 1) * NTG, :], vsg[:, :, h, :], [R_vsg], [R_vs])
    P.barrier()

    st["off"] = PERSIST
    ones64 = onesb[:, 0:64]
    mq = [A([128, T], BF16) for _ in range(4)]
    mk = [A([128, T], BF16) for _ in range(4)]
    mv = [A([128, 32, 64], BF16) for _ in range(4)]
    R_mq = [Res() for _ in range(4)]
    R_mk = [Res() for _ in range(4)]
    R_mv = [Res() for _ in range(4)]
    svb = [A([128, 2, 32, 64], BF16) for _ in range(2)]
    R_svb = [Res(), Res()]
    pTh = [[A([128, 512], BF16) for _ in range(2)] for _ in range(2)]
    R_pTh = [[Res(), Res()] for _ in range(2)]
    spTh = [[A([128, 512], BF16) for _ in range(2)] for _ in range(2)]
    R_spTh = [[Res(), Res()] for _ in range(2)]
    eTh = [A([128, 512], F32) for _ in range(2)]
    R_eTh = [Res(), Res()]
    spsh = [[A([128, 512], BF16) for _ in range(2)] for _ in range(2)]
    R_spsh = [[Res(), Res()] for _ in range(2)]
    rdenh = [A([128, 512], F32) for _ in range(2)]
    R_rdenh = [Res(), Res()]
    osth = [[A([128, 512], BF16) for _ in range(2)] for _ in range(2)]
    R_osth = [[Res(), Res()] for _ in range(2)]

    def drive(gens, pattern_head, pattern):
        alive = [True] * len(gens)

        def step(k):
            if alive[k]:
                try:
                    next(gens[k])
                except StopIteration:
                    alive[k] = False
        for k in pattern_head:
            step(k)
        while any(alive):
            for k in pattern:
                step(k)

    def mla_gen(h, h2):
        b4 = h % 4
        p0 = h2 * 64
        Q, K, V = mq[b4], mk[b4], mv[b4]
        ob = 4 + 2 * h2
        db = ob + 1
        sbanks = (0, 1) if h2 == 0 else (2, 3)
        stt = {"n": 0}

        def issue_s(qg, kb):
            r = kb - 4 * qg
            lo = max(r, 0) * 128
            sb_ = sbanks[stt["n"] % 2]
            stt["n"] += 1
            MM(bank(sb_, lo, 512), K[0:96, kb * 128:(kb + 1) * 128], Q[0:96, qg * 512 + lo:(qg + 1) * 512],
               True, True, [R_mk[b4], R_mq[b4]], [PB[sb_]])
            return sb_, lo, r

        blocks = [(qg, kb) for qg in range(NG) for kb in range(4 * (qg + 1))]
        cur = issue_s(*blocks[0])
        nxt = None
        for bi, (qg, kb) in enumerate(blocks):
            nkb = 4 * (qg + 1)
            sb_, lo, r = cur
            if bi + 1 < len(blocks):
                nxt = issue_s(*blocks[bi + 1])
            pt = pTh[h2][bi % 2]
            Rp = R_pTh[h2][bi % 2]
            ACTF(pt[:, lo:512], bank(sb_, lo, 512), AF.Exp, [PB[sb_]], [Rp])
            if r >= 0:
                ASEL(pt[:, lo:lo + 128], pt[:, lo:lo + 128], [[1, 128]], ALU.is_ge, 0, -1, [Rp], [Rp])
            yield
            MM(bank(ob, lo, 512, p0, p0 + 64), V[:, kb, :], pt[:, lo:512], kb == 0, kb == nkb - 1,
               [R_mv[b4], Rp], [PB[ob]])
            MM(bank(db, lo, 512, p0, p0 + 64), ones64, pt[:, lo:512], kb == 0, kb == nkb - 1,
               [R_const, Rp], [PB[db]])
            if kb == nkb - 1:
                rd = rdenh[h2][p0:p0 + 64, :]
                P.op("dve", lambda e, o_=rd, i_=bank(db, 0, 512, p0, p0 + 64): e.reciprocal(out=o_, in_=i_),
                     [PB[db]], [R_rdenh[h2]])
                ot = osth[h2][qg % 2][p0:p0 + 64, :]
                Ro = R_osth[h2][qg % 2]
                TT("dve", ot, bank(ob, 0, 512, p0, p0 + 64), rd, ALU.mult, [PB[ob], R_rdenh[h2]], [Ro])
                DMA(ACTQ, om_s[h // 2, p0:p0 + 64, qg * 512:(qg + 1) * 512], ot, [Ro], [R_om])
            cur = nxt
            yield

    def loads(kind, i):
        if kind == "mla":
            for h2 in range(2):
                h = 2 * i + h2
                b4 = h % 4
                DMA(ACTQ, mq[b4][0:96, :], qt_s[h], [R_qt], [R_mq[b4]])
                DMA(ACTQ, mk[b4][0:96, :], kt_s[h], [R_kt], [R_mk[b4]])
                DMA(ACTQ, mv[b4], vm_s[h], [R_vm], [R_mv[b4]])
        else:
            b2 = i % 2
            DMA(ACTQ, mq[b2], sqt_s[i], [R_sqt], [R_mq[b2]])
            DMA(ACTQ, mk[b2], skt_s[i], [R_skt], [R_mk[b2]])
            DMA(ACTQ, svb[b2][:, 0], vs_s[2 * i], [R_vs], [R_svb[b2]])
            DMA(ACTQ, svb[b2][:, 1], vs_s[2 * i + 1], [R_vs], [R_svb[b2]])

    units = [("mla", i) for i in range(4)] + [("sb", i) for i in range(4)]

    def sb_gen(m, hh):
        b2 = m % 2
        p0 = hh * 64
        Q = mq[b2][p0:p0 + 64, :]
        K = mk[b2][p0:p0 + 64, :]
        V = svb[b2]
        zb = (0, 1) if hh == 0 else (2, 3)
        obs = (4, 5) if hh == 0 else (6, 7)
        stt = {"n": 0, "b": 0}
        pending = [None]

        def issue_z(qg, kb):
            r = kb - 4 * qg
            lo = max(r, 0) * 128
            sb_ = zb[stt["n"] % 2]
            stt["n"] += 1
            MM(bank(sb_, lo, 512), K[:, kb * 128:(kb + 1) * 128], Q[:, qg * 512 + lo:(qg + 1) * 512],
               True, True, [R_mk[b2], R_mq[b2]], [PB[sb_]])
            return sb_, lo, r

        def make_av(ob, lo, kb, pt, Rp, first, last, qg):
            def f():
                MM(bank(ob, lo, 512, p0, p0 + 64), V[:, hh, kb, :], pt[:, lo:512], first, last,
                   [R_svb[b2], Rp], [PB[ob]])
                if last:
                    ot = osth[hh][qg % 2][p0:p0 + 64, :]
                    Ro = R_osth[hh][qg % 2]
                    CP("dve", ot, bank(ob, 0, 512, p0, p0 + 64), [PB[ob]], [Ro])
                    DMA(ACTQ, os_s[m, p0:p0 + 64, qg * 512:(qg + 1) * 512], ot, [Ro], [R_os])
            return f

        for qg in range(NG):
            nkb = 4 * (qg + 1)
            ob = obs[qg % 2]
            sps = spsh[hh][qg % 2]
            Rs = R_spsh[hh][qg % 2]
            MEMSET("pool", sps, 0.0, [Rs])
            order = list(range(nkb - 1, -1, -1))
            cur = issue_z(qg, order[0])
            nxt = None
            for ii, kb in enumerate(order):
                sb_, lo, r = cur
                bsel = stt["b"] % 2
                stt["b"] += 1
                spt = spTh[hh][bsel]
                Rsp = R_spTh[hh][bsel]
                pt = pTh[hh][bsel]
                Rp = R_pTh[hh][bsel]
                if pending[0] is not None:
                    pending[0]()
                    pending[0] = None
                ACTF(eTh[hh][:, lo:512], bank(sb_, lo, 512), AF.Exp, [PB[sb_]], [R_eTh[hh]])
                ACTF(spt[:, lo:512], eTh[hh][:, lo:512], AF.Ln, [R_eTh[hh]], [Rsp], bias=1.0)
                if r >= 0:
                    ASEL(spt[:, lo:lo + 128], spt[:, lo:lo + 128], [[1, 128]], ALU.is_gt, 0, -1, [Rsp], [Rsp])
                if ii + 1 < len(order):
                    nxt = issue_z(qg, order[ii + 1])
                yield
                MM(bank(sb_, lo, 512), trineg, spt[:, lo:512], False, ii == 0, [R_const, Rsp], [PB[sb_]])
                if ii > 0:
                    MM(bank(sb_, lo, 512), negones, sps[:, lo:512], False, True, [R_const, Rs], [PB[sb_]])
                ACTF(pt[:, lo:512], bank(sb_, lo, 512), AF.Exp, [PB[sb_]], [Rp])
                if r >= 0:
                    ASEL(pt[:, lo:lo + 128], pt[:, lo:lo + 128], [[1, 128]], ALU.is_gt, 0, -1, [Rp], [Rp])
                if ii + 1 < len(order):
                    TT("dve", sps[:, lo:512], sps[:, lo:512], spt[:, lo:512], ALU.add, [Rs, Rsp], [Rs])
                pending[0] = make_av(ob, lo, kb, pt, Rp, ii == 0, ii == len(order) - 1, qg)
                cur = nxt
                yield
        if pending[0] is not None:
            pending[0]()
            pending[0] = None
        yield

    loads(*units[0])
    for ui_, (kind, i) in enumerate(units):
        if ui_ + 1 < len(units):
            loads(*units[ui_ + 1])
        if kind == "mla":
            drive([mla_gen(2 * i, 0), mla_gen(2 * i + 1, 1)], [], [0, 1])
        else:
            drive([sb_gen(i, 0), sb_gen(i, 1)], [0], [1, 0, 0, 1])
    P.barrier()

    st["off"] = P1
    wbm = A([128, 4, 1024], BF16)
    wbs = A([128, 4, 1024], BF16)
    wo = A([128, 8, 1024], BF16)
    wpg = A([128, 8, 1024], BF16)
    wpp = A([128, 2, 1024], BF16)
    R_w3 = Res("p3w")
    for dst, n in ((wbm, "w_branch_mla"), (wbs, "w_branch_sb"), (wo, "w_out"), (wpg, "w_ple_gate"), (wpp, "w_ple_proj")):
        DMA(WQ, dst, ws[n].rearrange("(c p) n -> p c n", p=128), [R_ws[n]], [R_w3])
    omg = A([128, 4, G], BF16)
    osg = A([128, 4, G], BF16)
    R_omg = Res()
    R_osg = Res()
    sgm = [A([128, G], F32)] * 2
    sgs = [A([128, G], F32)] * 2
    R_sgm = [Res()] * 2
    R_sgs = [Res()] * 2
    tm1 = [A([128, G], F32)] * 2
    tm2 = [A([128, G], F32)] * 2
    R_tm1 = [Res()] * 2
    R_tm2 = [Res()] * 2
    mT = A([128, 8, G], BF16)
    R_mT = Res()
    pg = A([128, NTG, 256], F32)
    R_pg = Res()
    pgb = A([128, 256], BF16)
    R_pgb = Res()
    ppT = A([128, 2, 128], BF16)
    R_ppT = Res()
    sgp = A([128, 1024], F32)
    R_sgp = Res()
    final = []
    for g in range(NG):
        rows = slice(g * G, (g + 1) * G)
        DMA(ACTQ, hg, h1_s[rows, :].rearrange("(t p) d -> p t d", p=128), [R_h1], [R_hg])
        DMA(ACTQ, omg, om_s[:, :, rows].rearrange("m p t -> p m t"), [R_om], [R_omg])
        DMA(ACTQ, osg, os_s[:, :, rows].rearrange("m p t -> p m t"), [R_os], [R_osg])
        DMA(ACTQ, pg, p_d[rows, :].rearrange("(t p) d -> p t d", p=128), [], [R_pg])
        norm_to_uT(gbc["mix_norm"], 0)
        U = uT[0]
        RU = R_uT[0]
        for mb in range(2):
            sGm, RGm = ring_load(Win_v[:, :, C_GATE + mb * 512:C_GATE + (mb + 1) * 512], [128, 8, 512], R_ws["w_in"])
            sGs, RGs = ring_load(Win_v[:, :, C_GATE + 1024 + mb * 512:C_GATE + 1024 + (mb + 1) * 512], [128, 8, 512],
                                 R_ws["w_in"])
            for mm_ in range(4):
                m = mb * 4 + mm_
                i2 = m % 2
                for c in range(8):
                    MM(bank(0), sGm[:, c, mm_ * 128:(mm_ + 1) * 128], U[:, c, :], c == 0, c == 7, [RGm, RU], [PB[0]])
                for c in range(8):
                    MM(bank(1), sGs[:, c, mm_ * 128:(mm_ + 1) * 128], U[:, c, :], c == 0, c == 7, [RGs, RU], [PB[1]])
                for k in range(4):
                    MM(bank(2), wbm[:, k, m * 128:(m + 1) * 128], omg[:, k, :], k == 0, k == 3, [R_w3, R_omg], [PB[2]])
                for k in range(4):
                    MM(bank(3), wbs[:, k, m * 128:(m + 1) * 128], osg[:, k, :], k == 0, k == 3, [R_w3, R_osg], [PB[3]])
                ACTF(sgm[i2], bank(0), AF.Sigmoid, [PB[0]], [R_sgm[i2]])
                ACTF(sgs[i2], bank(1), AF.Sigmoid, [PB[1]], [R_sgs[i2]])
                TT("dve", tm1[i2], bank(2), sgm[i2], ALU.mult, [PB[2], R_sgm[i2]], [R_tm1[i2]])
                TT("dve", tm2[i2], bank(3), sgs[i2], ALU.mult, [PB[3], R_sgs[i2]], [R_tm2[i2]])
                TT("dve", mT[:, m, :], tm1[i2], tm2[i2], ALU.add, [R_tm1[i2], R_tm2[i2]], [R_mT])
        for t in range(NTG):
            tc_ = slice(t * 128, (t + 1) * 128)
            for hf in range(2):
                bk = 4 + (t % 2) * 2 + hf
                for c in range(8):
                    MM(bank(bk), mT[:, c, tc_], wo[:, c, hf * 512:(hf + 1) * 512], c == 0, c == 7, [R_mT, R_w3], [PB[bk]])
                dst = hg[:, t, hf * 512:(hf + 1) * 512]
                TT("dve", dst, bank(bk), dst, ALU.add, [PB[bk], R_hg], [R_hg])
        norm_to_uT(gbc["ffn2_norm"], 1)
        ffn("ffn2_w_in", "ffn2_w_out", 1)
        norm_to_uT(gbc["ple_norm"], 0)
        for t in range(NTG):
            tc_ = slice(t * 128, (t + 1) * 128)
            CP("dve", pgb, pg[:, t, :], [R_pg], [R_pgb])
            bp = bankbf(2)[:, 0:256].rearrange("p (c k) -> p c k", c=2)
            for c in range(2):
                TR(bp[:, c, :], pgb[:, c * 128:(c + 1) * 128], ident, [R_pgb, R_const], [PB[2]])
            CP("act", ppT, bp, [PB[2]], [R_ppT])
            for hf in range(2):
                bk = 4 + hf
                for c in range(8):
                    MM(bank(bk), U[:, c, tc_], wpg[:, c, hf * 512:(hf + 1) * 512], c == 0, c == 7, [RU, R_w3], [PB[bk]])
                ACTF(sgp[:, hf * 512:(hf + 1) * 512], bank(bk), AF.Sigmoid, [PB[bk]], [R_sgp])
                bk2 = 6 + hf
                for c in range(2):
                    MM(bank(bk2), ppT[:, c, :], wpp[:, c, hf * 512:(hf + 1) * 512], c == 0, c == 1, [R_ppT, R_w3], [PB[bk2]])
                TT("dve", sgp[:, hf * 512:(hf + 1) * 512], bank(bk2), sgp[:, hf * 512:(hf + 1) * 512], ALU.mult,
                   [PB[bk2], R_sgp], [R_sgp])
                dst = hg[:, t, hf * 512:(hf + 1) * 512]
                TT("dve", dst, dst, sgp[:, hf * 512:(hf + 1) * 512], ALU.add, [R_sgp, R_hg], [R_hg])
        final.append(DMA(ACTQ, out_d[rows, :].rearrange("(t p) d -> p t d", p=128), hg, [R_hg], [R_out]))

    print("arena marks", PERSIST, P1, st["off"], {e: len(P.ins[e]) for e in P.ENG})
    P.emit(nc, final_waits=final)
    es.close()
    return nc


_NC = None


def kernel(**inputs):
    global _NC
    if _NC is None:
        _NC = build_nc()
    nc = _NC
    x = np.asarray(inputs["x"], dtype=np.float32)
    p = np.asarray(inputs["p"], dtype=np.float32)
    pos = np.asarray(inputs["positions"], dtype=np.int32)
    common = {}
    for n, r, c in WEIGHTS:
        common[n] = np.ascontiguousarray(np.asarray(inputs[n], dtype=np.float32).reshape(r, c))
    for n, c in GAINS:
        common[n] = np.ascontiguousarray(np.asarray(inputs[n], dtype=np.float32).reshape(1, c))
    in_maps = []
    for b in range(8):
        m = dict(common)
        m["x"] = np.ascontiguousarray(x[b])
        m["p"] = np.ascontiguousarray(p[0, b])
        m["posT"] = np.ascontiguousarray(pos[b].reshape(32, 128).T)
        in_maps.append(m)
    res = run_bass_kernel_spmd(nc, in_maps, core_ids=list(range(8)))
    out = np.stack([np.asarray(r["out"], dtype=np.float32) for r in res.results], axis=0)
    return out
```

```python
import math
from contextlib import ExitStack

import numpy as np
import concourse.bass as bass
import concourse.mybir as mybir
from concourse.bass_utils import run_bass_kernel_spmd

F32 = mybir.dt.float32
BF16 = mybir.dt.bfloat16
I32 = mybir.dt.int32
AF = mybir.ActivationFunctionType
ALU = mybir.AluOpType
AX = mybir.AxisListType

T = 4096
D = 1024
DFF = 2816
G = 512
NG = T // G
NTG = G // 128
NJ = DFF // 128
EPS = 1e-6
DEBUG = False


class Res:
    __slots__ = ("name", "w", "r", "excl")

    def __init__(self, name="", excl=False):
        self.name = name
        self.w = None
        self.r = []
        self.excl = excl


class Prog:
    ENG = ["pe", "act", "dve", "pool", "sp"]
    EPOCH = 4096
    NDS = 16

    def __init__(self):
        self.ins = {e: [] for e in self.ENG}
        self.ndma = {e: 0 for e in self.ENG}
        self.dmas = {e: [] for e in self.ENG}

    def op(self, eng, fn, reads=(), writes=(), dma=False, extra=()):
        idx = len(self.ins[eng])
        me = (eng, idx)
        deps = set(extra)
        if any(r.excl for r in reads):
            writes = list(writes) + [r for r in reads if r.excl and r not in writes]
            reads = [r for r in reads if not r.excl]
        for r in reads:
            if r.w is not None:
                deps.add(r.w)
        for w in writes:
            if w.w is not None:
                deps.add(w.w)
            for rd in w.r:
                if rd[0] == eng and not dma and not self.ins[eng][rd[1]]["dma"]:
                    continue
                deps.add(rd)
        fd = set()
        for d in deps:
            if d == me:
                continue
            if d[0] == eng and eng == "pe":
                continue
            fd.add(d)
        rec = dict(fn=fn, deps=fd, dma=dma, signal=False, dj=None)
        if dma:
            rec["dj"] = self.ndma[eng]
            self.ndma[eng] += 1
            self.dmas[eng].append(me)
        self.ins[eng].append(rec)
        for d in fd:
            self.ins[d[0]][d[1]]["signal"] = True
        for r in reads:
            r.r.append(me)
        for w in writes:
            w.w = me
            w.r = []
        return me

    def barrier(self):
        tails = []
        for e in self.ENG:
            last = None
            for i in range(len(self.ins[e]) - 1, -1, -1):
                if not self.ins[e][i]["dma"]:
                    last = (e, i)
                    break
            if last is not None:
                tails.append(last)
            tails.extend(self.dmas[e][-self.NDS:])
        for e in self.ENG:
            self.op(e, lambda en: en.nop(), extra=[t for t in tails])

    def emit(self, nc, final_waits=()):
        with ExitStack() as es:
            esem = {}
            for e in self.ENG:
                k = 0
                for r in self.ins[e]:
                    if r["signal"] and not r["dma"]:
                        r["k"] = k
                        k += 1
                nep = (k + self.EPOCH - 1) // self.EPOCH
                esem[e] = [es.enter_context(nc.semaphore(f"s_{e}_{i}")) for i in range(nep)]
            dsem = {}
            for e in self.ENG:
                if self.ndma[e]:
                    dsem[e] = [es.enter_context(nc.semaphore(f"d_{e}_{i}")) for i in range(self.NDS)]
            block = es.enter_context(nc.Block())
            prog = self

            def run(e, engobj):
                seen = {}

                def wait_compute(e2, k):
                    if seen.get(e2, -1) >= k:
                        return
                    seen[e2] = k
                    engobj.wait_ge(esem[e2][k // prog.EPOCH], k % prog.EPOCH + 1)

                def wait_dma(q, j):
                    key = ("d", q, j % prog.NDS)
                    if seen.get(key, -1) >= j:
                        return
                    seen[key] = j
                    engobj.wait_ge(dsem[q][j % prog.NDS], 16 * (j // prog.NDS + 1))

                for r in prog.ins[e]:
                    for d in sorted(r["deps"]):
                        p = prog.ins[d[0]][d[1]]
                        if p["dma"]:
                            wait_dma(d[0], p["dj"])
                        else:
                            wait_compute(d[0], p["k"])
                    if r["dma"]:
                        j = r["dj"]
                        if j >= prog.NDS:
                            wait_dma(e, j - prog.NDS)
                        bi = r["fn"](engobj)
                        bi.then_inc(dsem[e][j % prog.NDS], 16)
                    else:
                        bi = r["fn"](engobj)
                        if r["signal"]:
                            k = r["k"]
                            bi.then_inc(esem[e][k // prog.EPOCH], 1)
                if e == "sp":
                    for d in final_waits:
                        p = prog.ins[d[0]][d[1]]
                        wait_dma(d[0], p["dj"])

            @block.tensor
            def _(eng):
                run("pe", eng)

            @block.scalar
            def _(eng):
                run("act", eng)

            @block.vector
            def _(eng):
                run("dve", eng)

            @block.gpsimd
            def _(eng):
                run("pool", eng)

            @block.sync
            def _(eng):
                run("sp", eng)


WEIGHTS = [
    ("ffn1_w_in", 1024, 5632), ("ffn1_w_out", 2816, 1024), ("w_in", 1024, 4256),
    ("w_q_up", 384, 768), ("w_kv_up", 256, 1024), ("w_branch_mla", 512, 1024),
    ("w_branch_sb", 512, 1024), ("w_out", 1024, 1024), ("ffn2_w_in", 1024, 5632),
    ("ffn2_w_out", 2816, 1024), ("w_ple_gate", 1024, 1024), ("w_ple_proj", 256, 1024),
]
GAINS = [("ffn1_norm", 1024), ("mix_norm", 1024), ("q_latent_norm", 384), ("kv_latent_norm", 256),
         ("q_head_norm", 96), ("k_head_norm", 96), ("ffn2_norm", 1024), ("ple_norm", 1024)]

C_SBQ = 672
C_SBK = C_SBQ + 512
C_SBV = C_SBK + 512
C_GATE = C_SBV + 512


def build_nc():
    nc = bass.Bass("TRN2", target_bir_lowering=False)
    P = Prog()
    ACTQ = "pool"
    WQ = "sp"

    def dram(name, shape, dt, kind):
        return nc.dram_tensor(name, shape, dt, kind=kind).ap()

    x_d = dram("x", [T, D], F32, "ExternalInput")
    p_d = dram("p", [T, 256], F32, "ExternalInput")
    pos_d = dram("posT", [128, 32], I32, "ExternalInput")
    w_d = {n: dram(n, [r, c], F32, "ExternalInput") for n, r, c in WEIGHTS}
    g_d = {n: dram(n, [1, c], F32, "ExternalInput") for n, c in GAINS}
    out_d = dram("out", [T, D], F32, "ExternalOutput")
    sk = "ExternalOutput" if DEBUG else "Internal"
    ws = {n: dram("s_" + n, [r, c], BF16, "Internal") for n, r, c in WEIGHTS}
    h1_s = dram("s_h1", [T, D], F32, sk)
    qt_s = dram("s_qt", [8, 96, T], BF16, sk)
    kt_s = dram("s_kt", [8, 96, T], BF16, sk)
    vm_s = dram("s_vm", [8, 128, 32, 64], BF16, sk)
    sqt_s = dram("s_sqt", [4, 128, T], BF16, sk)
    skt_s = dram("s_skt", [4, 128, T], BF16, sk)
    vs_s = dram("s_vs", [8, 128, 32, 64], BF16, sk)
    om_s = dram("s_om", [4, 128, T], BF16, sk)
    os_s = dram("s_os", [4, 128, T], BF16, sk)
    R_ws = {n: Res(n) for n, _, _ in WEIGHTS}
    R_h1, R_qt, R_kt, R_vm, R_sqt, R_skt, R_vs, R_om, R_os, R_out = [Res() for _ in range(10)]

    es = ExitStack()
    ARENA_N = 103 * 1024
    arena = es.enter_context(nc.sbuf_tensor("arena", [128, ARENA_N], BF16))
    ps = es.enter_context(nc.psum_tensor("ps", [128, 4096], F32))
    PB = [Res(f"bank{b}", excl=True) for b in range(8)]

    def bank(b, lo=0, hi=512, p0=0, p1=128):
        return ps[p0:p1, b * 512 + lo:b * 512 + hi]

    def bankbf(b):
        return ps[:, b * 512:(b + 1) * 512].bitcast(BF16)

    st = {"off": 0}

    def A(shape, dt):
        n = 1
        for s_ in shape[1:]:
            n *= s_
        nb = n * (2 if dt == BF16 else 4)
        ne = ((nb + 31) // 32) * 16
        off = st["off"]
        assert off + ne <= ARENA_N, ("arena overflow", off, ne)
        st["off"] = off + ne
        sl = arena[:, off:off + nb // 2]
        if dt != BF16:
            sl = sl.bitcast(dt)
        if len(shape) > 2:
            names = " ".join(f"d{i}" for i in range(len(shape) - 1))
            kw = {f"d{i}": shape[i + 1] for i in range(len(shape) - 1)}
            sl = sl.rearrange(f"p ({names}) -> p {names}", **kw)
        if shape[0] < 128:
            sl = sl[0:shape[0]]
        return sl

    def MM(out, lhsT, rhs, start, stop, r, w):
        P.op("pe", lambda e: e.matmul(out, lhsT=lhsT, rhs=rhs, start=start, stop=stop), r, w)

    def TR(out, in_, ident, r, w):
        P.op("pe", lambda e: e.transpose(out=out, in_=in_, identity=ident), r, w)

    def ACTF(out, in_, func, r, w, bias=None, scale=None, accum=None):
        kw = {}
        if bias is not None:
            kw["bias"] = bias
        if scale is not None:
            kw["scale"] = scale
        if accum is not None:
            kw["accum_out"] = accum
        P.op("act", lambda e: e.activation(out=out, in_=in_, func=func, **kw), r, w)

    def TT(eng, out, in0, in1, op, r, w):
        P.op(eng, lambda e: e.tensor_tensor(out=out, in0=in0, in1=in1, op=op), r, w)

    def TS(eng, out, in0, s1, op0, r, w, s2=None, op1=None):
        if op1 is None:
            P.op(eng, lambda e: e.tensor_scalar(out=out, in0=in0, scalar1=s1, scalar2=None, op0=op0), r, w)
        else:
            P.op(eng, lambda e: e.tensor_scalar(out=out, in0=in0, scalar1=s1, scalar2=s2, op0=op0, op1=op1), r, w)

    def STT(out, in0, scalar, in1, op0, op1, r, w):
        P.op("dve", lambda e: e.scalar_tensor_tensor(out=out, in0=in0, scalar=scalar, in1=in1, op0=op0, op1=op1), r, w)

    def CP(eng, out, in_, r, w, scale=None):
        if eng == "act":
            if scale is None:
                P.op("act", lambda e: e.activation(out=out, in_=in_, func=AF.Copy), r, w)
            else:
                P.op("act", lambda e: e.activation(out=out, in_=in_, func=AF.Copy, scale=scale), r, w)
        else:
            if scale is None:
                P.op(eng, lambda e: e.tensor_copy(out=out, in_=in_), r, w)
            else:
                P.op(eng, lambda e: e.tensor_scalar(out=out, in0=in_, scalar1=scale, scalar2=None, op0=ALU.mult), r, w)

    def DMA(q, out, in_, r, w):
        return P.op(q, lambda e: e.dma_start(out=out, in_=in_), r, w, dma=True)

    def MEMSET(eng, ap, val, w):
        P.op(eng, lambda e: e.memset(ap, val), (), w)

    def ASEL(out, in_, pattern, cmp, base, cm, r, w):
        P.op("pool", lambda e: e.affine_select(out=out, in_=in_, pattern=pattern, compare_op=cmp, fill=0.0,
                                                base=base, channel_multiplier=cm), r, w)

    ident = A([128, 128], BF16)
    onesb = A([128, 128], BF16)
    negones = A([128, 128], BF16)
    trineg = A([128, 128], BF16)
    epst = A([128, 1], F32)
    cosT = A([128, 32, 16], F32)
    sinT = A([128, 32, 16], F32)
    R_const = Res("const")
    R_cs = Res("cossin")
    MEMSET("pool", onesb, 1.0, [R_const])
    MEMSET("pool", negones, -1.0, [R_const])
    MEMSET("pool", epst, EPS, [R_const])
    ASEL(ident, onesb, [[-1, 128]], ALU.is_equal, 0, 1, [R_const], [R_const])
    ASEL(trineg, negones, [[-1, 128]], ALU.is_ge, 0, 1, [R_const], [R_const])
    gbc = {n: A([128, c], F32) for n, c in GAINS if c == 1024}
    R_g = Res("gains")
    for n in gbc:
        DMA(ACTQ, gbc[n], g_d[n].broadcast_to([128, 1024]), [], [R_g])
    PERSIST = st["off"]

    posi = A([128, 32], I32)
    posf = A([128, 32], F32)
    invf = A([128, 16], F32)
    ang = A([128, 32, 16], F32)
    kq = A([128, 32, 16], F32)
    ki = A([128, 32, 16], I32)
    r1 = A([128, 32, 16], F32)
    r2 = A([128, 32, 16], F32)
    Rt = Res("ropetmp")
    DMA(ACTQ, posi, pos_d, [], [Rt])
    CP("dve", posf, posi, [Rt], [Rt])
    inv_np = (np.float32(10000.0) ** (-np.arange(0, 32, 2, dtype=np.float32) / np.float32(32))).astype(np.float32)
    for j in range(16):
        MEMSET("dve", invf[:, j:j + 1], float(inv_np[j]), [Rt])
    TT("dve", ang, posf.unsqueeze(2).broadcast_to([128, 32, 16]), invf.unsqueeze(1).broadcast_to([128, 32, 16]),
       ALU.mult, [Rt], [Rt])
    TWO_PI = 2.0 * math.pi
    C1 = 6.28125
    C2 = TWO_PI - C1
    for (dst, shift) in ((sinT, 0.0), (cosT, math.pi / 2)):
        TS("dve", r1, ang, shift, ALU.add, [Rt], [Rt])
        TS("dve", kq, r1, 1.0 / TWO_PI, ALU.mult, [Rt], [Rt])
        CP("dve", ki, kq, [Rt], [Rt])
        CP("dve", kq, ki, [Rt], [Rt])
        STT(r1, kq, -C1, r1, ALU.mult, ALU.add, [Rt], [Rt])
        STT(r1, kq, -C2, r1, ALU.mult, ALU.add, [Rt], [Rt])
        TS("dve", r2, r1, math.pi, ALU.is_gt, [Rt], [Rt], s2=-TWO_PI, op1=ALU.mult)
        TT("dve", r1, r1, r2, ALU.add, [Rt], [Rt])
        TS("dve", r2, r1, -math.pi, ALU.is_lt, [Rt], [Rt], s2=TWO_PI, op1=ALU.mult)
        TT("dve", r1, r1, r2, ALU.add, [Rt], [Rt])
        TS("dve", r1, r1, math.pi, ALU.min, [Rt], [Rt], s2=-math.pi, op1=ALU.max)
        ACTF(dst, r1, AF.Sin, [Rt], [R_cs])

    stg_f = [A([128, 5632], F32) for _ in range(2)]
    stg_b = [A([128, 5632], BF16) for _ in range(2)]
    R_sf = [Res(), Res()]
    R_sb = [Res(), Res()]
    ui = 0
    for n, r, c in WEIGHTS:
        nr = r // 128
        per = max(1, 5632 // c)
        c0 = 0
        while c0 < nr:
            k = min(per, nr - c0)
            b = ui % 2
            src = w_d[n][c0 * 128:(c0 + k) * 128, :].rearrange("(i p) n -> p i n", p=128)
            dst = ws[n][c0 * 128:(c0 + k) * 128, :].rearrange("(i p) n -> p i n", p=128)
            sf = stg_f[b][:, 0:k * c].rearrange("p (i n) -> p i n", i=k)
            sb_ = stg_b[b][:, 0:k * c].rearrange("p (i n) -> p i n", i=k)
            DMA(WQ, sf, src, [], [R_sf[b]])
            CP("dve" if ui % 2 == 0 else "act", sb_, sf, [R_sf[b]], [R_sb[b]])
            DMA(ACTQ, dst, sb_, [R_sb[b]], [R_ws[n]])
            ui += 1
            c0 += k
    P.barrier()
    st["off"] = PERSIST

    RING_N = 4
    ring = [A([128, 4096], BF16) for _ in range(RING_N)]
    R_ring = [Res(f"ring{i}") for i in range(RING_N)]
    rst = {"i": 0}

    def ring_load(src_ap, shape, Rsrc):
        i = rst["i"] % RING_N
        rst["i"] += 1
        n = 1
        for s_ in shape[1:]:
            n *= s_
        dst = ring[i][:, 0:n]
        if len(shape) == 3:
            dst = dst.rearrange("p (a b) -> p a b", a=shape[1])
        DMA(WQ, dst, src_ap, [Rsrc], [R_ring[i]])
        return dst, R_ring[i]

    hg = A([128, NTG, 1024], F32)
    R_hg = Res("hg")
    _uT0 = A([128, 8, G], BF16)
    uT = [_uT0, _uT0]
    _RuT = Res("uT")
    R_uT = [_RuT, _RuT]
    u_tm = [A([128, 1024], BF16) for _ in range(2)]
    R_utm = [Res(), Res()]
    junk = A([128, 1024], BF16)
    R_junk = Res()
    gT = A([128, NJ, G], BF16)
    R_gT = Res("gT")
    sa = [A([128, G], BF16) for _ in range(2)]
    R_sa = [Res(), Res()]
    NSTAT = 12
    stat = A([128, NSTAT * 3 * 8], F32)
    sst = {"i": 0}

    def stat3():
        i = sst["i"] % NSTAT
        sst["i"] += 1
        base = i * 24
        return [stat[:, base + 8 * k: base + 8 * k + 8] for k in range(3)], Res()

    def rstd_from_ssq(ssq, lnv, rstd, n, ncol, Rs):
        ACTF(lnv[:, 0:ncol], ssq[:, 0:ncol], AF.Ln, [Rs, R_const], [Rs], bias=epst[:, 0:1], scale=1.0 / n)
        ACTF(rstd[:, 0:ncol], lnv[:, 0:ncol], AF.Exp, [Rs], [Rs], scale=-0.5)

    def norm_to_uT(gain, ui_):
        (ssq, lnv, rstd), Rs = stat3()
        for t in range(NTG):
            ACTF(junk, hg[:, t, :], AF.Square, [R_hg], [R_junk, Rs], accum=ssq[:, t:t + 1])
        rstd_from_ssq(ssq, lnv, rstd, 1024.0, NTG, Rs)
        for t in range(NTG):
            b = t % 2
            STT(u_tm[b], hg[:, t, :], rstd[:, t:t + 1], gain, ALU.mult, ALU.mult, [R_hg, Rs, R_g], [R_utm[b]])
            for c in range(8):
                TR(bankbf(b)[:, c * 128:(c + 1) * 128], u_tm[b][:, c * 128:(c + 1) * 128], ident,
                   [R_utm[b], R_const], [PB[b]])
            CP("act" if t % 2 == 0 else "dve", uT[ui_][:, :, t * 128:(t + 1) * 128],
               bankbf(b).rearrange("p (c k) -> p c k", c=8), [PB[b]], [R_uT[ui_]])

    JB = [(j0, min(j0 + 4, NJ)) for j0 in range(0, NJ, 4)]

    def ffn(w1n, w2n, ui_):
        W1 = ws[w1n].rearrange("(c p) n -> p c n", p=128)
        W2 = ws[w2n].rearrange("(j p) n -> p j n", p=128)
        for (j0, j1) in JB:
            nj = j1 - j0
            sA, RA = ring_load(W1[:, :, j0 * 128:j1 * 128], [128, 8, nj * 128], R_ws[w1n])
            sB, RB = ring_load(W1[:, :, DFF + j0 * 128:DFF + j1 * 128], [128, 8, nj * 128], R_ws[w1n])
            for jj in range(nj):
                j = j0 + jj
                ab = 2 * (j % 2)
                bb = ab + 1
                for c in range(8):
                    MM(bank(ab), sA[:, c, jj * 128:(jj + 1) * 128], uT[ui_][:, c, :], c == 0, c == 7,
                       [RA, R_uT[ui_]], [PB[ab]])
                for c in range(8):
                    MM(bank(bb), sB[:, c, jj * 128:(jj + 1) * 128], uT[ui_][:, c, :], c == 0, c == 7,
                       [RB, R_uT[ui_]], [PB[bb]])
                ACTF(sa[j % 2], bank(ab), AF.Silu, [PB[ab]], [R_sa[j % 2]])
                TT("dve", gT[:, j, :], bank(bb), sa[j % 2], ALU.mult, [PB[bb], R_sa[j % 2]], [R_gT])
        for tp in range(2):
            for (j0, j1) in JB:
                nj = j1 - j0
                sW, RW = ring_load(W2[:, j0:j1, :], [128, nj, 1024], R_ws[w2n])
                for jj in range(nj):
                    j = j0 + jj
                    for t in (2 * tp, 2 * tp + 1):
                        for hf in range(2):
                            bk = 4 + (t % 2) * 2 + hf
                            MM(bank(bk), gT[:, j, t * 128:(t + 1) * 128], sW[:, jj, hf * 512:(hf + 1) * 512],
                               j == 0, j == NJ - 1, [RW, R_gT], [PB[bk]])
            for t in (2 * tp, 2 * tp + 1):
                for hf in range(2):
                    bk = 4 + (t % 2) * 2 + hf
                    dst = hg[:, t, hf * 512:(hf + 1) * 512]
                    STT(dst, bank(bk), 0.5, dst, ALU.mult, ALU.add, [PB[bk], R_hg], [R_hg])

    P1 = st["off"]
    wtok = A([128, 8, 672], BF16)
    wv = A([128, 8, 512], BF16)
    wq = A([128, 3, 768], BF16)
    wkv = A([128, 2, 1024], BF16)
    R_w1 = Res("p1w")
    Win_v = ws["w_in"].rearrange("(c p) n -> p c n", p=128)
    DMA(WQ, wtok, Win_v[:, :, 0:672], [R_ws["w_in"]], [R_w1])
    DMA(WQ, wv, Win_v[:, :, C_SBV:C_SBV + 512], [R_ws["w_in"]], [R_w1])
    DMA(WQ, wq, ws["w_q_up"].rearrange("(c p) n -> p c n", p=128), [R_ws["w_q_up"]], [R_w1])
    DMA(WQ, wkv, ws["w_kv_up"].rearrange("(c p) n -> p c n", p=128), [R_ws["w_kv_up"]], [R_w1])
    gql = A([128, 384], F32)
    gkvl = A([128, 256], F32)
    gqh = A([128, 8, 96], F32)
    gkh = A([128, 8, 96], F32)
    DMA(ACTQ, gql, g_d["q_latent_norm"].broadcast_to([128, 384]), [], [R_g])
    DMA(ACTQ, gkvl, g_d["kv_latent_norm"].broadcast_to([128, 256]), [], [R_g])
    DMA(ACTQ, gqh, bass.AP(g_d["q_head_norm"].tensor, 0, [[0, 128], [0, 8], [1, 96]]), [], [R_g])
    DMA(ACTQ, gkh, bass.AP(g_d["k_head_norm"].tensor, 0, [[0, 128], [0, 8], [1, 96]]), [], [R_g])
    TS("dve", gqh, gqh, 1.0 / math.sqrt(96.0), ALU.mult, [R_g], [R_g])
    cql = A([128, 672], F32)
    R_cql = Res()
    lat_bf = A([128, 640], BF16)
    R_lat = Res()
    latT = A([128, 5, 128], BF16)
    R_latT = Res()
    qf = A([128, 8, 96], F32)
    R_qf = Res()
    kf = A([128, 8, 96], F32)
    R_kf = Res()
    sq = A([128, 8, 96], F32)
    R_sq = Res()
    rt = [A([128, 8, 16], F32) for _ in range(4)]
    R_rt = Res()
    qb = A([128, 8, 96], BF16)
    R_qb = Res()
    qTg = A([96, 8, G], BF16)
    R_qTg = Res()
    kTg = A([96, 8, G], BF16)
    R_kTg = Res()
    vmg = A([128, NTG, 8, 64], BF16)
    R_vmg = Res()
    vsg = A([128, NTG, 8, 64], BF16)
    R_vsg = Res()
    sqTg = A([128, 4, G], BF16)
    R_sqTg = Res()
    skTg = A([128, 4, G], BF16)
    R_skTg = Res()

    def head_norm_rope(src, Rsrc, gains, tile_idx, dstT, RdstT, tcol):
        (ssq, lnv, rstd), Rs = stat3()
        TT("dve", sq, src, src, ALU.mult, [Rsrc], [R_sq])
        P.op("dve", lambda e: e.tensor_reduce(out=ssq[:, 0:8], in_=sq, axis=AX.X, op=ALU.add), [R_sq], [Rs])
        rstd_from_ssq(ssq, lnv, rstd, 96.0, 8, Rs)
        TT("dve", src, src, gains, ALU.mult, [Rsrc, R_g], [Rsrc])
        TT("dve", src, src, rstd[:, 0:8].unsqueeze(2).broadcast_to([128, 8, 96]), ALU.mult, [Rsrc, Rs], [Rsrc])
        cs = cosT[:, tile_idx, :].unsqueeze(1).broadcast_to([128, 8, 16])
        sn = sinT[:, tile_idx, :].unsqueeze(1).broadcast_to([128, 8, 16])
        x1 = src[:, :, 64:80]
        x2 = src[:, :, 80:96]
        TT("dve", rt[0], x1, cs, ALU.mult, [Rsrc, R_cs], [R_rt])
        TT("dve", rt[1], x2, sn, ALU.mult, [Rsrc, R_cs], [R_rt])
        TT("dve", rt[2], x2, cs, ALU.mult, [Rsrc, R_cs], [R_rt])
        TT("dve", rt[3], x1, sn, ALU.mult, [Rsrc, R_cs], [R_rt])
        CP("dve", qb[:, :, 0:64], src[:, :, 0:64], [Rsrc], [R_qb])
        TT("dve", qb[:, :, 64:80], rt[0], rt[1], ALU.subtract, [R_rt], [R_qb])
        TT("dve", qb[:, :, 80:96], rt[2], rt[3], ALU.add, [R_rt], [R_qb])
        bq = bankbf(2)[0:96].rearrange("p (h k) -> p h k", h=8)
        for h in range(8):
            TR(bq[:, h, :], qb[:, h, :], ident, [R_qb, R_const], [PB[2]])
        CP("act", dstT[:, :, tcol * 128:(tcol + 1) * 128], bq, [PB[2]], [RdstT])

    for g in range(NG):
        rows = slice(g * G, (g + 1) * G)
        DMA(ACTQ, hg, x_d[rows, :].rearrange("(t p) d -> p t d", p=128), [], [R_hg])
        norm_to_uT(gbc["ffn1_norm"], 0)
        ffn("ffn1_w_in", "ffn1_w_out", 0)
        DMA(ACTQ, h1_s[rows, :].rearrange("(t p) d -> p t d", p=128), hg, [R_hg], [R_h1])
        norm_to_uT(gbc["mix_norm"], 1)
        U = uT[1]
        RU = R_uT[1]
        for (c0, dstT, RdT, scl) in ((C_SBQ, sqTg, R_sqTg, 0.125), (C_SBK, skTg, R_skTg, None)):
            sW, RW = ring_load(Win_v[:, :, c0:c0 + 512], [128, 8, 512], R_ws["w_in"])
            for m in range(4):
                bk = m % 2
                for c in range(8):
                    MM(bank(bk), sW[:, c, m * 128:(m + 1) * 128], U[:, c, :], c == 0, c == 7, [RW, RU], [PB[bk]])
                CP("act" if m % 2 == 0 else "dve", dstT[:, m, :], bank(bk), [PB[bk]], [RdT], scale=scl)
        DMA(ACTQ, sqt_s[:, :, rows].rearrange("m p t -> p m t"), sqTg, [R_sqTg], [R_sqt])
        DMA(ACTQ, skt_s[:, :, rows].rearrange("m p t -> p m t"), skTg, [R_skTg], [R_skt])
        for t in range(NTG):
            tl = g * NTG + t
            tc_ = slice(t * 128, (t + 1) * 128)
            for c in range(8):
                MM(bank(3), U[:, c, tc_], wv[:, c, :], c == 0, c == 7, [RU, R_w1], [PB[3]])
            CP("act", vsg[:, t].rearrange("p h d -> p (h d)"), bank(3), [PB[3]], [R_vsg])
            for c in range(8):
                MM(bank(4), U[:, c, tc_], wtok[:, c, 0:512], c == 0, c == 7, [RU, R_w1], [PB[4]])
            for c in range(8):
                MM(bank(5, 0, 160), U[:, c, tc_], wtok[:, c, 512:672], c == 0, c == 7, [RU, R_w1], [PB[5]])
            CP("dve", cql[:, 0:512], bank(4), [PB[4]], [R_cql])
            CP("act", cql[:, 512:672], bank(5, 0, 160), [PB[5]], [R_cql])
            (ssq, lnv, rstd), Rs = stat3()
            ACTF(junk[:, 0:384], cql[:, 0:384], AF.Square, [R_cql], [R_junk, Rs], accum=ssq[:, 0:1])
            ACTF(junk[:, 0:256], cql[:, 384:640], AF.Square, [R_cql], [R_junk, Rs], accum=ssq[:, 1:2])
            ACTF(lnv[:, 0:1], ssq[:, 0:1], AF.Ln, [Rs, R_const], [Rs], bias=epst[:, 0:1], scale=1.0 / 384)
            ACTF(lnv[:, 1:2], ssq[:, 1:2], AF.Ln, [Rs, R_const], [Rs], bias=epst[:, 0:1], scale=1.0 / 256)
            ACTF(rstd[:, 0:2], lnv[:, 0:2], AF.Exp, [Rs], [Rs], scale=-0.5)
            STT(lat_bf[:, 0:384], cql[:, 0:384], rstd[:, 0:1], gql, ALU.mult, ALU.mult, [R_cql, Rs, R_g], [R_lat])
            STT(lat_bf[:, 384:640], cql[:, 384:640], rstd[:, 1:2], gkvl, ALU.mult, ALU.mult, [R_cql, Rs, R_g], [R_lat])
            bl = bankbf(3)[:, 0:640].rearrange("p (c k) -> p c k", c=5)
            for c in range(5):
                TR(bl[:, c, :], lat_bf[:, c * 128:(c + 1) * 128], ident, [R_lat, R_const], [PB[3]])
            CP("dve", latT, bl, [PB[3]], [R_latT])
            for c in range(3):
                MM(bank(4), latT[:, c, :], wq[:, c, 0:512], c == 0, c == 2, [R_latT, R_w1], [PB[4]])
            for c in range(3):
                MM(bank(5, 0, 256), latT[:, c, :], wq[:, c, 512:768], c == 0, c == 2, [R_latT, R_w1], [PB[5]])
            qflat = qf.rearrange("p h d -> p (h d)")
            CP("act", qflat[:, 0:512], bank(4), [PB[4]], [R_qf])
            CP("dve", qflat[:, 512:768], bank(5, 0, 256), [PB[5]], [R_qf])
            for hf in range(2):
                for c in range(2):
                    MM(bank(6 + hf), latT[:, 3 + c, :], wkv[:, c, hf * 512:(hf + 1) * 512], c == 0, c == 1,
                       [R_latT, R_w1], [PB[6 + hf]])
            for hf in range(2):
                bv = bank(6 + hf).rearrange("p (h d) -> p h d", h=4)
                CP("act" if hf == 0 else "dve", kf[:, hf * 4:(hf + 1) * 4, 0:64], bv[:, :, 0:64], [PB[6 + hf]], [R_kf])
                CP("dve" if hf == 0 else "act", vmg[:, t, hf * 4:(hf + 1) * 4, :], bv[:, :, 64:128], [PB[6 + hf]], [R_vmg])
            CP("dve", kf[:, :, 64:96], cql[:, 640:672].unsqueeze(1).broadcast_to([128, 8, 32]), [R_cql], [R_kf])
            head_norm_rope(qf, R_qf, gqh, tl, qTg, R_qTg, t)
            head_norm_rope(kf, R_kf, gkh, tl, kTg, R_kTg, t)
        DMA(ACTQ, qt_s[:, :, rows].rearrange("h d t -> d h t"), qTg, [R_qTg], [R_qt])
        DMA(ACTQ, kt_s[:, :, rows].rearrange("h d t -> d h t"), kTg, [R_kTg], [R_kt])
        for h in range(8):
            DMA(ACTQ, vm_s[h, :, g * NTG:(g + 1) * NTG, :], vmg[:, :, h, :], [R_vmg], [R_vm])
            DMA(ACTQ, vs_s[h, :, g * NTG:(g + 1) * NTG, :], vsg[:, :, h, :], [R_vsg], [R_vs])
    P.barrier()

    st["off"] = PERSIST
    ones64 = onesb[:, 0:64]
    mq = [A([128, T], BF16) for _ in range(4)]
    mk = [A([128, T], BF16) for _ in range(4)]
    mv = [A([128, 32, 128], BF16) for _ in range(4)]
    onz = [A([128, 128], BF16) for _ in range(2)]
    R_mq = [Res() for _ in range(4)]
    R_mk = [Res() for _ in range(4)]
    R_mv = [Res() for _ in range(4)]
    for i_ in range(4):
        MEMSET("dve" if i_ % 2 == 0 else "pool", mv[i_], 0.0, [R_mv[i_]])
    R_onz = Res()
    for i_ in range(2):
        MEMSET("pool", onz[i_], 0.0, [R_onz])
        MEMSET("pool", onz[i_][:, i_ * 64:(i_ + 1) * 64], 1.0, [R_onz])
    pTh = [[A([128, 512], BF16) for _ in range(2)] for _ in range(2)]
    R_pTh = [[Res(), Res()] for _ in range(2)]
    spTh = [[A([128, 512], BF16) for _ in range(2)] for _ in range(2)]
    R_spTh = [[Res(), Res()] for _ in range(2)]
    eTh = [A([128, 512], F32) for _ in range(2)]
    R_eTh = [Res(), Res()]
    spsh = [[A([128, 512], BF16) for _ in range(2)] for _ in range(2)]
    R_spsh = [[Res(), Res()] for _ in range(2)]
    rdenh = [A([128, 512], F32) for _ in range(2)]
    R_rdenh = [Res(), Res()]
    osth = [A([128, 512], BF16) for _ in range(3)]
    R_osth = [Res() for _ in range(3)]
    ocnt = {"n": 0}

    def drive(gens, pattern_head, pattern):
        alive = [True] * len(gens)

        def step(k):
            if alive[k]:
                try:
                    next(gens[k])
                except StopIteration:
                    alive[k] = False
        for k in pattern_head:
            step(k)
        while any(alive):
            for k in pattern:
                step(k)

    def mla_gen(h, h2):
        b4 = h % 4
        p0 = h2 * 64
        Q, K, V = mq[b4], mk[b4], mv[b4]
        sbanks = (0, 1) if h2 == 0 else (2, 3)
        stt = {"n": 0}

        def issue_s(qg, kb):
            r = kb - 4 * qg
            lo = max(r, 0) * 128
            sb_ = sbanks[stt["n"] % 2]
            stt["n"] += 1
            MM(bank(sb_, lo, 512), K[0:96, kb * 128:(kb + 1) * 128], Q[0:96, qg * 512 + lo:(qg + 1) * 512],
               True, True, [R_mk[b4], R_mq[b4]], [PB[sb_]])
            return sb_, lo, r

        blocks = [(qg, kb) for qg in range(NG) for kb in range(4 * (qg + 1))]
        cur = issue_s(*blocks[0])
        nxt = None
        for bi, (qg, kb) in enumerate(blocks):
            nkb = 4 * (qg + 1)
            sb_, lo, r = cur
            if bi + 1 < len(blocks):
                nxt = issue_s(*blocks[bi + 1])
            pt = pTh[h2][bi % 2]
            Rp = R_pTh[h2][bi % 2]
            ACTF(pt[:, lo:512], bank(sb_, lo, 512), AF.Exp, [PB[sb_]], [Rp])
            if r >= 0:
                ASEL(pt[:, lo:lo + 128], pt[:, lo:lo + 128], [[1, 128]], ALU.is_ge, 0, -1, [Rp], [Rp])
            yield
            ob = 4 + 2 * (qg % 2)
            db = ob + 1
            first = (h2 == 0 and kb == 0)
            last = (h2 == 1 and kb == nkb - 1)
            MM(bank(ob, lo, 512), V[:, kb, :], pt[:, lo:512], first, last, [R_mv[b4], Rp], [PB[ob]])
            MM(bank(db, lo, 512), onz[h2], pt[:, lo:512], first, last, [R_onz, Rp], [PB[db]])
            if last:
                rd = rdenh[qg % 2]
                Rr = R_rdenh[qg % 2]
                P.op("dve", lambda e, o_=rd, i_=bank(db): e.reciprocal(out=o_, in_=i_), [PB[db]], [Rr])
                oi = ocnt["n"] % 3
                ocnt["n"] += 1
                TT("dve", osth[oi], bank(ob), rd, ALU.mult, [PB[ob], Rr], [R_osth[oi]])
                DMA(ACTQ, om_s[h // 2, :, qg * 512:(qg + 1) * 512], osth[oi], [R_osth[oi]], [R_om])
            cur = nxt
            yield

    def loads(kind, i):
        if kind == "mla":
            for h2 in range(2):
                h = 2 * i + h2
                b4 = h % 4
                DMA(ACTQ, mq[b4][0:96, :], qt_s[h], [R_qt], [R_mq[b4]])
                DMA(ACTQ, mk[b4][0:96, :], kt_s[h], [R_kt], [R_mk[b4]])
                DMA(ACTQ, mv[b4][:, :, h2 * 64:(h2 + 1) * 64], vm_s[h], [R_vm], [R_mv[b4]])
        else:
            b2 = i % 2
            DMA(ACTQ, mq[b2], sqt_s[i], [R_sqt], [R_mq[b2]])
            for hh in range(2):
                kb_ = 2 * b2 + hh
                if i < 2:
                    MEMSET("dve", mk[kb_], 0.0, [R_mk[kb_]])
                DMA(ACTQ, mk[kb_][hh * 64:(hh + 1) * 64, :], skt_s[i, hh * 64:(hh + 1) * 64, :], [R_skt], [R_mk[kb_]])
                DMA(ACTQ, mv[kb_][:, :, hh * 64:(hh + 1) * 64], vs_s[2 * i + hh], [R_vs], [R_mv[kb_]])

    units = [("mla", i) for i in range(4)] + [("sb", i) for i in range(4)]

    def sb_gen(m, hh):
        b2 = m % 2
        p0 = hh * 64
        kb_ = 2 * b2 + hh
        Q = mq[b2]
        K = mk[kb_]
        V = mv[kb_]
        zb = (0, 1) if hh == 0 else (2, 3)
        stt = {"n": 0, "b": 0}
        pending = [None]

        def issue_z(qg, kb):
            r = kb - 4 * qg
            lo = max(r, 0) * 128
            sb_ = zb[stt["n"] % 2]
            stt["n"] += 1
            MM(bank(sb_, lo, 512), K[:, kb * 128:(kb + 1) * 128], Q[:, qg * 512 + lo:(qg + 1) * 512],
               True, True, [R_mk[kb_], R_mq[b2]], [PB[sb_]])
            return sb_, lo, r

        def make_av(ob, lo, kb, pt, Rp, first, last, qg):
            first = first and hh == 0
            last = last and hh == 1

            def f():
                MM(bank(ob, lo, 512), V[:, kb, :], pt[:, lo:512], first, last, [R_mv[kb_], Rp], [PB[ob]])
                if last:
                    oi = ocnt["n"] % 3
                    ocnt["n"] += 1
                    CP("dve", osth[oi], bank(ob), [PB[ob]], [R_osth[oi]])
                    DMA(ACTQ, os_s[m, :, qg * 512:(qg + 1) * 512], osth[oi], [R_osth[oi]], [R_os])
            return f

        for qg in range(NG):
            nkb = 4 * (qg + 1)
            ob = 4 + (qg % 4)
            sps = spsh[hh][qg % 2]
            Rs = R_spsh[hh][qg % 2]
            MEMSET("pool", sps, 0.0, [Rs])
            order = list(range(nkb - 1, -1, -1))
            cur = issue_z(qg, order[0])
            nxt = None
            for ii, kb in enumerate(order):
                sb_, lo, r = cur
                bsel = stt["b"] % 2
                stt["b"] += 1
                spt = spTh[hh][bsel]
                Rsp = R_spTh[hh][bsel]
                pt = pTh[hh][bsel]
                Rp = R_pTh[hh][bsel]
                if pending[0] is not None:
                    pending[0]()
                    pending[0] = None
                ACTF(eTh[hh][:, lo:512], bank(sb_, lo, 512), AF.Exp, [PB[sb_]], [R_eTh[hh]])
                ACTF(spt[:, lo:512], eTh[hh][:, lo:512], AF.Ln, [R_eTh[hh]], [Rsp], bias=1.0)
                if r >= 0:
                    ASEL(spt[:, lo:lo + 128], spt[:, lo:lo + 128], [[1, 128]], ALU.is_gt, 0, -1, [Rsp], [Rsp])
                if ii + 1 < len(order):
                    nxt = issue_z(qg, order[ii + 1])
                yield
                MM(bank(sb_, lo, 512), trineg, spt[:, lo:512], False, ii == 0, [R_const, Rsp], [PB[sb_]])
                if ii > 0:
                    MM(bank(sb_, lo, 512), negones, sps[:, lo:512], False, True, [R_const, Rs], [PB[sb_]])
                ACTF(pt[:, lo:512], bank(sb_, lo, 512), AF.Exp, [PB[sb_]], [Rp])
                if r >= 0:
                    ASEL(pt[:, lo:lo + 128], pt[:, lo:lo + 128], [[1, 128]], ALU.is_gt, 0, -1, [Rp], [Rp])
                if ii + 1 < len(order):
                    TT("dve", sps[:, lo:512], sps[:, lo:512], spt[:, lo:512], ALU.add, [Rs, Rsp], [Rs])
                pending[0] = make_av(ob, lo, kb, pt, Rp, ii == 0, ii == len(order) - 1, qg)
                cur = nxt
                yield
        if pending[0] is not None:
            pending[0]()
            pending[0] = None
        yield

    loads(*units[0])
    for ui_, (kind, i) in enumerate(units):
        if ui_ + 1 < len(units):
            loads(*units[ui_ + 1])
        if kind == "mla":
            drive([mla_gen(2 * i, 0), mla_gen(2 * i + 1, 1)], [], [0, 1])
        else:
            drive([sb_gen(i, 0), sb_gen(i, 1)], [0], [1, 0, 0, 1])
    P.barrier()

    st["off"] = P1
    wbm = A([128, 4, 1024], BF16)
    wbs = A([128, 4, 1024], BF16)
    wo = A([128, 8, 1024], BF16)
    wpg = A([128, 8, 1024], BF16)
    wpp = A([128, 2, 1024], BF16)
    R_w3 = Res("p3w")
    for dst, n in ((wbm, "w_branch_mla"), (wbs, "w_branch_sb"), (wo, "w_out"), (wpg, "w_ple_gate"), (wpp, "w_ple_proj")):
        DMA(WQ, dst, ws[n].rearrange("(c p) n -> p c n", p=128), [R_ws[n]], [R_w3])
    omg = A([128, 4, G], BF16)
    osg = A([128, 4, G], BF16)
    R_omg = Res()
    R_osg = Res()
    sgm = [A([128, G], F32)] * 2
    sgs = [A([128, G], F32)] * 2
    R_sgm = [Res()] * 2
    R_sgs = [Res()] * 2
    tm1 = [A([128, G], F32)] * 2
    tm2 = [A([128, G], F32)] * 2
    R_tm1 = [Res()] * 2
    R_tm2 = [Res()] * 2
    mT = A([128, 8, G], BF16)
    R_mT = Res()
    pg = A([128, NTG, 256], F32)
    R_pg = Res()
    pgb = A([128, 256], BF16)
    R_pgb = Res()
    ppT = A([128, 2, 128], BF16)
    R_ppT = Res()
    sgp = A([128, 1024], F32)
    R_sgp = Res()
    final = []
    for g in range(NG):
        rows = slice(g * G, (g + 1) * G)
        DMA(ACTQ, hg, h1_s[rows, :].rearrange("(t p) d -> p t d", p=128), [R_h1], [R_hg])
        DMA(ACTQ, omg, om_s[:, :, rows].rearrange("m p t -> p m t"), [R_om], [R_omg])
        DMA(ACTQ, osg, os_s[:, :, rows].rearrange("m p t -> p m t"), [R_os], [R_osg])
        DMA(ACTQ, pg, p_d[rows, :].rearrange("(t p) d -> p t d", p=128), [], [R_pg])
        norm_to_uT(gbc["mix_norm"], 0)
        U = uT[0]
        RU = R_uT[0]
        for mb in range(2):
            sGm, RGm = ring_load(Win_v[:, :, C_GATE + mb * 512:C_GATE + (mb + 1) * 512], [128, 8, 512], R_ws["w_in"])
            sGs, RGs = ring_load(Win_v[:, :, C_GATE + 1024 + mb * 512:C_GATE + 1024 + (mb + 1) * 512], [128, 8, 512],
                                 R_ws["w_in"])
            for mm_ in range(4):
                m = mb * 4 + mm_
                i2 = m % 2
                for c in range(8):
                    MM(bank(0), sGm[:, c, mm_ * 128:(mm_ + 1) * 128], U[:, c, :], c == 0, c == 7, [RGm, RU], [PB[0]])
                for c in range(8):
                    MM(bank(1), sGs[:, c, mm_ * 128:(mm_ + 1) * 128], U[:, c, :], c == 0, c == 7, [RGs, RU], [PB[1]])
                for k in range(4):
                    MM(bank(2), wbm[:, k, m * 128:(m + 1) * 128], omg[:, k, :], k == 0, k == 3, [R_w3, R_omg], [PB[2]])
                for k in range(4):
                    MM(bank(3), wbs[:, k, m * 128:(m + 1) * 128], osg[:, k, :], k == 0, k == 3, [R_w3, R_osg], [PB[3]])
                ACTF(sgm[i2], bank(0), AF.Sigmoid, [PB[0]], [R_sgm[i2]])
                ACTF(sgs[i2], bank(1), AF.Sigmoid, [PB[1]], [R_sgs[i2]])
                TT("dve", tm1[i2], bank(2), sgm[i2], ALU.mult, [PB[2], R_sgm[i2]], [R_tm1[i2]])
                TT("dve", tm2[i2], bank(3), sgs[i2], ALU.mult, [PB[3], R_sgs[i2]], [R_tm2[i2]])
                TT("dve", mT[:, m, :], tm1[i2], tm2[i2], ALU.add, [R_tm1[i2], R_tm2[i2]], [R_mT])
        for t in range(NTG):
            tc_ = slice(t * 128, (t + 1) * 128)
            for hf in range(2):
                bk = 4 + (t % 2) * 2 + hf
                for c in range(8):
                    MM(bank(bk), mT[:, c, tc_], wo[:, c, hf * 512:(hf + 1) * 512], c == 0, c == 7, [R_mT, R_w3], [PB[bk]])
                dst = hg[:, t, hf * 512:(hf + 1) * 512]
                TT("dve", dst, bank(bk), dst, ALU.add, [PB[bk], R_hg], [R_hg])
        norm_to_uT(gbc["ffn2_norm"], 1)
        ffn("ffn2_w_in", "ffn2_w_out", 1)
        norm_to_uT(gbc["ple_norm"], 0)
        for t in range(NTG):
            tc_ = slice(t * 128, (t + 1) * 128)
            CP("dve", pgb, pg[:, t, :], [R_pg], [R_pgb])
            bp = bankbf(2)[:, 0:256].rearrange("p (c k) -> p c k", c=2)
            for c in range(2):
                TR(bp[:, c, :], pgb[:, c * 128:(c + 1) * 128], ident, [R_pgb, R_const], [PB[2]])
            CP("act", ppT, bp, [PB[2]], [R_ppT])
            for hf in range(2):
                bk = 4 + hf
                for c in range(8):
                    MM(bank(bk), U[:, c, tc_], wpg[:, c, hf * 512:(hf + 1) * 512], c == 0, c == 7, [RU, R_w3], [PB[bk]])
                ACTF(sgp[:, hf * 512:(hf + 1) * 512], bank(bk), AF.Sigmoid, [PB[bk]], [R_sgp])
                bk2 = 6 + hf
                for c in range(2):
                    MM(bank(bk2), ppT[:, c, :], wpp[:, c, hf * 512:(hf + 1) * 512], c == 0, c == 1, [R_ppT, R_w3], [PB[bk2]])
                TT("dve", sgp[:, hf * 512:(hf + 1) * 512], bank(bk2), sgp[:, hf * 512:(hf + 1) * 512], ALU.mult,
                   [PB[bk2], R_sgp], [R_sgp])
                dst = hg[:, t, hf * 512:(hf + 1) * 512]
                TT("dve", dst, dst, sgp[:, hf * 512:(hf + 1) * 512], ALU.add, [R_sgp, R_hg], [R_hg])
        final.append(DMA(ACTQ, out_d[rows, :].rearrange("(t p) d -> p t d", p=128), hg, [R_hg], [R_out]))

    print("arena marks", PERSIST, P1, st["off"], {e: len(P.ins[e]) for e in P.ENG})
    P.emit(nc, final_waits=final)
    es.close()
    return nc


_NC = None


def kernel(**inputs):
    global _NC
    if _NC is None:
        _NC = build_nc()
    nc = _NC
    x = np.asarray(inputs["x"], dtype=np.float32)
    p = np.asarray(inputs["p"], dtype=np.float32)
    pos = np.asarray(inputs["positions"], dtype=np.int32)
    common = {}
    for n, r, c in WEIGHTS:
        common[n] = np.ascontiguousarray(np.asarray(inputs[n], dtype=np.float32).reshape(r, c))
    for n, c in GAINS:
        common[n] = np.ascontiguousarray(np.asarray(inputs[n], dtype=np.float32).reshape(1, c))
    in_maps = []
    for b in range(8):
        m = dict(common)
        m["x"] = np.ascontiguousarray(x[b])
        m["p"] = np.ascontiguousarray(p[0, b])
        m["posT"] = np.ascontiguousarray(pos[b].reshape(32, 128).T)
        in_maps.append(m)
    res = run_bass_kernel_spmd(nc, in_maps, core_ids=list(range(8)))
    out = np.stack([np.asarray(r["out"], dtype=np.float32) for r in res.results], axis=0)
    return out
```

```python
import math
from contextlib import ExitStack

import numpy as np
import concourse.bass as bass
import concourse.mybir as mybir
from concourse.bass_utils import run_bass_kernel_spmd

F32 = mybir.dt.float32
BF16 = mybir.dt.bfloat16
I32 = mybir.dt.int32
AF = mybir.ActivationFunctionType
ALU = mybir.AluOpType
AX = mybir.AxisListType

T = 4096
D = 1024
DFF = 2816
G = 512
NG = T // G
NTG = G // 128
NJ = DFF // 128
EPS = 1e-6
DEBUG = False


class Res:
    __slots__ = ("name", "w", "r", "excl")

    def __init__(self, name="", excl=False):
        self.name = name
        self.w = None
        self.r = []
        self.excl = excl


class Prog:
    ENG = ["pe", "act", "dve", "pool", "sp"]
    EPOCH = 4096
    NDS = 16

    def __init__(self):
        self.ins = {e: [] for e in self.ENG}
        self.ndma = {e: 0 for e in self.ENG}
        self.dmas = {e: [] for e in self.ENG}

    def op(self, eng, fn, reads=(), writes=(), dma=False, extra=()):
        idx = len(self.ins[eng])
        me = (eng, idx)
        deps = set(extra)
        if any(r.excl for r in reads):
            writes = list(writes) + [r for r in reads if r.excl and r not in writes]
            reads = [r for r in reads if not r.excl]
        for r in reads:
            if r.w is not None:
                deps.add(r.w)
        for w in writes:
            if w.w is not None:
                deps.add(w.w)
            for rd in w.r:
                if rd[0] == eng and not dma and not self.ins[eng][rd[1]]["dma"]:
                    continue
                deps.add(rd)
        fd = set()
        for d in deps:
            if d == me:
                continue
            if d[0] == eng and eng == "pe":
                continue
            fd.add(d)
        rec = dict(fn=fn, deps=fd, dma=dma, signal=False, dj=None)
        if dma:
            rec["dj"] = self.ndma[eng]
            self.ndma[eng] += 1
            self.dmas[eng].append(me)
        self.ins[eng].append(rec)
        for d in fd:
            self.ins[d[0]][d[1]]["signal"] = True
        for r in reads:
            r.r.append(me)
        for w in writes:
            w.w = me
            w.r = []
        return me

    def barrier(self):
        tails = []
        for e in self.ENG:
            last = None
            for i in range(len(self.ins[e]) - 1, -1, -1):
                if not self.ins[e][i]["dma"]:
                    last = (e, i)
                    break
            if last is not None:
                tails.append(last)
            tails.extend(self.dmas[e][-self.NDS:])
        for e in self.ENG:
            self.op(e, lambda en: en.nop(), extra=[t for t in tails])

    def emit(self, nc, final_waits=()):
        with ExitStack() as es:
            esem = {}
            for e in self.ENG:
                k = 0
                for r in self.ins[e]:
                    if r["signal"] and not r["dma"]:
                        r["k"] = k
                        k += 1
                nep = (k + self.EPOCH - 1) // self.EPOCH
                esem[e] = [es.enter_context(nc.semaphore(f"s_{e}_{i}")) for i in range(nep)]
            dsem = {}
            for e in self.ENG:
                if self.ndma[e]:
                    dsem[e] = [es.enter_context(nc.semaphore(f"d_{e}_{i}")) for i in range(self.NDS)]
            block = es.enter_context(nc.Block())
            prog = self

            def run(e, engobj):
                seen = {}

                def wait_compute(e2, k):
                    if seen.get(e2, -1) >= k:
                        return
                    seen[e2] = k
                    engobj.wait_ge(esem[e2][k // prog.EPOCH], k % prog.EPOCH + 1)

                def wait_dma(q, j):
                    key = ("d", q, j % prog.NDS)
                    if seen.get(key, -1) >= j:
                        return
                    seen[key] = j
                    engobj.wait_ge(dsem[q][j % prog.NDS], 16 * (j // prog.NDS + 1))

                for r in prog.ins[e]:
                    for d in sorted(r["deps"]):
                        p = prog.ins[d[0]][d[1]]
                        if p["dma"]:
                            wait_dma(d[0], p["dj"])
                        else:
                            wait_compute(d[0], p["k"])
                    if r["dma"]:
                        j = r["dj"]
                        if j >= prog.NDS:
                            wait_dma(e, j - prog.NDS)
                        bi = r["fn"](engobj)
                        bi.then_inc(dsem[e][j % prog.NDS], 16)
                    else:
                        bi = r["fn"](engobj)
                        if r["signal"]:
                            k = r["k"]
                            bi.then_inc(esem[e][k // prog.EPOCH], 1)
                if e == "sp":
                    for d in final_waits:
                        p = prog.ins[d[0]][d[1]]
                        wait_dma(d[0], p["dj"])

            @block.tensor
            def _(eng):
                run("pe", eng)

            @block.scalar
            def _(eng):
                run("act", eng)

            @block.vector
            def _(eng):
                run("dve", eng)

            @block.gpsimd
            def _(eng):
                run("pool", eng)

            @block.sync
            def _(eng):
                run("sp", eng)


WEIGHTS = [
    ("ffn1_w_in", 1024, 5632), ("ffn1_w_out", 2816, 1024), ("w_in", 1024, 4256),
    ("w_q_up", 384, 768), ("w_kv_up", 256, 1024), ("w_branch_mla", 512, 1024),
    ("w_branch_sb", 512, 1024), ("w_out", 1024, 1024), ("ffn2_w_in", 1024, 5632),
    ("ffn2_w_out", 2816, 1024), ("w_ple_gate", 1024, 1024), ("w_ple_proj", 256, 1024),
]
GAINS = [("ffn1_norm", 1024), ("mix_norm", 1024), ("q_latent_norm", 384), ("kv_latent_norm", 256),
         ("q_head_norm", 96), ("k_head_norm", 96), ("ffn2_norm", 1024), ("ple_norm", 1024)]

C_SBQ = 672
C_SBK = C_SBQ + 512
C_SBV = C_SBK + 512
C_GATE = C_SBV + 512


def build_nc():
    nc = bass.Bass("TRN2", target_bir_lowering=False)
    P = Prog()
    ACTQ = "pool"
    WQ = "sp"

    def dram(name, shape, dt, kind):
        return nc.dram_tensor(name, shape, dt, kind=kind).ap()

    x_d = dram("x", [T, D], F32, "ExternalInput")
    p_d = dram("p", [T, 256], F32, "ExternalInput")
    pos_d = dram("posT", [128, 32], I32, "ExternalInput")
    w_d = {n: dram(n, [r, c], F32, "ExternalInput") for n, r, c in WEIGHTS}
    g_d = {n: dram(n, [1, c], F32, "ExternalInput") for n, c in GAINS}
    out_d = dram("out", [T, D], F32, "ExternalOutput")
    sk = "ExternalOutput" if DEBUG else "Internal"
    ws = {n: dram("s_" + n, [r, c], BF16, "Internal") for n, r, c in WEIGHTS}
    h1_s = dram("s_h1", [T, D], F32, sk)
    qt_s = dram("s_qt", [8, 96, T], BF16, sk)
    kt_s = dram("s_kt", [8, 96, T], BF16, sk)
    vm_s = dram("s_vm", [8, 128, 32, 64], BF16, sk)
    sqt_s = dram("s_sqt", [4, 128, T], BF16, sk)
    skt_s = dram("s_skt", [4, 128, T], BF16, sk)
    vs_s = dram("s_vs", [8, 128, 32, 64], BF16, sk)
    om_s = dram("s_om", [4, 128, T], BF16, sk)
    os_s = dram("s_os", [4, 128, T], BF16, sk)
    R_ws = {n: Res(n) for n, _, _ in WEIGHTS}
    R_h1, R_qt, R_kt, R_vm, R_sqt, R_skt, R_vs, R_om, R_os, R_out = [Res() for _ in range(10)]

    es = ExitStack()
    ARENA_N = 103 * 1024
    arena = es.enter_context(nc.sbuf_tensor("arena", [128, ARENA_N], BF16))
    ps = es.enter_context(nc.psum_tensor("ps", [128, 4096], F32))
    PB = [Res(f"bank{b}", excl=True) for b in range(8)]

    def bank(b, lo=0, hi=512, p0=0, p1=128):
        return ps[p0:p1, b * 512 + lo:b * 512 + hi]

    def bankbf(b):
        return ps[:, b * 512:(b + 1) * 512].bitcast(BF16)

    st = {"off": 0}

    def A(shape, dt):
        n = 1
        for s_ in shape[1:]:
            n *= s_
        nb = n * (2 if dt == BF16 else 4)
        ne = ((nb + 31) // 32) * 16
        off = st["off"]
        assert off + ne <= ARENA_N, ("arena overflow", off, ne)
        st["off"] = off + ne
        sl = arena[:, off:off + nb // 2]
        if dt != BF16:
            sl = sl.bitcast(dt)
        if len(shape) > 2:
            names = " ".join(f"d{i}" for i in range(len(shape) - 1))
            kw = {f"d{i}": shape[i + 1] for i in range(len(shape) - 1)}
            sl = sl.rearrange(f"p ({names}) -> p {names}", **kw)
        if shape[0] < 128:
            sl = sl[0:shape[0]]
        return sl

    def MM(out, lhsT, rhs, start, stop, r, w):
        P.op("pe", lambda e: e.matmul(out, lhsT=lhsT, rhs=rhs, start=start, stop=stop), r, w)

    def TR(out, in_, ident, r, w):
        P.op("pe", lambda e: e.transpose(out=out, in_=in_, identity=ident), r, w)

    def ACTF(out, in_, func, r, w, bias=None, scale=None, accum=None):
        kw = {}
        if bias is not None:
            kw["bias"] = bias
        if scale is not None:
            kw["scale"] = scale
        if accum is not None:
            kw["accum_out"] = accum
        P.op("act", lambda e: e.activation(out=out, in_=in_, func=func, **kw), r, w)

    def TT(eng, out, in0, in1, op, r, w):
        P.op(eng, lambda e: e.tensor_tensor(out=out, in0=in0, in1=in1, op=op), r, w)

    def TS(eng, out, in0, s1, op0, r, w, s2=None, op1=None):
        if op1 is None:
            P.op(eng, lambda e: e.tensor_scalar(out=out, in0=in0, scalar1=s1, scalar2=None, op0=op0), r, w)
        else:
            P.op(eng, lambda e: e.tensor_scalar(out=out, in0=in0, scalar1=s1, scalar2=s2, op0=op0, op1=op1), r, w)

    def STT(out, in0, scalar, in1, op0, op1, r, w):
        P.op("dve", lambda e: e.scalar_tensor_tensor(out=out, in0=in0, scalar=scalar, in1=in1, op0=op0, op1=op1), r, w)

    def CP(eng, out, in_, r, w, scale=None):
        if eng == "act":
            if scale is None:
                P.op("act", lambda e: e.activation(out=out, in_=in_, func=AF.Copy), r, w)
            else:
                P.op("act", lambda e: e.activation(out=out, in_=in_, func=AF.Copy, scale=scale), r, w)
        else:
            if scale is None:
                P.op(eng, lambda e: e.tensor_copy(out=out, in_=in_), r, w)
            else:
                P.op(eng, lambda e: e.tensor_scalar(out=out, in0=in_, scalar1=scale, scalar2=None, op0=ALU.mult), r, w)

    def DMA(q, out, in_, r, w):
        return P.op(q, lambda e: e.dma_start(out=out, in_=in_), r, w, dma=True)

    def MEMSET(eng, ap, val, w):
        P.op(eng, lambda e: e.memset(ap, val), (), w)

    def ASEL(out, in_, pattern, cmp, base, cm, r, w):
        P.op("pool", lambda e: e.affine_select(out=out, in_=in_, pattern=pattern, compare_op=cmp, fill=0.0,
                                                base=base, channel_multiplier=cm), r, w)

    ident = A([128, 128], BF16)
    onesb = A([128, 128], BF16)
    negones = A([128, 128], BF16)
    trineg = A([128, 128], BF16)
    epst = A([128, 1], F32)
    cosT = A([128, 32, 16], F32)
    sinT = A([128, 32, 16], F32)
    R_const = Res("const")
    R_cs = Res("cossin")
    MEMSET("pool", onesb, 1.0, [R_const])
    MEMSET("pool", negones, -1.0, [R_const])
    MEMSET("pool", epst, EPS, [R_const])
    ASEL(ident, onesb, [[-1, 128]], ALU.is_equal, 0, 1, [R_const], [R_const])
    ASEL(trineg, negones, [[-1, 128]], ALU.is_ge, 0, 1, [R_const], [R_const])
    gbc = {n: A([128, c], F32) for n, c in GAINS if c == 1024}
    R_g = Res("gains")
    for n in gbc:
        DMA(ACTQ, gbc[n], g_d[n].broadcast_to([128, 1024]), [], [R_g])
    PERSIST = st["off"]

    posi = A([128, 32], I32)
    posf = A([128, 32], F32)
    invf = A([128, 16], F32)
    ang = A([128, 32, 16], F32)
    kq = A([128, 32, 16], F32)
    ki = A([128, 32, 16], I32)
    r1 = A([128, 32, 16], F32)
    r2 = A([128, 32, 16], F32)
    Rt = Res("ropetmp")
    DMA(ACTQ, posi, pos_d, [], [Rt])
    CP("dve", posf, posi, [Rt], [Rt])
    inv_np = (np.float32(10000.0) ** (-np.arange(0, 32, 2, dtype=np.float32) / np.float32(32))).astype(np.float32)
    for j in range(16):
        MEMSET("dve", invf[:, j:j + 1], float(inv_np[j]), [Rt])
    TT("dve", ang, posf.unsqueeze(2).broadcast_to([128, 32, 16]), invf.unsqueeze(1).broadcast_to([128, 32, 16]),
       ALU.mult, [Rt], [Rt])
    TWO_PI = 2.0 * math.pi
    C1 = 6.28125
    C2 = TWO_PI - C1
    for (dst, shift) in ((sinT, 0.0), (cosT, math.pi / 2)):
        TS("dve", r1, ang, shift, ALU.add, [Rt], [Rt])
        TS("dve", kq, r1, 1.0 / TWO_PI, ALU.mult, [Rt], [Rt])
        CP("dve", ki, kq, [Rt], [Rt])
        CP("dve", kq, ki, [Rt], [Rt])
        STT(r1, kq, -C1, r1, ALU.mult, ALU.add, [Rt], [Rt])
        STT(r1, kq, -C2, r1, ALU.mult, ALU.add, [Rt], [Rt])
        TS("dve", r2, r1, math.pi, ALU.is_gt, [Rt], [Rt], s2=-TWO_PI, op1=ALU.mult)
        TT("dve", r1, r1, r2, ALU.add, [Rt], [Rt])
        TS("dve", r2, r1, -math.pi, ALU.is_lt, [Rt], [Rt], s2=TWO_PI, op1=ALU.mult)
        TT("dve", r1, r1, r2, ALU.add, [Rt], [Rt])
        TS("dve", r1, r1, math.pi, ALU.min, [Rt], [Rt], s2=-math.pi, op1=ALU.max)
        ACTF(dst, r1, AF.Sin, [Rt], [R_cs])

    stg_f = [A([128, 5632], F32) for _ in range(2)]
    stg_b = [A([128, 5632], BF16) for _ in range(2)]
    R_sf = [Res(), Res()]
    R_sb = [Res(), Res()]
    ui = 0
    for n, r, c in WEIGHTS:
        nr = r // 128
        per = max(1, 5632 // c)
        c0 = 0
        while c0 < nr:
            k = min(per, nr - c0)
            b = ui % 2
            src = w_d[n][c0 * 128:(c0 + k) * 128, :].rearrange("(i p) n -> p i n", p=128)
            dst = ws[n][c0 * 128:(c0 + k) * 128, :].rearrange("(i p) n -> p i n", p=128)
            sf = stg_f[b][:, 0:k * c].rearrange("p (i n) -> p i n", i=k)
            sb_ = stg_b[b][:, 0:k * c].rearrange("p (i n) -> p i n", i=k)
            DMA(WQ, sf, src, [], [R_sf[b]])
            CP("dve" if ui % 2 == 0 else "act", sb_, sf, [R_sf[b]], [R_sb[b]])
            DMA(ACTQ, dst, sb_, [R_sb[b]], [R_ws[n]])
            ui += 1
            c0 += k
    P.barrier()
    st["off"] = PERSIST

    RING_N = 4
    ring = [A([128, 4096], BF16) for _ in range(RING_N)]
    R_ring = [Res(f"ring{i}") for i in range(RING_N)]
    rst = {"i": 0}

    def ring_load(src_ap, shape, Rsrc):
        i = rst["i"] % RING_N
        rst["i"] += 1
        n = 1
        for s_ in shape[1:]:
            n *= s_
        dst = ring[i][:, 0:n]
        if len(shape) == 3:
            dst = dst.rearrange("p (a b) -> p a b", a=shape[1])
        DMA(WQ, dst, src_ap, [Rsrc], [R_ring[i]])
        return dst, R_ring[i]

    hg = A([128, NTG, 1024], F32)
    R_hg = Res("hg")
    _uT0 = A([128, 8, G], BF16)
    uT = [_uT0, _uT0]
    _RuT = Res("uT")
    R_uT = [_RuT, _RuT]
    u_tm = [A([128, 1024], BF16) for _ in range(2)]
    R_utm = [Res(), Res()]
    junk = A([128, 1024], BF16)
    R_junk = Res()
    gT = A([128, NJ, G], BF16)
    R_gT = Res("gT")
    sa = [A([128, G], BF16) for _ in range(2)]
    R_sa = [Res(), Res()]
    NSTAT = 12
    stat = A([128, NSTAT * 3 * 8], F32)
    sst = {"i": 0}

    def stat3():
        i = sst["i"] % NSTAT
        sst["i"] += 1
        base = i * 24
        return [stat[:, base + 8 * k: base + 8 * k + 8] for k in range(3)], Res()

    def rstd_from_ssq(ssq, lnv, rstd, n, ncol, Rs):
        ACTF(lnv[:, 0:ncol], ssq[:, 0:ncol], AF.Ln, [Rs, R_const], [Rs], bias=epst[:, 0:1], scale=1.0 / n)
        ACTF(rstd[:, 0:ncol], lnv[:, 0:ncol], AF.Exp, [Rs], [Rs], scale=-0.5)

    def norm_to_uT(gain, ui_):
        (ssq, lnv, rstd), Rs = stat3()
        for t in range(NTG):
            ACTF(junk, hg[:, t, :], AF.Square, [R_hg], [R_junk, Rs], accum=ssq[:, t:t + 1])
        rstd_from_ssq(ssq, lnv, rstd, 1024.0, NTG, Rs)
        for t in range(NTG):
            b = t % 2
            STT(u_tm[b], hg[:, t, :], rstd[:, t:t + 1], gain, ALU.mult, ALU.mult, [R_hg, Rs, R_g], [R_utm[b]])
            for c in range(8):
                TR(bankbf(b)[:, c * 128:(c + 1) * 128], u_tm[b][:, c * 128:(c + 1) * 128], ident,
                   [R_utm[b], R_const], [PB[b]])
            CP("act" if t % 2 == 0 else "dve", uT[ui_][:, :, t * 128:(t + 1) * 128],
               bankbf(b).rearrange("p (c k) -> p c k", c=8), [PB[b]], [R_uT[ui_]])

    JB = [(j0, min(j0 + 4, NJ)) for j0 in range(0, NJ, 4)]

    def ffn(w1n, w2n, ui_):
        W1 = ws[w1n].rearrange("(c p) n -> p c n", p=128)
        W2 = ws[w2n].rearrange("(j p) n -> p j n", p=128)
        for (j0, j1) in JB:
            nj = j1 - j0
            sA, RA = ring_load(W1[:, :, j0 * 128:j1 * 128], [128, 8, nj * 128], R_ws[w1n])
            sB, RB = ring_load(W1[:, :, DFF + j0 * 128:DFF + j1 * 128], [128, 8, nj * 128], R_ws[w1n])
            for jj in range(nj):
                j = j0 + jj
                ab = 2 * (j % 2)
                bb = ab + 1
                for c in range(8):
                    MM(bank(ab), sA[:, c, jj * 128:(jj + 1) * 128], uT[ui_][:, c, :], c == 0, c == 7,
                       [RA, R_uT[ui_]], [PB[ab]])
                for c in range(8):
                    MM(bank(bb), sB[:, c, jj * 128:(jj + 1) * 128], uT[ui_][:, c, :], c == 0, c == 7,
                       [RB, R_uT[ui_]], [PB[bb]])
                ACTF(sa[j % 2], bank(ab), AF.Silu, [PB[ab]], [R_sa[j % 2]])
                TT("dve", gT[:, j, :], bank(bb), sa[j % 2], ALU.mult, [PB[bb], R_sa[j % 2]], [R_gT])
        for tp in range(2):
            for (j0, j1) in JB:
                nj = j1 - j0
                sW, RW = ring_load(W2[:, j0:j1, :], [128, nj, 1024], R_ws[w2n])
                for jj in range(nj):
                    j = j0 + jj
                    for t in (2 * tp, 2 * tp + 1):
                        for hf in range(2):
                            bk = 4 + (t % 2) * 2 + hf
                            MM(bank(bk), gT[:, j, t * 128:(t + 1) * 128], sW[:, jj, hf * 512:(hf + 1) * 512],
                               j == 0, j == NJ - 1, [RW, R_gT], [PB[bk]])
            for t in (2 * tp, 2 * tp + 1):
                for hf in range(2):
                    bk = 4 + (t % 2) * 2 + hf
                    dst = hg[:, t, hf * 512:(hf + 1) * 512]
                    STT(dst, bank(bk), 0.5, dst, ALU.mult, ALU.add, [PB[bk], R_hg], [R_hg])

    P1 = st["off"]
    wtok = A([128, 8, 672], BF16)
    wv = A([128, 8, 512], BF16)
    wq = A([128, 3, 768], BF16)
    wkv = A([128, 2, 1024], BF16)
    R_w1 = Res("p1w")
    Win_v = ws["w_in"].rearrange("(c p) n -> p c n", p=128)
    DMA(WQ, wtok, Win_v[:, :, 0:672], [R_ws["w_in"]], [R_w1])
    DMA(WQ, wv, Win_v[:, :, C_SBV:C_SBV + 512], [R_ws["w_in"]], [R_w1])
    DMA(WQ, wq, ws["w_q_up"].rearrange("(c p) n -> p c n", p=128), [R_ws["w_q_up"]], [R_w1])
    DMA(WQ, wkv, ws["w_kv_up"].rearrange("(c p) n -> p c n", p=128), [R_ws["w_kv_up"]], [R_w1])
    gql = A([128, 384], F32)
    gkvl = A([128, 256], F32)
    gqh = A([128, 8, 96], F32)
    gkh = A([128, 8, 96], F32)
    DMA(ACTQ, gql, g_d["q_latent_norm"].broadcast_to([128, 384]), [], [R_g])
    DMA(ACTQ, gkvl, g_d["kv_latent_norm"].broadcast_to([128, 256]), [], [R_g])
    DMA(ACTQ, gqh, bass.AP(g_d["q_head_norm"].tensor, 0, [[0, 128], [0, 8], [1, 96]]), [], [R_g])
    DMA(ACTQ, gkh, bass.AP(g_d["k_head_norm"].tensor, 0, [[0, 128], [0, 8], [1, 96]]), [], [R_g])
    TS("dve", gqh, gqh, 1.0 / math.sqrt(96.0), ALU.mult, [R_g], [R_g])
    cql = A([128, 672], F32)
    R_cql = Res()
    lat_bf = A([128, 640], BF16)
    R_lat = Res()
    latT = A([128, 5, 128], BF16)
    R_latT = Res()
    qf = A([128, 8, 96], F32)
    R_qf = Res()
    kf = A([128, 8, 96], F32)
    R_kf = Res()
    sq = A([128, 8, 96], F32)
    R_sq = Res()
    rt = [A([128, 8, 16], F32) for _ in range(4)]
    R_rt = Res()
    qb = A([128, 8, 96], BF16)
    R_qb = Res()
    qTg = A([96, 8, G], BF16)
    R_qTg = Res()
    kTg = A([96, 8, G], BF16)
    R_kTg = Res()
    vmg = A([128, NTG, 8, 64], BF16)
    R_vmg = Res()
    vsg = A([128, NTG, 8, 64], BF16)
    R_vsg = Res()
    sqTg = A([128, 4, G], BF16)
    R_sqTg = Res()
    skTg = A([128, 4, G], BF16)
    R_skTg = Res()

    def head_norm_rope(src, Rsrc, gains, tile_idx, dstT, RdstT, tcol):
        (ssq, lnv, rstd), Rs = stat3()
        TT("dve", sq, src, src, ALU.mult, [Rsrc], [R_sq])
        P.op("dve", lambda e: e.tensor_reduce(out=ssq[:, 0:8], in_=sq, axis=AX.X, op=ALU.add), [R_sq], [Rs])
        rstd_from_ssq(ssq, lnv, rstd, 96.0, 8, Rs)
        TT("dve", src, src, gains, ALU.mult, [Rsrc, R_g], [Rsrc])
        TT("dve", src, src, rstd[:, 0:8].unsqueeze(2).broadcast_to([128, 8, 96]), ALU.mult, [Rsrc, Rs], [Rsrc])
        cs = cosT[:, tile_idx, :].unsqueeze(1).broadcast_to([128, 8, 16])
        sn = sinT[:, tile_idx, :].unsqueeze(1).broadcast_to([128, 8, 16])
        x1 = src[:, :, 64:80]
        x2 = src[:, :, 80:96]
        TT("dve", rt[0], x1, cs, ALU.mult, [Rsrc, R_cs], [R_rt])
        TT("dve", rt[1], x2, sn, ALU.mult, [Rsrc, R_cs], [R_rt])
        TT("dve", rt[2], x2, cs, ALU.mult, [Rsrc, R_cs], [R_rt])
        TT("dve", rt[3], x1, sn, ALU.mult, [Rsrc, R_cs], [R_rt])
        CP("dve", qb[:, :, 0:64], src[:, :, 0:64], [Rsrc], [R_qb])
        TT("dve", qb[:, :, 64:80], rt[0], rt[1], ALU.subtract, [R_rt], [R_qb])
        TT("dve", qb[:, :, 80:96], rt[2], rt[3], ALU.add, [R_rt], [R_qb])
        bq = bankbf(2)[0:96].rearrange("p (h k) -> p h k", h=8)
        for h in range(8):
            TR(bq[:, h, :], qb[:, h, :], ident, [R_qb, R_const], [PB[2]])
        CP("act", dstT[:, :, tcol * 128:(tcol + 1) * 128], bq, [PB[2]], [RdstT])

    for g in range(NG):
        rows = slice(g * G, (g + 1) * G)
        DMA(ACTQ, hg, x_d[rows, :].rearrange("(t p) d -> p t d", p=128), [], [R_hg])
        norm_to_uT(gbc["ffn1_norm"], 0)
        ffn("ffn1_w_in", "ffn1_w_out", 0)
        DMA(ACTQ, h1_s[rows, :].rearrange("(t p) d -> p t d", p=128), hg, [R_hg], [R_h1])
        norm_to_uT(gbc["mix_norm"], 1)
        U = uT[1]
        RU = R_uT[1]
        for (c0, dstT, RdT, scl) in ((C_SBQ, sqTg, R_sqTg, 0.125), (C_SBK, skTg, R_skTg, None)):
            sW, RW = ring_load(Win_v[:, :, c0:c0 + 512], [128, 8, 512], R_ws["w_in"])
            for m in range(4):
                bk = m % 2
                for c in range(8):
                    MM(bank(bk), sW[:, c, m * 128:(m + 1) * 128], U[:, c, :], c == 0, c == 7, [RW, RU], [PB[bk]])
                CP("act" if m % 2 == 0 else "dve", dstT[:, m, :], bank(bk), [PB[bk]], [RdT], scale=scl)
        DMA(ACTQ, sqt_s[:, :, rows].rearrange("m p t -> p m t"), sqTg, [R_sqTg], [R_sqt])
        DMA(ACTQ, skt_s[:, :, rows].rearrange("m p t -> p m t"), skTg, [R_skTg], [R_skt])
        for t in range(NTG):
            tl = g * NTG + t
            tc_ = slice(t * 128, (t + 1) * 128)
            for c in range(8):
                MM(bank(3), U[:, c, tc_], wv[:, c, :], c == 0, c == 7, [RU, R_w1], [PB[3]])
            CP("act", vsg[:, t].rearrange("p h d -> p (h d)"), bank(3), [PB[3]], [R_vsg])
            for c in range(8):
                MM(bank(4), U[:, c, tc_], wtok[:, c, 0:512], c == 0, c == 7, [RU, R_w1], [PB[4]])
            for c in range(8):
                MM(bank(5, 0, 160), U[:, c, tc_], wtok[:, c, 512:672], c == 0, c == 7, [RU, R_w1], [PB[5]])
            CP("dve", cql[:, 0:512], bank(4), [PB[4]], [R_cql])
            CP("act", cql[:, 512:672], bank(5, 0, 160), [PB[5]], [R_cql])
            (ssq, lnv, rstd), Rs = stat3()
            ACTF(junk[:, 0:384], cql[:, 0:384], AF.Square, [R_cql], [R_junk, Rs], accum=ssq[:, 0:1])
            ACTF(junk[:, 0:256], cql[:, 384:640], AF.Square, [R_cql], [R_junk, Rs], accum=ssq[:, 1:2])
            ACTF(lnv[:, 0:1], ssq[:, 0:1], AF.Ln, [Rs, R_const], [Rs], bias=epst[:, 0:1], scale=1.0 / 384)
            ACTF(lnv[:, 1:2], ssq[:, 1:2], AF.Ln, [Rs, R_const], [Rs], bias=epst[:, 0:1], scale=1.0 / 256)
            ACTF(rstd[:, 0:2], lnv[:, 0:2], AF.Exp, [Rs], [Rs], scale=-0.5)
            STT(lat_bf[:, 0:384], cql[:, 0:384], rstd[:, 0:1], gql, ALU.mult, ALU.mult, [R_cql, Rs, R_g], [R_lat])
            STT(lat_bf[:, 384:640], cql[:, 384:640], rstd[:, 1:2], gkvl, ALU.mult, ALU.mult, [R_cql, Rs, R_g], [R_lat])
            bl = bankbf(3)[:, 0:640].rearrange("p (c k) -> p c k", c=5)
            for c in range(5):
                TR(bl[:, c, :], lat_bf[:, c * 128:(c + 1) * 128], ident, [R_lat, R_const], [PB[3]])
            CP("dve", latT, bl, [PB[3]], [R_latT])
            for c in range(3):
                MM(bank(4), latT[:, c, :], wq[:, c, 0:512], c == 0, c == 2, [R_latT, R_w1], [PB[4]])
            for c in range(3):
                MM(bank(5, 0, 256), latT[:, c, :], wq[:, c, 512:768], c == 0, c == 2, [R_latT, R_w1], [PB[5]])
            qflat = qf.rearrange("p h d -> p (h d)")
            CP("act", qflat[:, 0:512], bank(4), [PB[4]], [R_qf])
            CP("dve", qflat[:, 512:768], bank(5, 0, 256), [PB[5]], [R_qf])
            for hf in range(2):
                for c in range(2):
                    MM(bank(6 + hf), latT[:, 3 + c, :], wkv[:, c, hf * 512:(hf + 1) * 512], c == 0, c == 1,
                       [R_latT, R_w1], [PB[6 + hf]])
            for hf in range(2):
                bv = bank(6 + hf).rearrange("p (h d) -> p h d", h=4)
                CP("act" if hf == 0 else "dve", kf[:, hf * 4:(hf + 1) * 4, 0:64], bv[:, :, 0:64], [PB[6 + hf]], [R_kf])
                CP("dve" if hf == 0 else "act", vmg[:, t, hf * 4:(hf + 1) * 4, :], bv[:, :, 64:128], [PB[6 + hf]], [R_vmg])
            CP("dve", kf[:, :, 64:96], cql[:, 640:672].unsqueeze(1).broadcast_to([128, 8, 32]), [R_cql], [R_kf])
            head_norm_rope(qf, R_qf, gqh, tl, qTg, R_qTg, t)
            head_norm_rope(kf, R_kf, gkh, tl, kTg, R_kTg, t)
        DMA(ACTQ, qt_s[:, :, rows].rearrange("h d t -> d h t"), qTg, [R_qTg], [R_qt])
        DMA(ACTQ, kt_s[:, :, rows].rearrange("h d t -> d h t"), kTg, [R_kTg], [R_kt])
        for h in range(8):
            DMA(ACTQ, vm_s[h, :, g * NTG:(g + 1) * NTG, :], vmg[:, :, h, :], [R_vmg], [R_vm])
            DMA(ACTQ, vs_s[h, :, g * NTG:(g + 1) * NTG, :], vsg[:, :, h, :], [R_vsg], [R_vs])
    P.barrier()

    st["off"] = PERSIST
    ones64 = onesb[:, 0:64]
    mq = [A([128, T], BF16) for _ in range(4)]
    mk = [A([128, T], BF16) for _ in range(4)]
    mv = [A([128, 32, 128], BF16) for _ in range(4)]
    onz = [A([128, 128], BF16) for _ in range(2)]
    R_mq = [Res() for _ in range(4)]
    R_mk = [Res() for _ in range(4)]
    R_mv = [Res() for _ in range(4)]
    for i_ in range(4):
        MEMSET("dve" if i_ % 2 == 0 else "pool", mv[i_], 0.0, [R_mv[i_]])
    R_onz = Res()
    for i_ in range(2):
        MEMSET("pool", onz[i_], 0.0, [R_onz])
        MEMSET("pool", onz[i_][:, i_ * 64:(i_ + 1) * 64], 1.0, [R_onz])
    pTh = [[A([128, 512], BF16) for _ in range(2)] for _ in range(2)]
    R_pTh = [[Res(), Res()] for _ in range(2)]
    spTh = [[A([128, 512], BF16) for _ in range(2)] for _ in range(2)]
    R_spTh = [[Res(), Res()] for _ in range(2)]
    eTh = [A([128, 512], F32) for _ in range(2)]
    R_eTh = [Res(), Res()]
    spsh = [[A([128, 512], BF16) for _ in range(2)] for _ in range(2)]
    R_spsh = [[Res(), Res()] for _ in range(2)]
    rdenh = [A([128, 512], F32) for _ in range(2)]
    R_rdenh = [Res(), Res()]
    osth = [A([128, 512], BF16) for _ in range(3)]
    R_osth = [Res() for _ in range(3)]
    ocnt = {"n": 0}

    def drive(gens, pattern_head, pattern):
        alive = [True] * len(gens)

        def step(k):
            if alive[k]:
                try:
                    next(gens[k])
                except StopIteration:
                    alive[k] = False
        for k in pattern_head:
            step(k)
        while any(alive):
            for k in pattern:
                step(k)

    def mla_gen(h, h2):
        b4 = h % 4
        p0 = h2 * 64
        Q, K, V = mq[b4], mk[b4], mv[b4]
        sbanks = (0, 1) if h2 == 0 else (2, 3)
        stt = {"n": 0}

        def issue_s(qg, kb):
            r = kb - 4 * qg
            lo = max(r, 0) * 128
            sb_ = sbanks[stt["n"] % 2]
            stt["n"] += 1
            MM(bank(sb_, lo, 512), K[0:96, kb * 128:(kb + 1) * 128], Q[0:96, qg * 512 + lo:(qg + 1) * 512],
               True, True, [R_mk[b4], R_mq[b4]], [PB[sb_]])
            return sb_, lo, r

        blocks = [(qg, kb) for qg in range(NG) for kb in range(4 * (qg + 1))]
        cur = issue_s(*blocks[0])
        nxt = None
        for bi, (qg, kb) in enumerate(blocks):
            nkb = 4 * (qg + 1)
            sb_, lo, r = cur
            if bi + 1 < len(blocks):
                nxt = issue_s(*blocks[bi + 1])
            pt = pTh[h2][bi % 2]
            Rp = R_pTh[h2][bi % 2]
            ACTF(pt[:, lo:512], bank(sb_, lo, 512), AF.Exp, [PB[sb_]], [Rp])
            if r >= 0:
                ASEL(pt[:, lo:lo + 128], pt[:, lo:lo + 128], [[1, 128]], ALU.is_ge, 0, -1, [Rp], [Rp])
            yield
            ob = 4 + 2 * (qg % 2)
            db = ob + 1
            first = (h2 == 0 and kb == 0)
            last = (h2 == 1 and kb == nkb - 1)
            MM(bank(ob, lo, 512), V[:, kb, :], pt[:, lo:512], first, last, [R_mv[b4], Rp], [PB[ob]])
            MM(bank(db, lo, 512), onz[h2], pt[:, lo:512], first, last, [R_onz, Rp], [PB[db]])
            if last:
                rd = rdenh[qg % 2]
                Rr = R_rdenh[qg % 2]
                P.op("dve", lambda e, o_=rd, i_=bank(db): e.reciprocal(out=o_, in_=i_), [PB[db]], [Rr])
                oi = ocnt["n"] % 3
                ocnt["n"] += 1
                TT("dve", osth[oi], bank(ob), rd, ALU.mult, [PB[ob], Rr], [R_osth[oi]])
                DMA(ACTQ, om_s[h // 2, :, qg * 512:(qg + 1) * 512], osth[oi], [R_osth[oi]], [R_om])
            cur = nxt
            yield

    def loads(kind, i):
        if kind == "mla":
            for h2 in range(2):
                h = 2 * i + h2
                b4 = h % 4
                DMA(ACTQ, mq[b4][0:96, :], qt_s[h], [R_qt], [R_mq[b4]])
                DMA(ACTQ, mk[b4][0:96, :], kt_s[h], [R_kt], [R_mk[b4]])
                DMA(ACTQ, mv[b4][:, :, h2 * 64:(h2 + 1) * 64], vm_s[h], [R_vm], [R_mv[b4]])
        else:
            b2 = i % 2
            DMA(ACTQ, mq[b2], sqt_s[i], [R_sqt], [R_mq[b2]])
            for hh in range(2):
                kb_ = 2 * b2 + hh
                if i < 2:
                    MEMSET("dve", mk[kb_], 0.0, [R_mk[kb_]])
                DMA(ACTQ, mk[kb_][hh * 64:(hh + 1) * 64, :], skt_s[i, hh * 64:(hh + 1) * 64, :], [R_skt], [R_mk[kb_]])
                DMA(ACTQ, mv[kb_][:, :, hh * 64:(hh + 1) * 64], vs_s[2 * i + hh], [R_vs], [R_mv[kb_]])

    units = [("mla", i) for i in range(4)] + [("sb", i) for i in range(4)]

    def sb_gen(m, hh):
        b2 = m % 2
        p0 = hh * 64
        kb_ = 2 * b2 + hh
        Q = mq[b2]
        K = mk[kb_]
        V = mv[kb_]
        zb = (0, 1) if hh == 0 else (2, 3)
        stt = {"n": 0, "b": 0}
        pending = [None]

        def issue_z(qg, kb):
            r = kb - 4 * qg
            lo = max(r, 0) * 128
            sb_ = zb[stt["n"] % 2]
            stt["n"] += 1
            MM(bank(sb_, lo, 512), K[:, kb * 128:(kb + 1) * 128], Q[:, qg * 512 + lo:(qg + 1) * 512],
               True, True, [R_mk[kb_], R_mq[b2]], [PB[sb_]])
            return sb_, lo, r

        def make_av(ob, lo, kb, pt, Rp, first, last, qg):
            first = first and hh == 0
            last = last and hh == 1

            def f():
                MM(bank(ob, lo, 512), V[:, kb, :], pt[:, lo:512], first, last, [R_mv[kb_], Rp], [PB[ob]])
                if last:
                    oi = ocnt["n"] % 3
                    ocnt["n"] += 1
                    CP("dve", osth[oi], bank(ob), [PB[ob]], [R_osth[oi]])
                    DMA(ACTQ, os_s[m, :, qg * 512:(qg + 1) * 512], osth[oi], [R_osth[oi]], [R_os])
            return f

        for qg in range(NG):
            nkb = 4 * (qg + 1)
            ob = 4 + (qg % 4)
            sps = spsh[hh][qg % 2]
            Rs = R_spsh[hh][qg % 2]
            MEMSET("pool", sps, 0.0, [Rs])
            order = list(range(nkb - 1, -1, -1))
            cur = issue_z(qg, order[0])
            nxt = None
            for ii, kb in enumerate(order):
                sb_, lo, r = cur
                bsel = stt["b"] % 2
                stt["b"] += 1
                spt = spTh[hh][bsel]
                Rsp = R_spTh[hh][bsel]
                pt = pTh[hh][bsel]
                Rp = R_pTh[hh][bsel]
                ACTF(eTh[hh][:, lo:512], bank(sb_, lo, 512), AF.Exp, [PB[sb_]], [R_eTh[hh]])
                ACTF(spt[:, lo:512], eTh[hh][:, lo:512], AF.Ln, [R_eTh[hh]], [Rsp], bias=1.0)
                if r >= 0:
                    ASEL(spt[:, lo:lo + 128], spt[:, lo:lo + 128], [[1, 128]], ALU.is_gt, 0, -1, [Rsp], [Rsp])
                MM(bank(sb_, lo, 512), trineg, spt[:, lo:512], False, ii == 0, [R_const, Rsp], [PB[sb_]])
                if ii > 0:
                    MM(bank(sb_, lo, 512), negones, sps[:, lo:512], False, True, [R_const, Rs], [PB[sb_]])
                if ii + 1 < len(order):
                    nxt = issue_z(qg, order[ii + 1])
                yield
                ACTF(pt[:, lo:512], bank(sb_, lo, 512), AF.Exp, [PB[sb_]], [Rp])
                if r >= 0:
                    ASEL(pt[:, lo:lo + 128], pt[:, lo:lo + 128], [[1, 128]], ALU.is_gt, 0, -1, [Rp], [Rp])
                if ii + 1 < len(order):
                    TT("dve", sps[:, lo:512], sps[:, lo:512], spt[:, lo:512], ALU.add, [Rs, Rsp], [Rs])
                make_av(ob, lo, kb, pt, Rp, ii == 0, ii == len(order) - 1, qg)()
                cur = nxt
                yield
        if pending[0] is not None:
            pending[0]()
            pending[0] = None
        yield

    loads(*units[0])
    for ui_, (kind, i) in enumerate(units):
        if ui_ + 1 < len(units):
            loads(*units[ui_ + 1])
        if kind == "mla":
            drive([mla_gen(2 * i, 0), mla_gen(2 * i + 1, 1)], [], [0, 1])
        else:
            drive([sb_gen(i, 0), sb_gen(i, 1)], [0], [1, 0, 0, 1])
    P.barrier()

    st["off"] = P1
    wbm = A([128, 4, 1024], BF16)
    wbs = A([128, 4, 1024], BF16)
    wo = A([128, 8, 1024], BF16)
    wpg = A([128, 8, 1024], BF16)
    wpp = A([128, 2, 1024], BF16)
    R_w3 = Res("p3w")
    for dst, n in ((wbm, "w_branch_mla"), (wbs, "w_branch_sb"), (wo, "w_out"), (wpg, "w_ple_gate"), (wpp, "w_ple_proj")):
        DMA(WQ, dst, ws[n].rearrange("(c p) n -> p c n", p=128), [R_ws[n]], [R_w3])
    omg = A([128, 4, G], BF16)
    osg = A([128, 4, G], BF16)
    R_omg = Res()
    R_osg = Res()
    sgm = [A([128, G], F32)] * 2
    sgs = [A([128, G], F32)] * 2
    R_sgm = [Res()] * 2
    R_sgs = [Res()] * 2
    tm1 = [A([128, G], F32)] * 2
    tm2 = [A([128, G], F32)] * 2
    R_tm1 = [Res()] * 2
    R_tm2 = [Res()] * 2
    mT = A([128, 8, G], BF16)
    R_mT = Res()
    pg = A([128, NTG, 256], F32)
    R_pg = Res()
    pgb = A([128, 256], BF16)
    R_pgb = Res()
    ppT = A([128, 2, 128], BF16)
    R_ppT = Res()
    sgp = A([128, 1024], F32)
    R_sgp = Res()
    final = []
    for g in range(NG):
        rows = slice(g * G, (g + 1) * G)
        DMA(ACTQ, hg, h1_s[rows, :].rearrange("(t p) d -> p t d", p=128), [R_h1], [R_hg])
        DMA(ACTQ, omg, om_s[:, :, rows].rearrange("m p t -> p m t"), [R_om], [R_omg])
        DMA(ACTQ, osg, os_s[:, :, rows].rearrange("m p t -> p m t"), [R_os], [R_osg])
        DMA(ACTQ, pg, p_d[rows, :].rearrange("(t p) d -> p t d", p=128), [], [R_pg])
        norm_to_uT(gbc["mix_norm"], 0)
        U = uT[0]
        RU = R_uT[0]
        for mb in range(2):
            sGm, RGm = ring_load(Win_v[:, :, C_GATE + mb * 512:C_GATE + (mb + 1) * 512], [128, 8, 512], R_ws["w_in"])
            sGs, RGs = ring_load(Win_v[:, :, C_GATE + 1024 + mb * 512:C_GATE + 1024 + (mb + 1) * 512], [128, 8, 512],
                                 R_ws["w_in"])
            for mm_ in range(4):
                m = mb * 4 + mm_
                i2 = m % 2
                for c in range(8):
                    MM(bank(0), sGm[:, c, mm_ * 128:(mm_ + 1) * 128], U[:, c, :], c == 0, c == 7, [RGm, RU], [PB[0]])
                for c in range(8):
                    MM(bank(1), sGs[:, c, mm_ * 128:(mm_ + 1) * 128], U[:, c, :], c == 0, c == 7, [RGs, RU], [PB[1]])
                for k in range(4):
                    MM(bank(2), wbm[:, k, m * 128:(m + 1) * 128], omg[:, k, :], k == 0, k == 3, [R_w3, R_omg], [PB[2]])
                for k in range(4):
                    MM(bank(3), wbs[:, k, m * 128:(m + 1) * 128], osg[:, k, :], k == 0, k == 3, [R_w3, R_osg], [PB[3]])
                ACTF(sgm[i2], bank(0), AF.Sigmoid, [PB[0]], [R_sgm[i2]])
                ACTF(sgs[i2], bank(1), AF.Sigmoid, [PB[1]], [R_sgs[i2]])
                TT("dve", tm1[i2], bank(2), sgm[i2], ALU.mult, [PB[2], R_sgm[i2]], [R_tm1[i2]])
                TT("dve", tm2[i2], bank(3), sgs[i2], ALU.mult, [PB[3], R_sgs[i2]], [R_tm2[i2]])
                TT("dve", mT[:, m, :], tm1[i2], tm2[i2], ALU.add, [R_tm1[i2], R_tm2[i2]], [R_mT])
        for t in range(NTG):
            tc_ = slice(t * 128, (t + 1) * 128)
            for hf in range(2):
                bk = 4 + (t % 2) * 2 + hf
                for c in range(8):
                    MM(bank(bk), mT[:, c, tc_], wo[:, c, hf * 512:(hf + 1) * 512], c == 0, c == 7, [R_mT, R_w3], [PB[bk]])
                dst = hg[:, t, hf * 512:(hf + 1) * 512]
                TT("dve", dst, bank(bk), dst, ALU.add, [PB[bk], R_hg], [R_hg])
        norm_to_uT(gbc["ffn2_norm"], 1)
        ffn("ffn2_w_in", "ffn2_w_out", 1)
        norm_to_uT(gbc["ple_norm"], 0)
        for t in range(NTG):
            tc_ = slice(t * 128, (t + 1) * 128)
            CP("dve", pgb, pg[:, t, :], [R_pg], [R_pgb])
            bp = bankbf(2)[:, 0:256].rearrange("p (c k) -> p c k", c=2)
            for c in range(2):
                TR(bp[:, c, :], pgb[:, c * 128:(c + 1) * 128], ident, [R_pgb, R_const], [PB[2]])
            CP("act", ppT, bp, [PB[2]], [R_ppT])
            for hf in range(2):
                bk = 4 + hf
                for c in range(8):
                    MM(bank(bk), U[:, c, tc_], wpg[:, c, hf * 512:(hf + 1) * 512], c == 0, c == 7, [RU, R_w3], [PB[bk]])
                ACTF(sgp[:, hf * 512:(hf + 1) * 512], bank(bk), AF.Sigmoid, [PB[bk]], [R_sgp])
                bk2 = 6 + hf
                for c in range(2):
                    MM(bank(bk2), ppT[:, c, :], wpp[:, c, hf * 512:(hf + 1) * 512], c == 0, c == 1, [R_ppT, R_w3], [PB[bk2]])
                TT("dve", sgp[:, hf * 512:(hf + 1) * 512], bank(bk2), sgp[:, hf * 512:(hf + 1) * 512], ALU.mult,
                   [PB[bk2], R_sgp], [R_sgp])
                dst = hg[:, t, hf * 512:(hf + 1) * 512]
                TT("dve", dst, dst, sgp[:, hf * 512:(hf + 1) * 512], ALU.add, [R_sgp, R_hg], [R_hg])
        final.append(DMA(ACTQ, out_d[rows, :].rearrange("(t p) d -> p t d", p=128), hg, [R_hg], [R_out]))

    print("arena marks", PERSIST, P1, st["off"], {e: len(P.ins[e]) for e in P.ENG})
    P.emit(nc, final_waits=final)
    es.close()
    return nc


_NC = None


def kernel(**inputs):
    global _NC
    if _NC is None:
        _NC = build_nc()
    nc = _NC
    x = np.asarray(inputs["x"], dtype=np.float32)
    p = np.asarray(inputs["p"], dtype=np.float32)
    pos = np.asarray(inputs["positions"], dtype=np.int32)
    common = {}
    for n, r, c in WEIGHTS:
        common[n] = np.ascontiguousarray(np.asarray(inputs[n], dtype=np.float32).reshape(r, c))
    for n, c in GAINS:
        common[n] = np.ascontiguousarray(np.asarray(inputs[n], dtype=np.float32).reshape(1, c))
    in_maps = []
    for b in range(8):
        m = dict(common)
        m["x"] = np.ascontiguousarray(x[b])
        m["p"] = np.ascontiguousarray(p[0, b])
        m["posT"] = np.ascontiguousarray(pos[b].reshape(32, 128).T)
        in_maps.append(m)
    res = run_bass_kernel_spmd(nc, in_maps, core_ids=list(range(8)))
    out = np.stack([np.asarray(r["out"], dtype=np.float32) for r in res.results], axis=0)
    return out
```
